# Optimizing a Trainium2 kernel written in Bass

```python
import jax, jax.numpy as jnp
from jax import lax
import numpy as np

D_MODEL = 1024
BATCH = 8
SEQ = 2048
DEPTH = 2
DEC_BATCH = 128
DEC_SEQ = 4
PAST_LEN = 16384
PAGE_SIZE = 128

N_HEADS = 8
HEAD_K = 128
HEAD_V = 128
QK_W = N_HEADS * HEAD_K
V_W = N_HEADS * HEAD_V
CONV_W = 4
CONV_CH = 2 * QK_W + V_W
CHUNK = 64
POOL_GROUPS = 4
POOL_WINDOWS = (2, 4, 8, 16)
POOL_W = D_MODEL
POOL_GC = POOL_W // POOL_GROUPS
POOL_HIST = max(POOL_WINDOWS) - 1
PLE_DIM = 256
IN_SIZES = (QK_W, QK_W, V_W, V_W, N_HEADS, N_HEADS, POOL_W, POOL_W, D_MODEL, D_MODEL)
IN_W = sum(IN_SIZES)
IN_SPLITS = tuple(int(s) for s in np.cumsum(IN_SIZES)[:-1])
EPS = 1e-6

kernel_name = 'hybrid_gdn_pool_decoder_step'


def rmsnorm(x, w):
    xf = x.astype(jnp.float32)
    y = xf * lax.rsqrt(jnp.mean(xf * xf, axis=-1, keepdims=True) + EPS)
    return (y * w.astype(jnp.float32)).astype(x.dtype)


def l2norm(x):
    xf = x.astype(jnp.float32)
    return xf * lax.rsqrt(jnp.sum(xf * xf, axis=-1, keepdims=True) + EPS)


def causal_conv(xc, hist, w):
    full = jnp.concatenate([hist.astype(xc.dtype), xc], axis=1)
    T = xc.shape[1]
    out = full[:, 0:T] * w[0]
    for j in range(1, CONV_W):
        out = out + full[:, j:j + T] * w[j]
    return jax.nn.silu(out), full[:, -(CONV_W - 1):]


def gated_delta_chunked(q, k, v, g, beta, s0, chunk):
    B, T, H, K = q.shape
    V = v.shape[-1]
    n = T // chunk

    def blk(t):
        t = t.reshape((B, n, chunk, H) + t.shape[3:])
        return jnp.moveaxis(t, 3, 1)

    qc, kc, vc, gc, bc = blk(q), blk(k), blk(v), blk(g), blk(beta)
    G = jnp.cumsum(gc, axis=-1)
    idx = jnp.arange(chunk)
    lower = idx[:, None] >= idx[None, :]
    strict = idx[:, None] > idx[None, :]
    decay = jnp.exp(jnp.where(lower, G[..., :, None] - G[..., None, :], -jnp.inf))
    kk = jnp.einsum('bhnik,bhnjk->bhnij', kc, kc)
    a_mat = jnp.where(strict, kk * decay * bc[..., :, None], 0.0)
    eye = jnp.eye(chunk, dtype=jnp.float32)
    rhs = jnp.concatenate([vc * bc[..., None], kc * (bc * jnp.exp(G))[..., None]], axis=-1)
    sol = lax.linalg.triangular_solve(eye + a_mat, rhs, left_side=True, lower=True)
    w_val, w_key = sol[..., :V], sol[..., V:]
    qk = jnp.einsum('bhnik,bhnjk->bhnij', qc, kc) * decay
    q_dec = qc * jnp.exp(G)[..., None]
    k_tail = kc * jnp.exp(G[..., -1:] - G)[..., None]
    g_last = jnp.exp(G[..., -1])

    def step(s, xs):
        wv, wk, qk_i, qd, kt, gl = xs
        u = wv - jnp.einsum('bhck,bhkv->bhcv', wk, s)
        o = jnp.einsum('bhck,bhkv->bhcv', qd, s) + jnp.einsum('bhij,bhjv->bhiv', qk_i, u)
        s = s * gl[..., None, None] + jnp.einsum('bhck,bhcv->bhkv', kt, u)
        return s, o

    xs = tuple(jnp.moveaxis(t, 2, 0) for t in (w_val, w_key, qk, q_dec, k_tail, g_last))
    s_fin, o = lax.scan(step, s0, xs)
    o = jnp.transpose(o, (1, 0, 3, 2, 4)).reshape(B, T, H, V)
    return o, s_fin


def pool_mix(u, hist, pos0, w_grp, scale):
    B, T, _ = u.shape
    full = jnp.concatenate([hist.astype(u.dtype), u], axis=1)
    cs = jnp.cumsum(full.astype(jnp.float32), axis=1)
    cs = jnp.concatenate([jnp.zeros_like(cs[:, :1]), cs], axis=1)
    cs = cs.reshape(B, POOL_HIST + T + 1, POOL_GROUPS, POOL_GC)
    win = jnp.array(POOL_WINDOWS, dtype=jnp.int32)
    t = jnp.arange(T, dtype=jnp.int32)
    lo = (POOL_HIST + 1 + t)[:, None] - win[None, :]
    cs_hi = cs[:, POOL_HIST + 1:]
    cs_lo = cs[:, lo, jnp.arange(POOL_GROUPS)[None, :]]
    count = jnp.minimum((pos0 + 1 + t)[:, None], win[None, :]).astype(jnp.float32)
    ug = u.astype(jnp.float32).reshape(B, T, POOL_GROUPS, POOL_GC)
    y = (cs_hi - cs_lo) / count[None, :, :, None] - ug
    y = jnp.einsum('btgc,gcd->btgd', y, w_grp.astype(jnp.float32)).reshape(B, T, POOL_W)
    y = y * scale.astype(jnp.float32)
    return y.astype(u.dtype), full[:, -POOL_HIST:]


def mixer_layer(x, p_i, conv_hist, s0, pool_hist, pos0, norm_mix, w_in, conv_w, a_log, dt_bias, gdn_norm,
                w_proj_a, pool_w, pool_scale, w_proj_b, w_out, norm_ple, w_ple_gate, w_ple_proj):
    B, T, _ = x.shape
    h = rmsnorm(x, norm_mix)
    proj = h @ w_in
    q_r, k_r, v_r, z, a_r, b_r, u, gp, ga, gb = jnp.split(proj, IN_SPLITS, axis=-1)
    qkv, conv_new = causal_conv(jnp.concatenate([q_r, k_r, v_r], axis=-1), conv_hist, conv_w)
    q, k, v = jnp.split(qkv, (QK_W, 2 * QK_W), axis=-1)
    q = l2norm(q.reshape(B, T, N_HEADS, HEAD_K)) * (HEAD_K ** -0.5)
    k = l2norm(k.reshape(B, T, N_HEADS, HEAD_K))
    v = v.reshape(B, T, N_HEADS, HEAD_V).astype(jnp.float32)
    g = -jnp.exp(a_log.astype(jnp.float32)) * jax.nn.softplus(a_r.astype(jnp.float32) + dt_bias.astype(jnp.float32))
    beta = jax.nn.sigmoid(b_r.astype(jnp.float32))
    chunk = CHUNK if T % CHUNK == 0 else T
    o, s_new = gated_delta_chunked(q, k, v, g, beta, s0.astype(jnp.float32), chunk)
    o = rmsnorm(o, gdn_norm).astype(x.dtype) * jax.nn.silu(z.reshape(B, T, N_HEADS, HEAD_V))
    y_a = o.reshape(B, T, V_W) @ w_proj_a
    y_pool, pool_new = pool_mix(u, pool_hist, pos0, pool_w, pool_scale)
    y_b = (y_pool * jax.nn.silu(gp)) @ w_proj_b
    m = jax.nn.sigmoid(ga) * y_a + jax.nn.sigmoid(gb) * y_b
    x = x + m @ w_out
    x = x + jax.nn.sigmoid(rmsnorm(x, norm_ple) @ w_ple_gate) * (p_i @ w_ple_proj)
    return x, conv_new, s_new.astype(x.dtype), pool_new


def run_group(x, p, conv_state, delta_state, pool_state, pos0, norm_mix, w_in, conv_w, a_log, dt_bias, gdn_norm,
              w_proj_a, pool_w, pool_scale, w_proj_b, w_out, norm_ple, w_ple_gate, w_ple_proj, final_norm):
    B = x.shape[0]
    convs, deltas, pools = [], [], []
    for i in range(DEPTH):
        if conv_state is None:
            ch = jnp.zeros((B, CONV_W - 1, CONV_CH), x.dtype)
            s0 = jnp.zeros((B, N_HEADS, HEAD_K, HEAD_V), jnp.float32)
            ph = jnp.zeros((B, POOL_HIST, POOL_W), x.dtype)
        else:
            ch, s0, ph = conv_state[i], delta_state[i], pool_state[i]
        x, c_new, s_new, p_new = mixer_layer(
            x, p[i], ch, s0, ph, pos0, norm_mix[i], w_in[i], conv_w[i], a_log[i], dt_bias[i], gdn_norm[i],
            w_proj_a[i], pool_w[i], pool_scale[i], w_proj_b[i], w_out[i], norm_ple[i], w_ple_gate[i], w_ple_proj[i])
        convs.append(c_new)
        deltas.append(s_new)
        pools.append(p_new)
    y = rmsnorm(x, final_norm)
    return y, jnp.stack(convs), jnp.stack(deltas), jnp.stack(pools)


def setup_inputs(seed: int = 0) -> dict:
    key = jax.random.key(seed)
    ks = jax.random.split(key, 24)
    f32 = jnp.float32
    nrm = lambda k, shape, s: (jax.random.normal(k, shape, f32) * s)
    dt = jnp.exp(jax.random.uniform(ks[9], (DEPTH, N_HEADS), f32, np.log(1e-3), np.log(1e-1)))
    return {
        'x_prompt': nrm(ks[0], (BATCH, SEQ, D_MODEL), 1.0),
        'x_sample': nrm(ks[1], (DEC_BATCH, DEC_SEQ, D_MODEL), 1.0),
        'p_prompt': nrm(ks[2], (DEPTH, BATCH, SEQ, PLE_DIM), 1.0),
        'p_sample': nrm(ks[3], (DEPTH, DEC_BATCH, DEC_SEQ, PLE_DIM), 1.0),
        'state_conv': nrm(ks[4], (DEPTH, DEC_BATCH, CONV_W - 1, CONV_CH), 1.0),
        'state_delta': nrm(ks[5], (DEPTH, DEC_BATCH, N_HEADS, HEAD_K, HEAD_V), HEAD_K ** -0.5),
        'state_pool': nrm(ks[6], (DEPTH, DEC_BATCH, POOL_HIST, POOL_W), 1.0),
        'norm_mix': 1.0 + nrm(ks[7], (DEPTH, D_MODEL), 0.02),
        'w_in': nrm(ks[8], (DEPTH, D_MODEL, IN_W), D_MODEL ** -0.5),
        'conv_w': nrm(ks[10], (DEPTH, CONV_W, CONV_CH), 0.5),
        'a_log': jnp.log(jax.random.uniform(ks[11], (DEPTH, N_HEADS), f32, 1.0, 16.0)),
        'dt_bias': dt + jnp.log(-jnp.expm1(-dt)),
        'gdn_norm': 1.0 + nrm(ks[12], (DEPTH, HEAD_V), 0.02),
        'w_proj_a': nrm(ks[13], (DEPTH, V_W, D_MODEL), V_W ** -0.5),
        'pool_w': nrm(ks[14], (DEPTH, POOL_GROUPS, POOL_GC, POOL_GC), POOL_GC ** -0.5),
        'pool_scale': 1.0 + nrm(ks[15], (DEPTH, POOL_W), 0.1),
        'w_proj_b': nrm(ks[16], (DEPTH, POOL_W, D_MODEL), POOL_W ** -0.5),
        'w_out': nrm(ks[17], (DEPTH, D_MODEL, D_MODEL), D_MODEL ** -0.5),
        'norm_ple': 1.0 + nrm(ks[18], (DEPTH, D_MODEL), 0.02),
        'w_ple_gate': nrm(ks[19], (DEPTH, D_MODEL, D_MODEL), D_MODEL ** -0.5),
        'w_ple_proj': nrm(ks[20], (DEPTH, PLE_DIM, D_MODEL), PLE_DIM ** -0.5),
        'final_norm': 1.0 + nrm(ks[21], (D_MODEL,), 0.02),
    }


def reference(x_prompt, x_sample, p_prompt, p_sample, state_conv, state_delta, state_pool, norm_mix, w_in, conv_w,
              a_log, dt_bias, gdn_norm, w_proj_a, pool_w, pool_scale, w_proj_b, w_out, norm_ple, w_ple_gate,
              w_ple_proj, final_norm):
    y_prompt, conv_p, delta_p, pool_p = run_group(
        x_prompt, p_prompt, None, None, None, 0, norm_mix, w_in, conv_w, a_log, dt_bias, gdn_norm, w_proj_a,
        pool_w, pool_scale, w_proj_b, w_out, norm_ple, w_ple_gate, w_ple_proj, final_norm)
    y_sample, conv_s, delta_s, pool_s = run_group(
        x_sample, p_sample, state_conv, state_delta, state_pool, PAST_LEN, norm_mix, w_in, conv_w, a_log, dt_bias,
        gdn_norm, w_proj_a, pool_w, pool_scale, w_proj_b, w_out, norm_ple, w_ple_gate, w_ple_proj, final_norm)
    return (y_prompt, y_sample, conv_p, delta_p, pool_p, conv_s, delta_s, pool_s)
```

```python
import numpy as np
import concourse.bass as bass
import concourse.mybir as mybir
from concourse.bass_utils import run_bass_kernel_spmd

F32 = mybir.dt.float32
BF16 = mybir.dt.bfloat16
AF = mybir.ActivationFunctionType
ALU = mybir.AluOpType
AX = mybir.AxisListType

D = 1024
DEPTH = 2
NH = 8
PT = 2048
ST = 64
TOK = PT + ST
NT = 17
NSEQ = 16
INW = 8208
POOL_WINDOWS = (2, 4, 8, 16)
EPS = 1e-6
C_Q, C_K, C_V, C_Z, C_A, C_B, C_U, C_GP, C_GA, C_GB = 0, 1024, 2048, 3072, 4096, 4104, 4112, 5136, 6160, 7184

ENGS = ("pe", "act", "dve", "pool", "sp")


class Rec:
    __slots__ = ("eng", "fn", "deps", "signal", "ticket", "dma", "dsem", "dval")

    def __init__(self, eng, fn, dma=False):
        self.eng = eng
        self.fn = fn
        self.deps = []
        self.signal = False
        self.ticket = None
        self.dma = dma
        self.dsem = None
        self.dval = None


class Prog:
    def __init__(self, nc, n_dma_sems=12):
        self.nc = nc
        self.q = {e: [] for e in ENGS}
        self.last_w = {}
        self.readers = {}
        self.n_dma_sems = n_dma_sems
        self.dma_count = {e: 0 for e in ENGS}
        self.dma_hist = {e: [] for e in ENGS}
        self.out_dmas = []

    def _add_dep(self, x, d, kind):
        if d is None or d is x:
            return
        if not d.dma and d.eng == x.eng and not x.dma:
            if x.eng == "pe":
                return
        if d not in x.deps:
            x.deps.append(d)
            if not d.dma:
                d.signal = True

    def op(self, eng, fn, r=(), w=(), dma=False):
        x = Rec(eng, fn, dma)
        for k in r:
            self._add_dep(x, self.last_w.get(k), "RAW")
            if isinstance(k, tuple) and k[0] == "ps":
                for rd in self.readers.get(k, ()):
                    if rd.eng != eng:
                        self._add_dep(x, rd, "RAR")
        for k in w:
            self._add_dep(x, self.last_w.get(k), "WAW")
            for rd in self.readers.get(k, ()):
                self._add_dep(x, rd, "WAR")
        for k in r:
            self.readers.setdefault(k, []).append(x)
        for k in w:
            self.last_w[k] = x
            self.readers[k] = []
        if dma:
            n = self.dma_count[eng]
            self.dma_count[eng] += 1
            x.dsem = n % self.n_dma_sems
            x.dval = 16 * (n // self.n_dma_sems + 1)
            hist = self.dma_hist[eng]
            if n >= self.n_dma_sems:
                x.deps.append(hist[n - self.n_dma_sems])
            hist.append(x)
        self.q[eng].append(x)
        return x

    def dma(self, eng, out, in_, r=(), w=(), is_output=False, **kw):
        x = self.op(eng, lambda e: e.dma_start(out=out, in_=in_, **kw), r=r, w=w, dma=True)
        if is_output:
            self.out_dmas.append(x)
        return x

    def barrier(self):
        lasts = []
        for e in ENGS:
            for x in reversed(self.q[e]):
                if not x.dma and x.fn is not None:
                    lasts.append(x)
                    break
            lasts.extend(self.dma_hist[e][-self.n_dma_sems:])
        for e in ENGS:
            b = Rec(e, None)
            for d in lasts:
                if d.dma or d.eng != e:
                    b.deps.append(d)
                    if not d.dma:
                        d.signal = True
            self.q[e].append(b)
        self.last_w = {}
        self.readers = {}

    def emit(self):
        nc = self.nc
        from contextlib import ExitStack
        with ExitStack() as es:
            esem = {e: es.enter_context(nc.semaphore("sem_" + e)) for e in ENGS}
            dsem = {e: [es.enter_context(nc.semaphore(f"dsem_{e}_{i}")) for i in range(self.n_dma_sems)]
                    for e in ENGS if self.dma_count[e] > 0}
            for e in ENGS:
                c = 0
                for x in self.q[e]:
                    if x.signal and not x.dma:
                        c += 1
                        x.ticket = c
            final = Rec("sp", None)
            final.deps = list(self.out_dmas)
            block = es.enter_context(nc.Block())
            handles = {"pe": block.tensor, "act": block.scalar, "dve": block.vector, "pool": block.gpsimd,
                       "sp": block.sync}

            def run_engine(ename):
                def body(e):
                    seen = {}

                    def do_waits(x):
                        for d in x.deps:
                            if d.dma:
                                key = ("d", d.eng, d.dsem)
                                sem, val = dsem[d.eng][d.dsem], d.dval
                            else:
                                key = ("e", d.eng)
                                sem, val = esem[d.eng], d.ticket
                            if seen.get(key, 0) >= val:
                                continue
                            seen[key] = val
                            e.wait_ge(sem, val)

                    for x in self.q[ename]:
                        do_waits(x)
                        if x.fn is None:
                            continue
                        ins = x.fn(e)
                        if x.dma:
                            ins.then_inc(dsem[ename][x.dsem], 16)
                        elif x.signal:
                            ins.then_inc(esem[ename], 1)
                    if ename == "sp":
                        do_waits(final)
                return body

            for ename in ENGS:
                handles[ename](run_engine(ename))
        return nc


class Recorder:
    def __init__(self):
        self.ops = []

    def op(self, eng, fn, r=(), w=(), dma=False):
        self.ops.append(("op", eng, (fn,), dict(r=r, w=w, dma=dma)))

    def dma(self, eng, out, in_, r=(), w=(), is_output=False, **kw):
        self.ops.append(("dma", eng, (out, in_), dict(r=r, w=w, is_output=is_output, **kw)))


def merge_threads(P, recs, max_run=64):
    idx = [0] * len(recs)
    cur = 0
    n = len(recs)
    while any(idx[i] < len(recs[i].ops) for i in range(n)):
        if idx[cur] >= len(recs[cur].ops):
            cur = (cur + 1) % n
            continue
        ops = recs[cur].ops
        eng = ops[idx[cur]][1]
        cnt = 0
        while idx[cur] < len(ops) and ops[idx[cur]][1] == eng and cnt < max_run:
            kind, e_, a, kw = ops[idx[cur]]
            if kind == "op":
                P.op(e_, *a, **kw)
            else:
                P.dma(e_, *a, **kw)
            idx[cur] += 1
            cnt += 1
        cur = (cur + 1) % n


def mm(out, lhsT, rhs, start=True, stop=True):
    return lambda e: e.matmul(out, lhsT=lhsT, rhs=rhs, start=start, stop=stop)


def trp(out, in_, ident):
    return lambda e: e.transpose(out, in_, ident)


def actf(out, in_, func, scale=None, bias=None, accum=None):
    kw = {}
    if scale is not None:
        kw["scale"] = scale
    if bias is not None:
        kw["bias"] = bias
    if accum is not None:
        kw["accum_out"] = accum
    return lambda e: e.activation(out=out, in_=in_, func=func, **kw)


def tt(out, a, b, op):
    return lambda e: e.tensor_tensor(out=out, in0=a, in1=b, op=op)


def ts(out, a, s1, op0, s2=None, op1=None):
    if op1 is None:
        return lambda e: e.tensor_scalar(out=out, in0=a, scalar1=s1, scalar2=None, op0=op0)
    return lambda e: e.tensor_scalar(out=out, in0=a, scalar1=s1, scalar2=s2, op0=op0, op1=op1)


def stt(out, in0, scalar, in1, op0, op1):
    return lambda e: e.scalar_tensor_tensor(out=out, in0=in0, scalar=scalar, in1=in1, op0=op0, op1=op1)


def cp(out, in_):
    return lambda e: e.tensor_copy(out=out, in_=in_)


def acp(out, in_):
    return lambda e: e.copy(out=out, in_=in_)


def rcp(out, in_):
    return lambda e: e.reciprocal(out=out, in_=in_)


def red(out, in_):
    return lambda e: e.tensor_reduce(out=out, in_=in_, axis=AX.X, op=ALU.add)


def mset(out, v):
    return lambda e: e.memset(out, v)


CF_ID, CF_NM2, CF_MUD, CF_UTRI, CF_ONES = 0, 128, 384, 512, 640
CF_NM2S, CF_MUDS, CF_UTRIS, CF_ONESS, CF_SEGT = 768, 896, 960, 1024, 1088
NCF = 1104
CB_ID, CB_SEG, CB_BC, CB_BP, CB_BF, CB_SH0, CB_SH1, CB_SN = 0, 128, 1152, 1664, 2176, 2688, 2944, 3200
NCB = 3456


def make_consts():
    cf = np.zeros((128, NCF), np.float32)
    cb = np.zeros((128, NCB), np.float32)
    p = np.arange(128)[:, None]
    f = np.arange(128)[None, :]
    cf[:, CF_ID:CF_ID + 128] = np.eye(128)
    cf[:, CF_NM2:CF_NM2 + 128] = -1.0 * (f > p)
    cf[:, CF_NM2 + 128:CF_NM2 + 256] = -1.0 * (p > f)
    cf[:, CF_MUD:CF_MUD + 128] = (f >= p)
    cf[:, CF_UTRI:CF_UTRI + 128] = (p <= f)
    cf[:, CF_ONES:CF_ONES + 128] = 1.0
    ps = np.arange(64)[:, None]
    fs = np.arange(64)[None, :]
    same = (ps % 16) == (fs % 16)
    tp, tf = ps // 16, fs // 16
    cf[:64, CF_NM2S:CF_NM2S + 64] = -1.0 * (same & (tf > tp))
    cf[:64, CF_NM2S + 64:CF_NM2S + 128] = -1.0 * (same & (tp > tf))
    cf[:64, CF_MUDS:CF_MUDS + 64] = (same & (tf >= tp))
    cf[:64, CF_UTRIS:CF_UTRIS + 64] = (same & (tp <= tf))
    cf[:64, CF_ONESS:CF_ONESS + 64] = same
    cf[:64, CF_SEGT:CF_SEGT + 16] = (np.arange(64)[:, None] % 16) == np.arange(16)[None, :]
    cb[:, CB_ID:CB_ID + 128] = np.eye(128)
    seg = ((np.arange(64)[None, :] % 16) == np.arange(16)[:, None]).astype(np.float32)
    cb[:, CB_SEG:CB_SEG + 1024] = seg.reshape(1, 1024)
    for g, W in enumerate(POOL_WINDOWS):
        tq = np.arange(128)[:, None]
        t = np.arange(128)[None, :]
        cur = ((tq > t - W) & (tq <= t)) / W - (tq == t)
        prev = ((tq - 128 > t - W)) / W
        cnt = np.minimum(t + 1, W)
        first = ((tq >= np.maximum(0, t - W + 1)) & (tq <= t)) / cnt - (tq == t)
        cb[:, CB_BC + g * 128:CB_BC + (g + 1) * 128] = cur
        cb[:, CB_BP + g * 128:CB_BP + (g + 1) * 128] = prev
        cb[:, CB_BF + g * 128:CB_BF + (g + 1) * 128] = first
        r = np.arange(240)[:, None]
        j, s1 = r // 16, r % 16
        c = np.arange(64)[None, :]
        tau, s2 = c // 16, c % 16
        sh = ((s1 == s2) & (j >= 16 + tau - W)) / W
        cb[:, CB_SH0 + g * 64:CB_SH0 + (g + 1) * 64] = sh[:128]
        cb[:112, CB_SH1 + g * 64:CB_SH1 + (g + 1) * 64] = sh[128:]
        r = np.arange(64)[:, None]
        tau1, s1 = r // 16, r % 16
        sn = (s1 == s2) * (((tau1 > tau - W) & (tau1 <= tau)) / W - (tau1 == tau))
        cb[:64, CB_SN + g * 64:CB_SN + (g + 1) * 64] = sn
    return cf, cb


A_STAGGER = 0


def build(depth=DEPTH, stop_after=None, dbg=False, dbg_tile=0):
    nc = bass.Bass("TRN2", target_bir_lowering=False)
    P = Prog(nc)

    def din(name, shape):
        return nc.dram_tensor(name, list(shape), F32, kind="ExternalInput").ap()

    def dout(name, shape):
        return nc.dram_tensor(name, list(shape), F32, kind="ExternalOutput").ap()

    x_all = din("x_all", [TOK, D])
    p_all = din("p_all", [DEPTH, TOK, 256])
    sconv = din("sconv", [DEPTH, 48, 3072])
    sdelta = din("sdelta", [DEPTH, NSEQ, NH, 128, 128])
    spool = din("spool", [DEPTH, 240, D])
    norm_mix = din("norm_mix", [DEPTH, D])
    w_in = din("w_in", [DEPTH, D, INW])
    conv_w = din("conv_w", [DEPTH, 4, 3072])
    a_log = din("a_log", [DEPTH, NH])
    dt_bias = din("dt_bias", [DEPTH, NH])
    gdn_norm = din("gdn_norm", [DEPTH, 128])
    w_proj_a = din("w_proj_a", [DEPTH, D, D])
    pool_w = din("pool_w", [DEPTH, 4, 256, 256])
    pool_scale = din("pool_scale", [DEPTH, D])
    w_proj_b = din("w_proj_b", [DEPTH, D, D])
    w_out = din("w_out", [DEPTH, D, D])
    norm_ple = din("norm_ple", [DEPTH, D])
    w_ple_gate = din("w_ple_gate", [DEPTH, D, D])
    w_ple_proj = din("w_ple_proj", [DEPTH, 256, D])
    final_norm = din("final_norm", [D])
    cf_d = din("cf", [128, NCF])
    cb_d = din("cb", [128, NCB])

    y_all = dout("y_all", [TOK, D])
    conv_p = dout("conv_p", [DEPTH, 3, 3072])
    delta_p = dout("delta_p", [DEPTH, NH, 128, 128])
    pool_p = dout("pool_p", [DEPTH, 15, D])
    conv_s = dout("conv_s", [DEPTH, 3, NSEQ, 3072])
    delta_s = dout("delta_s", [DEPTH, NSEQ, NH, 128, 128])
    pool_s = dout("pool_s", [DEPTH, 15, NSEQ, D])
    xs = nc.dram_tensor("xscratch", [TOK, D], F32, kind="Internal").ap()
    dbg_seen = set()

    def DBG(name, ap, keys):
        if not dbg or name in dbg_seen:
            return
        dbg_seen.add(name)
        shp = list(ap.shape)
        d = nc.dram_tensor("dbg_" + name, shp, F32, kind="ExternalOutput").ap()
        P.dma("sp" if ap.dtype == F32 else "pool", d, ap, r=keys, is_output=True)

    total = (nc.sbuf_bytes_remaining // 64) * 64 - 256
    base = nc.bump_sbuf(total)[0]
    cursor = [0]

    def alloc(name, shape, dt, at=None):
        nbytes = int(np.prod(shape[1:])) * (4 if dt == F32 else 2)
        nbytes = (nbytes + 63) // 64 * 64
        if at is None:
            off = cursor[0]
            cursor[0] += nbytes
            assert cursor[0] <= total, (name, cursor[0], total)
        else:
            off = at
        return nc.alloc_sbuf_tensor_at(name, list(shape), dt, offset=base + off)

    hT = alloc("hT", [128, 8, TOK], BF16)
    oT = alloc("oT", [128, 8, TOK], BF16)
    AR_OFF = cursor[0]
    NSLOT = 16
    AR = alloc("AR", [128, NSLOT * 2048], BF16)
    CF = alloc("CF", [128, NCF], F32)
    CB = alloc("CB", [128, NCB], BF16)
    epsc = alloc("epsc", [128, 1], F32)
    onec = alloc("onec", [128, 1], F32)
    colv = [alloc(f"colv{l}", [128, 128], F32) for l in range(DEPTH)]
    dtb = [alloc(f"dtb{l}", [128, 8], F32) for l in range(DEPTH)]
    nexpA = [alloc(f"nexpA{l}", [128, 8], F32) for l in range(DEPTH)]
    W0 = cursor[0]
    uid = [0]

    def walloc(shape, dt, name=None):
        uid[0] += 1
        return alloc(f"{name or 'w'}_{uid[0]}", shape, dt)

    def wreset():
        cursor[0] = W0

    PS = [nc.alloc_psum_tensor(f"ps{b}", [128, 512], F32) for b in range(8)]

    def psk(*banks):
        return [("ps", b) for b in banks]

    def ark(*slots):
        return [("ar", s) for s in slots]

    TW = [128] * 16 + [64]
    TO = [128 * i for i in range(17)]
    identb = CB[:, CB_ID:CB_ID + 128]
    identf = CF[:, CF_ID:CF_ID + 128]

    def wslot(s0, n, ncols):
        return AR[:, s0 * 2048:(s0 + n) * 2048].rearrange("p (k c) -> p k c", k=8)

    def load_w(s0, wmat, col0, ncols):
        n = ncols // 256
        src = wmat.rearrange("(k p) c -> p k c", p=128)
        for i in range(n):
            P.dma("pool", wslot(s0 + i, 1, 256), src[:, :, col0 + i * 256:col0 + (i + 1) * 256], w=ark(s0 + i))

    def load_w_wide(s0, wmat, col0, ncols):
        src = wmat.rearrange("(k p) c -> p k c", p=128)
        for i in range(ncols // 512):
            P.dma("pool", wslot(s0 + 2 * i, 2, 512), src[:, :, col0 + i * 512:col0 + (i + 1) * 512],
                  w=ark(s0 + 2 * i, s0 + 2 * i + 1))

    def wwide(s0, i):
        return wslot(s0 + 2 * i, 2, 512), ark(s0 + 2 * i, s0 + 2 * i + 1)

    def wview(s0, ncols, c0, cw):
        s = s0 + c0 // 256
        cc = c0 % 256
        assert cc + cw <= 256
        return wslot(s, 1, 256)[:, :, cc:cc + cw], ("ar", s)

    P.dma("sp", CF[:], cf_d, w=["CF"])
    P.dma("pool", CB[:], cb_d, w=["CB"])
    P.op("dve", mset(epsc[:], EPS), w=["epsc"])
    P.op("dve", mset(onec[:], 1.0), w=["onec"])
    VR = walloc([128, 128], F32, "VR")
    for l in range(depth):
        P.op("dve", mset(VR[:], 0.0), w=["VR"])
        P.dma("sp", VR[0:96, :], conv_w[l].rearrange("t (j p) -> (t j) p", p=128), w=["VR"])
        P.dma("sp", VR[96:104, :], norm_mix[l].rearrange("(k p) -> k p", p=128), w=["VR"])
        P.dma("sp", VR[104:112, :], norm_ple[l].rearrange("(k p) -> k p", p=128), w=["VR"])
        P.dma("sp", VR[112:120, :], pool_scale[l].rearrange("(k p) -> k p", p=128), w=["VR"])
        P.dma("sp", VR[120:121, :], gdn_norm[l].rearrange("(k p) -> k p", p=128), w=["VR"])
        P.op("pe", mm(PS[0][:, 0:128], VR[:, :], identf), r=["VR", "CF"], w=psk(0))
        P.op("dve", cp(colv[l][:], PS[0][:, 0:128]), r=psk(0), w=[f"colv{l}"])
        P.dma("sp", dtb[l][:], dt_bias[l].partition_broadcast(128), w=[f"dtb{l}"])
        P.dma("sp", nexpA[l][:], a_log[l].partition_broadcast(128), w=[f"nexpA{l}"])
        P.op("act", actf(nexpA[l][:], nexpA[l][:], AF.Exp), r=[f"nexpA{l}"], w=[f"nexpA{l}"])
        P.op("dve", ts(nexpA[l][:], nexpA[l][:], -1.0, ALU.mult), r=[f"nexpA{l}"], w=[f"nexpA{l}"])
    CV_CW, CV_NM, CV_NP, CV_PS, CV_GN = 0, 96, 104, 112, 120

    def make_h(l, c, xt, xkey, bufs, P=P, bank=2, sfx=""):
        tw, to = TW[c], TO[c]
        junk, ssq, hb = bufs
        P.op("act", actf(junk[:tw, :], xt[:tw, :], AF.Square, accum=ssq[:tw, 0:1]), r=[xkey],
             w=["junk" + sfx, "ssq" + sfx])
        P.op("act", actf(ssq[:tw, 1:2], ssq[:tw, 0:1], AF.Sqrt, scale=1.0 / D, bias=epsc[:tw, 0:1]),
             r=["ssq" + sfx, "epsc"], w=["ssq1" + sfx])
        P.op("dve", rcp(ssq[:tw, 2:3], ssq[:tw, 1:2]), r=["ssq1" + sfx], w=["ssq2" + sfx])
        P.op("dve", ts(hb[:tw, :], xt[:tw, :], ssq[:tw, 2:3], ALU.mult), r=[xkey, "ssq2" + sfx], w=["hb" + sfx])
        pst = PS[bank][:].bitcast(BF16).rearrange("p (k t) -> p k t", k=8)
        for k in range(8):
            P.op("pe", trp(pst[:, k, :tw], hb[:tw, k * 128:(k + 1) * 128], identb[:tw, :tw]), r=["hb" + sfx, "CB"],
                 w=psk(bank))
        P.op("dve", tt(hT[:, :, to:to + tw], pst[:, :, :tw],
                       colv[l][:, CV_NM:CV_NM + 8].unsqueeze(2).to_broadcast([128, 8, tw]), ALU.mult),
             r=psk(bank) + [f"colv{l}"], w=[("hT", c)])

    def phase0():
        wreset()
        xts = [walloc([128, D], F32, "xt") for _ in range(2)]
        junk = walloc([128, D], BF16, "junk")
        ssq = walloc([128, 4], F32, "ssq")
        hb = walloc([128, D], BF16, "hb")
        for c in range(NT):
            xt = xts[c % 2]
            P.dma("sp", xt[:TW[c], :], x_all[TO[c]:TO[c] + TW[c], :], w=[f"xt{c % 2}"])
            make_h(0, c, xt, f"xt{c % 2}", (junk, ssq, hb))

    def phaseA(l, hh):
        wreset()
        wl = w_in[l]
        WB = 0 if hh == 0 else 8
        A2B = 8 if hh == 0 else 0
        S_Q, S_K, S_V, S_Z = WB, WB + 2, WB + 4, WB + 6
        if hh == 0:
            pass
        else:
            load_w(15, wl, C_Z + 512 + 256, 256)
        Wab = walloc([128, 8, 16], BF16, "Wab")
        P.dma("pool", Wab[:], wl.rearrange("(k p) c -> p k c", p=128)[:, :, C_A:C_A + 16], w=["Wab"])
        Dg = walloc([128, 12, 4, 128], BF16, "Dg")
        for qkv in range(3):
            cwv = colv[l][:, CV_CW:CV_CW + 96].rearrange("p (t j) -> p j t", t=4)[:, qkv * 8 + 4 * hh:qkv * 8 + 4 * hh + 4, :]
            P.op("dve", tt(Dg[:, qkv * 4:(qkv + 1) * 4, :, :],
                           identf.unsqueeze(1).unsqueeze(1).to_broadcast([128, 4, 4, 128]),
                           cwv.unsqueeze(3).to_broadcast([128, 4, 4, 128]), ALU.mult),
                 r=["CF", f"colv{l}"], w=["Dg"])
        a2 = [AR_OFF + A2B * 4096, AR_OFF + (A2B + 7) * 4096]

        def alloc2(name, shape, dt, always=False):
            nbytes = (int(np.prod(shape[1:])) * (4 if dt == F32 else 2) + 63) // 64 * 64
            i = 1 if always else 0
            off = a2[i]
            a2[i] += nbytes
            lim = AR_OFF + (A2B + 8) * 4096 if always else AR_OFF + (A2B + 7) * 4096
            assert a2[i] <= lim, (name, a2[i], lim)
            uid[0] += 1
            return alloc(f"{name}_{uid[0]}", shape, dt, at=off)

        H4 = 4 * hh
        gall = walloc([128, NT, 4], F32, "gall")
        ball = walloc([128, NT, 4], F32, "ball")
        sc3all = walloc([128, NT, 12], F32, "sc3all")
        eall = walloc([128, NT, 12], F32, "eall")
        negg = walloc([128, NT, 4], F32, "negg")
        psab = PS[0][:, 0:NT * 16].rearrange("p (c j) -> p c j", c=NT)
        for c in range(NT):
            for k in range(8):
                P.op("pe", mm(psab[:TW[c], c, :], hT[:, k, TO[c]:TO[c] + TW[c]], Wab[:, k, :], start=(k == 0),
                              stop=(k == 7)), r=["Wab", ("hT", c)], w=psk(0))
        P.op("dve", tt(gall[:], psab[:, :, H4:H4 + 4], dtb[l][:, H4:H4 + 4].unsqueeze(1).to_broadcast([128, NT, 4]),
                       ALU.add), r=psk(0) + [f"dtb{l}"], w=["gall"])
        P.op("act", actf(gall[:], gall[:], AF.Exp), r=["gall"], w=["gall"])
        P.op("act", actf(gall[:], gall[:], AF.Ln, bias=onec[:, 0:1]), r=["gall", "onec"], w=["gall"])
        P.op("dve", tt(gall[:], gall[:], nexpA[l][:, H4:H4 + 4].unsqueeze(1).to_broadcast([128, NT, 4]), ALU.mult),
             r=["gall", f"nexpA{l}"], w=["gall"])
        P.op("act", actf(ball[:], psab[:, :, 8 + H4:8 + H4 + 4], AF.Exp, scale=-1.0), r=psk(0), w=["ball"])
        P.op("dve", ts(ball[:], ball[:], 1.0, ALU.add), r=["ball"], w=["ball"])
        P.op("dve", rcp(ball[:], ball[:]), r=["ball"], w=["ball"])
        psg = PS[1][:, 0:NT * 8].rearrange("p (c j) -> p c j", c=NT)
        for c in range(NT):
            tw_ = TW[c]
            utri_ = CF[:tw_, CF_UTRIS:CF_UTRIS + tw_] if c == 16 else CF[:tw_, CF_UTRI:CF_UTRI + tw_]
            ones__ = CF[:tw_, CF_ONESS:CF_ONESS + tw_] if c == 16 else CF[:tw_, CF_ONES:CF_ONES + tw_]
            P.op("pe", mm(psg[:tw_, c, 0:4], utri_, gall[:tw_, c, :]), r=["CF", "gall"], w=psk(1))
            P.op("pe", mm(psg[:tw_, c, 4:8], ones__, gall[:tw_, c, :]), r=["CF", "gall"], w=psk(1))
        P.op("dve", cp(sc3all[:, :, 0:4], psg[:, :, 0:4]), r=psk(1), w=["sc3all"])
        P.op("dve", tt(sc3all[:, :, 4:8], psg[:, :, 4:8], sc3all[:, :, 0:4], ALU.subtract), r=psk(1) + ["sc3all"],
             w=["sc3all"])
        P.op("dve", cp(sc3all[:, :, 8:12], psg[:, :, 4:8]), r=psk(1), w=["sc3all"])
        P.op("act", actf(eall[:], sc3all[:], AF.Exp), r=["sc3all"], w=["eall"])
        P.op("dve", ts(negg[:], sc3all[:, :, 0:4], -1.0, ALU.mult), r=["sc3all"], w=["negg"])
        tabs = (gall, ball, eall, negg)

        recs = [Recorder() for _ in range(2)]
        for th in range(2):
            for _ in phaseA_thread(recs[th], l, hh, th, Wab, Dg, (S_Q, S_K, S_V, S_Z), alloc2, tabs):
                pass
        merge_threads(P, recs)
        smp_keys = [n + f"_{t}" for t in range(2)
                    for n in ("full", "sc16", "SS", "SSb", "kmask", "qmask", "uexp", "eGlS", "gexp")]

        def prefetch(s0, wmat, col0, ncols):
            n = ncols // 256
            src = wmat.rearrange("(k p) c -> p k c", p=128)
            for i in range(n):
                P.dma("pool", wslot(s0 + i, 1, 256), src[:, :, col0 + i * 256:col0 + (i + 1) * 256],
                      w=ark(s0 + i) + smp_keys)

        if hh == 0:
            prefetch(8, wl, C_Q + 512, 512)
            prefetch(10, wl, C_K + 512, 512)
            prefetch(12, wl, C_V + 512, 512)
            prefetch(14, wl, C_Z + 512, 256)
        else:
            prefetch(0, w_in[l], C_GA, 1024)
            prefetch(4, w_proj_a[l], 0, 768)

    def phaseA_thread(P, l, hh, th, Wab, Dg, slots, alloc2, tabs):
        gall, ball, eall, negg = tabs
        NHT = 2
        HW = NHT * 128
        H0 = 4 * hh + NHT * th
        S_Q, S_K, S_V, S_Z = slots
        T_ = f"_{th}"
        B = [4 * th + i for i in range(4)]
        b0, b1, b2, b3 = B

        def K_(name):
            return name + T_

        xpre = [walloc([128, 6, 131], BF16, "xpre") for _ in range(2)]
        qkvs = walloc([128, 768], F32, "qkvs")
        F4a = walloc([128, 512], F32, "F4a")
        QN = walloc([128, HW], BF16, "QN")
        KN = walloc([128, HW], BF16, "KN")
        KB = walloc([128, HW], BF16, "KB")
        KT = walloc([128, HW], BF16, "KT")
        VB = walloc([128, HW], BF16, "VB")
        QKT = walloc([128, 6, 128], BF16, "QKT")
        E = walloc([128, NHT, 128], F32, "E")
        EM2 = walloc([128, NHT, 2, 128], BF16, "EM2")
        EMd = walloc([128, NHT, 128], BF16, "EMd")
        X = [walloc([128, NHT, 2, 128], BF16, "X") for _ in range(2)]
        R = [walloc([128, NHT, 128], BF16, "R") for _ in range(2)]
        qkmT = walloc([128, NHT, 128], BF16, "qkmT")
        Sst = walloc([128, NHT, 128], F32, "S")
        Sb = walloc([128, NHT, 128], BF16, "Sb")
        rhs2 = walloc([128, HW], BF16, "rhs2")
        ubf = walloc([128, HW], BF16, "ubf")
        tmp2 = walloc([128, HW], F32, "tmp2")
        ofp = walloc([128, HW], F32, "o")
        zs = walloc([128, HW], F32, "zs")
        ofb = walloc([128, HW], BF16, "of")
        sc = walloc([128, 96], F32, "sc")
        full = alloc2("full", [128, 6, 112], BF16)
        sc16 = alloc2("sc16", [128, 768], BF16)
        SS = alloc2("SS", [128, 4, NHT, 128], F32)
        SSb = alloc2("SSb", [128, 4, NHT, 128], BF16)
        kmask = alloc2("kmask", [128, NHT, 4, 64], BF16)
        qmask = alloc2("qmask", [128, NHT, 4, 64], BF16)
        uexp = alloc2("uexp", [128, NHT, 4, 128], BF16)
        eGlS = alloc2("eGlS", [128, 16 * NHT], F32)
        gexp = alloc2("gexp", [128, 16 * NHT], F32)
        NTs = alloc2("NTs", [128, NHT, 128], BF16, always=True)
        rbf = alloc2("rbf", [128, HW], BF16, always=True)
        qse = alloc2("qse", [128, HW], F32, always=True)

        APRE, EA, SP_, G4, EB, BETA = 0, 4, 8, 12, 16, 20
        SC3, EALL, SSQ8, RS8, CQ, CKB, CKT, NCBG, SSO, RSO, NEGG = 24, 36, 48, 56, 64, 68, 72, 76, 80, 84, 88

        P.op("dve", mset(Sst[:], 0.0), w=[K_("S")])
        P.op("dve", mset(Sb[:], 0.0), w=[K_("Sb")])
        P.op("dve", mset(xpre[0][:, :, 0:3], 0.0), w=[K_("xpre0")])

        wq = wview(S_Q, 512, th * 256, 256)
        wk = wview(S_K, 512, th * 256, 256)
        wv = wview(S_V, 512, th * 256, 256)
        wz = wview(S_Z, 512, th * 256, 256)
        wqkv = [wq, wk, wv]

        def jg(j):
            return (j // 2) * 4 + 2 * th + (j % 2)

        for qkv in range(3):
            c0 = qkv * 1024 + hh * 512 + th * 256
            P.dma("pool", sc16[0:48, qkv * 256:(qkv + 1) * 256], sconv[l][:, c0:c0 + 256], w=[K_("sc16")])
        pst3 = PS[b3][:].bitcast(BF16)[:, 0:6 * 48].rearrange("p (j t) -> p j t", j=6)
        for j in range(6):
            P.op("pe", trp(pst3[:, j, :], sc16[0:48, j * 128:(j + 1) * 128], identb[0:48, 0:48]),
                 r=[K_("sc16"), "CB"], w=psk(b3))
        P.op("act", acp(out=full[:, :, 0:48], in_=pst3), r=psk(b3), w=[K_("full")])
        yield

        def psa(j, tw):
            return PS[b0][:, j * 128:j * 128 + tw] if j < 4 else PS[b1][:, (j - 4) * 128:(j - 4) * 128 + tw]

        def psb_bank(j):
            return b2 if j < 4 else b3

        def emit_A1(c_):
            tw_, to_ = TW[c_], TO[c_]
            for j in range(6):
                wvw, wkey = wqkv[j // 2]
                for k in range(8):
                    P.op("pe", mm(psa(j, tw_), wvw[:, k, (j % 2) * 128:(j % 2 + 1) * 128], hT[:, k, to_:to_ + tw_],
                                  start=(k == 0), stop=(k == 7)), r=[wkey, ("hT", c_)], w=psk(b0 if j < 4 else b1))

        def emit_E(c_):
            tw_ = TW[c_]
            utri_ = CF[:tw_, CF_UTRIS:CF_UTRIS + tw_] if c_ == 16 else CF[:tw_, CF_UTRI:CF_UTRI + tw_]
            psgr = PS[b3][:, 256:512].rearrange("p (h f) -> p h f", h=NHT)
            for h in range(NHT):
                P.op("pe", mm(psgr[:tw_, h, :tw_],
                              gall[:tw_, c_, 2 * th + h:2 * th + h + 1].to_broadcast([tw_, tw_]), utri_),
                     r=["gall", "CF"], w=psk(b3))
            for h in range(NHT):
                P.op("act", actf(E[:tw_, h, :tw_], psgr[:tw_, h, :tw_], AF.Abs,
                                 bias=negg[:tw_, c_, 2 * th + h:2 * th + h + 1]), r=psk(b3) + ["negg"], w=[K_("E")])
            P.op("act", actf(E[:tw_, :, :tw_], E[:tw_, :, :tw_], AF.Exp, scale=-1.0), r=[K_("E")], w=[K_("E")])

        order = [16] + list(range(16))
        emit_A1(order[0])
        emit_E(order[0])
        for oi, c in enumerate(order):
            tw, to = TW[c], TO[c]
            smp = (c == 16)
            xk = K_(f"xpre{c % 2}")
            xcur, xprev = xpre[c % 2], xpre[(c + 1) % 2]
            hkey = ("hT", c)
            yield
            dstx = full if smp else xcur
            dkey = K_("full") if smp else xk
            c_off = 48 if smp else 3
            P.op("act", acp(out=dstx[:, 0:4, c_off:c_off + tw],
                            in_=PS[b0][:].rearrange("p (j t) -> p j t", j=4)[:, :, :tw]), r=psk(b0), w=[dkey])
            P.op("act", acp(out=dstx[:, 4:6, c_off:c_off + tw],
                            in_=PS[b1][:, 0:256].rearrange("p (j t) -> p j t", j=2)[:, :, :tw]), r=psk(b1), w=[dkey])
            if not smp and c > 0:
                P.op("dve", cp(xcur[:, :, 0:3], xprev[:, :, 128:131]), r=[K_(f"xpre{(c + 1) % 2}")], w=[xk])
            XS, shift, xskey = (full, 16, K_("full")) if smp else (xcur, 1, xk)
            for j in range(6):
                dst = PS[b2][:tw, j * 128:(j + 1) * 128] if j < 4 else PS[b3][:tw, (j - 4) * 128:(j - 3) * 128]
                for tap in range(4):
                    P.op("pe", mm(dst, XS[:, j, tap * shift:tap * shift + tw], Dg[:, jg(j), tap, :],
                                  start=(tap == 0), stop=(tap == 3)), r=[xskey, "Dg"], w=psk(psb_bank(j)))
            yield
            wvz, kz = wz
            for k in range(8):
                P.op("pe", mm(PS[b3][:tw, 256:512], hT[:, k, to:to + tw], wvz[:, k, :], start=(k == 0), stop=(k == 7)),
                     r=[hkey, kz], w=psk(b3))
            P.op("act", actf(qkvs[:tw, 0:512], PS[b2][:tw, :], AF.Silu), r=psk(b2), w=[K_("qkvs")])
            P.op("act", actf(qkvs[:tw, 512:768], PS[b3][:tw, 0:256], AF.Silu), r=psk(b3), w=[K_("qkvs")])
            P.op("act", actf(zs[:tw, :], PS[b3][:tw, 256:512], AF.Silu), r=psk(b3), w=[K_("zs")])
            N2 = NHT
            yield
            yield
            eG = eall[:tw, c, 2 * th:2 * th + 2]
            eGlG = eall[:tw, c, 4 + 2 * th:4 + 2 * th + 2]
            beta2 = ball[:tw, c, 2 * th:2 * th + 2]
            for j_ in range(4):
                P.op("act", actf(F4a[:tw, j_ * 128:(j_ + 1) * 128], qkvs[:tw, j_ * 128:(j_ + 1) * 128], AF.Square,
                                 accum=sc[:tw, SSQ8 + j_:SSQ8 + j_ + 1]), r=[K_("qkvs")],
                     w=[K_("F4a"), K_("sc_ssq")])
            P.op("act", actf(sc[:tw, SSQ8:SSQ8 + 4], sc[:tw, SSQ8:SSQ8 + 4], AF.Sqrt, bias=epsc[:tw, 0:1]),
                 r=[K_("sc_ssq"), "epsc"], w=[K_("sc_ssq")])
            P.op("dve", rcp(sc[:tw, RS8:RS8 + 4], sc[:tw, SSQ8:SSQ8 + 4]), r=[K_("sc_ssq")], w=[K_("sc_rs")])
            RSK = RS8 + N2
            P.op("dve", ts(sc[:tw, CQ:CQ + N2], sc[:tw, RS8:RS8 + N2], 128.0 ** -0.5, ALU.mult), r=[K_("sc_rs")],
                 w=[K_("sc_cq")])
            P.op("dve", tt(sc[:tw, CKB:CKB + N2], sc[:tw, RSK:RSK + N2], beta2, ALU.mult),
                 r=[K_("sc_rs"), "ball"], w=[K_("sc_ckb")])
            P.op("dve", tt(sc[:tw, CKT:CKT + N2], sc[:tw, RSK:RSK + N2], eGlG, ALU.mult),
                 r=[K_("sc_rs"), "eall"], w=[K_("sc_ckt")])
            P.op("dve", stt(sc[:tw, NCBG:NCBG + N2], beta2, -1.0, eG, ALU.mult, ALU.mult),
                 r=["ball", "eall"], w=[K_("sc_ncbg")])

            def bcs(ap2):
                return ap2.unsqueeze(2).to_broadcast([tw, N2, 128])

            def bc(col):
                return sc[:tw, col:col + N2].unsqueeze(2).to_broadcast([tw, N2, 128])

            def v3(t_, c0=0):
                return t_[:tw, c0:c0 + HW].rearrange("p (h d) -> p h d", h=N2)

            P.op("dve", tt(v3(QN), v3(qkvs, 0), bc(CQ), ALU.mult), r=[K_("qkvs"), K_("sc_cq")], w=[K_("QN")])
            P.op("dve", tt(v3(KN), v3(qkvs, 256), bc(RSK), ALU.mult), r=[K_("qkvs"), K_("sc_rs")], w=[K_("KN")])
            P.op("dve", tt(v3(KB), v3(qkvs, 256), bc(CKB), ALU.mult), r=[K_("qkvs"), K_("sc_ckb")], w=[K_("KB")])
            P.op("dve", tt(v3(KT), v3(qkvs, 256), bc(CKT), ALU.mult), r=[K_("qkvs"), K_("sc_ckt")], w=[K_("KT")])
            P.op("dve", tt(v3(VB), v3(qkvs, 512), bcs(beta2), ALU.mult), r=[K_("qkvs"), "ball"], w=[K_("VB")])
            yield
            pst0 = PS[b0][:].bitcast(BF16).rearrange("p (j t) -> p j t", j=8)
            for i, (src, skey) in enumerate(((QN, "QN"), (KN, "KN"), (KB, "KB"))):
                for h in range(N2):
                    P.op("pe", trp(pst0[:, i * 2 + h, :tw], src[:tw, h * 128:(h + 1) * 128], identb[:tw, :tw]),
                         r=[K_(skey), "CB"], w=psk(b0))
            P.op("act", acp(out=QKT[:, :, :tw], in_=pst0[:, 0:6, :tw]), r=psk(b0), w=[K_("QKT")])
            yield
            psK = PS[b2][:].rearrange("p (h c f) -> p h c f", h=2, c=2)
            psQ = PS[b3][:, 0:256].rearrange("p (h f) -> p h f", h=N2)
            for h in range(N2):
                P.op("pe", mm(psK[:tw, h, 0, :tw], QKT[:, 2 + h, :tw], QKT[:, 4 + h, :tw]), r=[K_("QKT")], w=psk(b2))
                P.op("pe", mm(psK[:tw, h, 1, :tw], QKT[:, 4 + h, :tw], QKT[:, 2 + h, :tw]), r=[K_("QKT")], w=psk(b2))
                P.op("pe", mm(psQ[:tw, h, :tw], QKT[:, 2 + h, :tw], QKT[:, 0 + h, :tw]), r=[K_("QKT")], w=psk(b3))
            if smp:
                nm2 = CF[:tw, CF_NM2S:CF_NM2S + 128].rearrange("p (c f) -> p c f", c=2)
                mud = CF[:tw, CF_MUDS:CF_MUDS + 64]
            else:
                nm2 = CF[:tw, CF_NM2:CF_NM2 + 256].rearrange("p (c f) -> p c f", c=2)
                mud = CF[:tw, CF_MUD:CF_MUD + 128]
            P.op("dve", tt(EM2[:tw, :, :, :tw], E[:tw, :, :tw].unsqueeze(2).to_broadcast([tw, N2, 2, tw]),
                           nm2.unsqueeze(1).to_broadcast([tw, N2, 2, tw]), ALU.mult), r=[K_("E"), "CF"], w=[K_("EM2")])
            P.op("dve", tt(EMd[:tw, :, :tw], E[:tw, :, :tw], mud.unsqueeze(1).to_broadcast([tw, N2, tw]), ALU.mult),
                 r=[K_("E"), "CF"], w=[K_("EMd")])
            P.op("dve", tt(X[0][:tw, :, :, :tw], psK[:tw, :, :, :tw], EM2[:tw, :, :, :tw], ALU.mult),
                 r=psk(b2) + [K_("EM2")], w=[K_("X0"), K_("X0") + "b"])
            P.op("dve", tt(qkmT[:tw, :, :tw], psQ[:tw, :, :tw], EMd[:tw, :, :tw], ALU.mult), r=psk(b3) + [K_("EMd")],
                 w=[K_("qkmT")])
            P.op("act", acp(out=NTs[:tw, :, :tw], in_=X[0][:tw, :, 0, :tw]), r=[K_("X0")], w=[K_("NTs")])
            P.op("dve", tt(R[0][:tw, :, :tw], X[0][:tw, :, 0, :tw],
                           identb[:tw, :tw].unsqueeze(1).to_broadcast([tw, N2, tw]), ALU.add), r=[K_("X0"), "CB"],
                 w=[K_("R0")])
            yield
            L = 1 if smp else 6
            psKT = PS[b2][:, 0:256].rearrange("p (h f) -> p h f", h=N2)
            psKN = PS[b1][:, 256:512].rearrange("p (h f) -> p h f", h=N2)
            for lev in range(1, L + 1):
                xo, xn = X[(lev - 1) % 2], X[lev % 2]
                xok, xnk = K_(f"X{(lev - 1) % 2}"), K_(f"X{lev % 2}")
                ro, rn = R[(lev - 1) % 2], R[lev % 2]
                rok, rnk = K_(f"R{(lev - 1) % 2}"), K_(f"R{lev % 2}")
                last = (lev == L)
                for h in range(N2):
                    P.op("pe", mm(psKN[:tw, h, :tw], xo[:tw, h, 0, :tw], xo[:tw, h, 1, :tw]), r=[xok, xok + "b"],
                         w=psk(b1))
                if not last:
                    for h in range(N2):
                        P.op("pe", mm(psKT[:tw, h, :tw], xo[:tw, h, 1, :tw], xo[:tw, h, 0, :tw]), r=[xok, xok + "b"],
                             w=psk(b2))
                P.op("act", acp(out=xn[:tw, :, 1, :tw], in_=psKN[:tw, :, :tw]), r=psk(b1), w=[xnk + "b"])
                if not last:
                    P.op("dve", cp(xn[:tw, :, 0, :tw], psKT[:tw, :, :tw]), r=psk(b2), w=[xnk])
                for h in range(N2):
                    P.op("pe", mm(psQ[:tw, h, :tw], xn[:tw, h, 1, :tw], ro[:tw, h, :tw]), r=[xnk + "b", rok],
                         w=psk(b3))
                P.op("dve", tt(rn[:tw, :, :tw], ro[:tw, :, :tw], psQ[:tw, :, :tw], ALU.add), r=[rok] + psk(b3),
                     w=[rnk])
                yield
            TTm, ttk = R[L % 2], K_(f"R{L % 2}")
            ps_kS = PS[b0][:, 0:256].rearrange("p (h v) -> p h v", h=N2)
            ps_qS = PS[b0][:, 256:512].rearrange("p (h v) -> p h v", h=N2)
            ps_u = PS[b1][:, 0:256].rearrange("p (h v) -> p h v", h=N2)
            ps_au = PS[b1][:, 256:512].rearrange("p (h v) -> p h v", h=N2)
            ps_o2 = PS[b3][:, 256:512].rearrange("p (h v) -> p h v", h=N2)
            ps_ds = PS[b2][:, 0:256].rearrange("p (h v) -> p h v", h=N2)
            if not smp:
                for h in range(N2):
                    P.op("pe", mm(ps_kS[:tw, h, :], QKT[:, 2 + h, :tw], Sb[:, h, :]), r=[K_("QKT"), K_("Sb")],
                         w=psk(b0))
                    P.op("pe", mm(ps_qS[:tw, h, :], QKT[:, 0 + h, :tw], Sb[:, h, :]), r=[K_("QKT"), K_("Sb")],
                         w=psk(b0))
                P.op("dve", tt(v3(tmp2), ps_kS[:tw, :, :], bc(NCBG), ALU.mult), r=psk(b0) + [K_("sc_ncbg")],
                     w=[K_("tmp2")])
                P.op("dve", tt(v3(qse), ps_qS[:tw, :, :], bcs(eG), ALU.mult), r=psk(b0) + ["eall"],
                     w=[K_("qse")])
            else:
                segm = CB[:, CB_SEG:CB_SEG + 1024].rearrange("p (s t) -> p s t", s=16)
                P.op("dve", tt(gexp[:tw, :].rearrange("p (s h) -> p s h", s=16),
                               gall[:tw, c, 2 * th:2 * th + 2].unsqueeze(1).to_broadcast([tw, 16, N2]),
                               CF[:tw, CF_SEGT:CF_SEGT + 16].unsqueeze(2).to_broadcast([tw, 16, N2]), ALU.mult),
                     r=["gall", "CF"], w=[K_("gexp")])
                P.op("pe", mm(PS[b1][:, 0:16 * N2], CF[:tw, CF_ONES:CF_ONES + 128], gexp[:tw, :]),
                     r=["CF", K_("gexp")], w=psk(b1))
                P.op("act", actf(eGlS[:, :], PS[b1][:, 0:16 * N2], AF.Exp), r=psk(b1), w=[K_("eGlS")])
                for g in range(4):
                    for s_ in range(4):
                        P.dma("sp", SS[:, s_], sdelta[l, 4 * g + s_, H0:H0 + N2].rearrange("h k v -> k h v"),
                              w=[K_("SS")])
                    P.op("act", acp(out=SSb[:], in_=SS[:]), r=[K_("SS")], w=[K_("SSb")])
                    P.op("dve", tt(kmask[:], QKT[:, 2:4, :64].unsqueeze(2).to_broadcast([128, N2, 4, 64]),
                                   segm[:, 4 * g:4 * g + 4, :].unsqueeze(1).to_broadcast([128, N2, 4, 64]), ALU.mult),
                         r=[K_("QKT"), "CB"], w=[K_("kmask")])
                    P.op("dve", tt(qmask[:], QKT[:, 0:2, :64].unsqueeze(2).to_broadcast([128, N2, 4, 64]),
                                   segm[:, 4 * g:4 * g + 4, :].unsqueeze(1).to_broadcast([128, N2, 4, 64]), ALU.mult),
                         r=[K_("QKT"), "CB"], w=[K_("qmask")])
                    for s in range(4):
                        for h in range(N2):
                            first = (g == 0 and s == 0)
                            lastm = (g == 3 and s == 3)
                            P.op("pe", mm(PS[B[h]][:tw, 0:128], kmask[:, h, s, :], SSb[:, s, h, :], start=first,
                                          stop=lastm), r=[K_("kmask"), K_("SSb")], w=psk(B[h]))
                            P.op("pe", mm(PS[B[2 + h]][:tw, 0:128], qmask[:, h, s, :], SSb[:, s, h, :], start=first,
                                          stop=lastm), r=[K_("qmask"), K_("SSb")], w=psk(B[2 + h]))
                    yield
                for h in range(N2):
                    P.op("dve", ts(tmp2[:tw, h * 128:(h + 1) * 128], PS[B[h]][:tw, 0:128],
                                   sc[:tw, NCBG + h:NCBG + h + 1], ALU.mult), r=psk(B[h]) + [K_("sc_ncbg")],
                         w=[K_("tmp2")])
                    P.op("dve", ts(qse[:tw, h * 128:(h + 1) * 128], PS[B[2 + h]][:tw, 0:128],
                                   eall[:tw, c, 2 * th + h:2 * th + h + 1], ALU.mult), r=psk(B[2 + h]) + ["eall"],
                         w=[K_("qse")])
            P.op("dve", tt(rhs2[:tw, :], tmp2[:tw, :], VB[:tw, :], ALU.add), r=[K_("tmp2"), K_("VB")], w=[K_("rhs2")])
            for h in range(N2):
                P.op("pe", mm(ps_u[:tw, h, :], TTm[:tw, h, :tw], rhs2[:tw, h * 128:(h + 1) * 128]),
                     r=[ttk, K_("rhs2")], w=psk(b1))
            P.op("act", acp(out=ubf[:tw, :], in_=PS[b1][:tw, 0:256]), r=psk(b1), w=[K_("ubf")])
            yield
            for h in range(N2):
                P.op("pe", mm(ps_au[:tw, h, :], NTs[:tw, h, :tw], ubf[:tw, h * 128:(h + 1) * 128]),
                     r=[K_("NTs"), K_("ubf")], w=psk(b1))
            P.op("dve", tt(tmp2[:tw, :], rhs2[:tw, :], ubf[:tw, :], ALU.subtract), r=[K_("rhs2"), K_("ubf")],
                 w=[K_("tmp2")])
            P.op("dve", tt(rbf[:tw, :], tmp2[:tw, :], PS[b1][:tw, 256:512], ALU.add), r=[K_("tmp2")] + psk(b1),
                 w=[K_("rbf")])
            for h in range(N2):
                P.op("pe", mm(ps_u[:tw, h, :], TTm[:tw, h, :tw], rbf[:tw, h * 128:(h + 1) * 128]), r=[ttk, K_("rbf")],
                     w=psk(b1))
            P.op("dve", tt(ubf[:tw, :], ubf[:tw, :], PS[b1][:tw, 0:256], ALU.add), r=[K_("ubf")] + psk(b1),
                 w=[K_("ubf")])
            yield
            for h in range(N2):
                P.op("pe", mm(ps_o2[:tw, h, :], qkmT[:tw, h, :tw], ubf[:tw, h * 128:(h + 1) * 128]),
                     r=[K_("qkmT"), K_("ubf")], w=psk(b3))
            P.op("dve", tt(ofp[:tw, :], qse[:tw, :], PS[b3][:tw, 256:512], ALU.add), r=[K_("qse")] + psk(b3),
                 w=[K_("o")])
            if not smp:
                for h in range(N2):
                    P.op("pe", mm(ps_ds[:, h, :], KT[:tw, h * 128:(h + 1) * 128], ubf[:tw, h * 128:(h + 1) * 128]),
                         r=[K_("KT"), K_("ubf")], w=psk(b2))
                for h in range(N2):
                    P.op("dve", stt(Sst[:, h, :], Sst[:, h, :], eall[:, c, 8 + 2 * th + h:8 + 2 * th + h + 1],
                                    ps_ds[:, h, :], ALU.mult, ALU.add), r=[K_("S"), "eall"] + psk(b2), w=[K_("S")])
                P.op("act", acp(out=Sb[:], in_=Sst[:]), r=[K_("S")], w=[K_("Sb")])
                if c == 15:
                    P.dma("sp", delta_p[l, H0:H0 + N2].rearrange("h k v -> k h v"), Sst[:], r=[K_("S")],
                          is_output=True)
            else:
                segT = CF[:tw, CF_SEGT:CF_SEGT + 16]
                for g in range(4):
                    for s_ in range(4):
                        P.dma("sp", SS[:, s_], sdelta[l, 4 * g + s_, H0:H0 + N2].rearrange("h k v -> k h v"),
                              w=[K_("SS")])
                    P.op("dve", tt(uexp[:tw, :, :, :],
                                   ubf[:tw, :].rearrange("p (h v) -> p h v", h=N2).unsqueeze(2).to_broadcast([tw, N2, 4, 128]),
                                   segT[:, 4 * g:4 * g + 4].unsqueeze(1).unsqueeze(3).to_broadcast([tw, N2, 4, 128]),
                                   ALU.mult), r=[K_("ubf"), "CF"], w=[K_("uexp")])
                    for h in range(N2):
                        bank = B[2 + h]
                        psd = PS[bank][:].rearrange("p (s v) -> p s v", s=4)
                        P.op("pe", mm(PS[bank][:, :], KT[:tw, h * 128:(h + 1) * 128], uexp[:tw, h, :, :]),
                             r=[K_("KT"), K_("uexp")], w=psk(bank))
                        egl = eGlS[:, :].rearrange("p (s h) -> p s h", s=16)[:, 4 * g:4 * g + 4, h]
                        P.op("dve", tt(SS[:, :, h, :], SS[:, :, h, :], egl.unsqueeze(2).to_broadcast([128, 4, 128]),
                                       ALU.mult), r=[K_("SS"), K_("eGlS")], w=[K_("SS")])
                        P.op("dve", tt(SS[:, :, h, :], SS[:, :, h, :], psd, ALU.add), r=[K_("SS")] + psk(bank),
                             w=[K_("SS")])
                    for s_ in range(4):
                        P.dma("sp", delta_s[l, 4 * g + s_, H0:H0 + N2].rearrange("h k v -> k h v"), SS[:, s_],
                              r=[K_("SS")], is_output=True)
                    yield
            yield
            if oi + 1 < len(order):
                emit_A1(order[oi + 1])
            for j_ in range(N2):
                P.op("act", actf(F4a[:tw, j_ * 128:(j_ + 1) * 128], ofp[:tw, j_ * 128:(j_ + 1) * 128], AF.Square,
                                 accum=sc[:tw, SSO + j_:SSO + j_ + 1]), r=[K_("o")], w=[K_("F4a"), K_("sc_sso")])
            P.op("act", actf(sc[:tw, SSO:SSO + N2], sc[:tw, SSO:SSO + N2], AF.Sqrt, scale=1.0 / 128,
                             bias=epsc[:tw, 0:1]), r=[K_("sc_sso"), "epsc"], w=[K_("sc_sso")])
            P.op("dve", rcp(sc[:tw, RSO:RSO + N2], sc[:tw, SSO:SSO + N2]), r=[K_("sc_sso")], w=[K_("sc_rso")])
            if oi + 1 < len(order):
                emit_E(order[oi + 1])
            P.op("dve", tt(v3(tmp2), v3(ofp), bc(RSO), ALU.mult), r=[K_("o"), K_("sc_rso")], w=[K_("tmp2")])
            P.op("dve", tt(ofb[:tw, :], tmp2[:tw, :], zs[:tw, :], ALU.mult), r=[K_("tmp2"), K_("zs")], w=[K_("of")])
            pso = PS[b2][:].bitcast(BF16).rearrange("p (j t) -> p j t", j=8)
            for h in range(N2):
                P.op("pe", trp(pso[:, h, :tw], ofb[:tw, h * 128:(h + 1) * 128], identb[:tw, :tw]),
                     r=[K_("of"), "CB"], w=psk(b2))
            P.op("act", actf(oT[:, H0:H0 + N2, to:to + tw], pso[:, 0:N2, :tw], AF.Copy,
                             scale=colv[l][:, CV_GN:CV_GN + 1]), r=psk(b2) + [f"colv{l}"], w=[("oT", c, hh, th)])
            yield
            if c == 15 or smp:
                lo = to + 125 if c == 15 else to
                n = 3 if c == 15 else 64
                for qkv in range(3):
                    wvw, wkey = wqkv[qkv]
                    dst = PS[b2][:n, qkv * 256:(qkv + 1) * 256] if qkv < 2 else PS[b3][:n, 0:256]
                    for k in range(8):
                        P.op("pe", mm(dst, hT[:, k, lo:lo + n], wvw[:, k, :], start=(k == 0), stop=(k == 7)),
                             r=[hkey, wkey], w=psk(b2 if qkv < 2 else b3))
                P.op("dve", cp(qkvs[:n, 0:512], PS[b2][:n, :]), r=psk(b2), w=[K_("qkvs")])
                P.op("dve", cp(qkvs[:n, 512:768], PS[b3][:n, 0:256]), r=psk(b3), w=[K_("qkvs")])
                for qkv in range(3):
                    c0 = qkv * 1024 + hh * 512 + th * 256
                    if c == 15:
                        P.dma("sp", conv_p[l, :, c0:c0 + 256], qkvs[0:3, qkv * 256:(qkv + 1) * 256], r=[K_("qkvs")],
                              is_output=True)
                    else:
                        P.dma("sp", conv_s[l, :, :, c0:c0 + 256].rearrange("j s c -> (j s) c"),
                              qkvs[16:64, qkv * 256:(qkv + 1) * 256], r=[K_("qkvs")], is_output=True)
                yield

    GROUPS = [(0, 512), (512, 512), (1024, 512), (1536, 512), (2048, 64)]

    def gkeys(name, g0, n):
        return [(name, c) for c in range(g0 // 128, (g0 + n + 127) // 128)]

    def phaseC1(l):
        wreset()
        S_GA, S_PA = 0, 4
        load_w(7, w_proj_a[l], 768, 256)
        sg = [walloc([128, 512], F32, "sg") for _ in range(2)]
        mag = walloc([128, 8, 512], BF16, "mag")
        for (g0, n) in GROUPS:
            hk = gkeys("hT", g0, n)
            ok = [(k_[0], k_[1], hh, th_) for k_ in gkeys("oT", g0, n) for hh in range(2) for th_ in range(2)]
            for cc in range(8):
                wga, kga = wview(S_GA, 1024, cc * 128, 128)
                wpa, kpa = wview(S_PA, 1024, cc * 128, 128)
                ba, by = cc % 4, 4 + cc % 4
                for k in range(8):
                    P.op("pe", mm(PS[ba][:, :n], wga[:, k, :], hT[:, k, g0:g0 + n], start=(k == 0), stop=(k == 7)),
                         r=[kga] + hk, w=psk(ba))
                for k in range(8):
                    P.op("pe", mm(PS[by][:, :n], wpa[:, k, :], oT[:, k, g0:g0 + n], start=(k == 0), stop=(k == 7)),
                         r=[kpa] + ok, w=psk(by))
                P.op("act", actf(sg[cc % 2][:, :n], PS[ba][:, :n], AF.Sigmoid), r=psk(ba), w=[f"sg{cc % 2}"])
                P.op("dve", tt(mag[:, cc, :n], sg[cc % 2][:, :n], PS[by][:, :n], ALU.mult),
                     r=[f"sg{cc % 2}"] + psk(by), w=["mag"])
            P.op("pool", cp(oT[:, :, g0:g0 + n], mag[:, :, :n]), r=["mag"], w=ok)
        load_w_wide(8, w_in[l], C_U, 1024)
        load_w(12, w_in[l], C_GP, 1024)

    def phaseB(l):
        wreset()
        S_U, S_GP, S_GB, S_PB = 8, 12, 0, 4
        Wp = walloc([128, 2, 4, 256], BF16, "Wp")
        for kk_ in range(2):
            P.dma("pool", Wp[:, kk_], pool_w[l][:, kk_ * 128:(kk_ + 1) * 128, :].rearrange("g p d -> p g d"),
                  w=["Wp"])
        for i_ in range(4):
            load_w(S_GB + i_, w_in[l], C_GB + 256 * i_, 256)
            load_w(S_PB + i_, w_proj_b[l], 256 * i_, 256)
        ub = [walloc([128, D], BF16, "ub") for _ in range(2)]
        u32 = walloc([128, D], F32, "u32")
        hb0 = walloc([128, D], BF16, "hb0")
        hb1 = walloc([128, D], BF16, "hb1")
        YT = walloc([128, 8, 128], BF16, "YT")
        ypT = walloc([128, 8, 512], BF16, "ypT")
        sgp = [walloc([128, 512], F32, "sgp") for _ in range(2)]
        gyT = walloc([128, 8, 512], BF16, "gyT")
        sgb = [walloc([128, 512], F32, "sgb") for _ in range(2)]
        mb = walloc([128, 512], F32, "mb")
        P.dma("pool", hb0[:, :], spool[l, 0:128, :], w=["hb0"])
        P.op("dve", mset(hb1[:, :], 0.0), w=["hb1"])
        P.dma("pool", hb1[0:64, :], spool[l, 128:192, :], w=["hb1"])
        P.dma("pool", hb1[64:112, :], spool[l, 192:240, :], w=["hb1"])
        ps_flat = pool_s[l].rearrange("j s c -> (j s) c")
        import os
        SKIP = os.environ.get("KB_SKIP", "").split(",")
        for (r0, nr) in (((64, 128), (192, 48)) if "hist" not in SKIP else ()):
            P.dma("sp", u32[0:nr, :], spool[l, r0:r0 + nr, :], w=["u32"])
            P.dma("sp", ps_flat[r0 - 64:r0 - 64 + nr, :], u32[0:nr, :], r=["u32"], is_output=True)

        def band(off, g, tw, w=128):
            return CB[:tw, off + g * w:off + g * w + (tw if w == 128 else w)]

        for (g0, n) in (GROUPS if "smp" not in SKIP else GROUPS[:-1]):
            c0 = g0 // 128
            nt = max(1, n // 128)
            for ti in range(nt):
                c = c0 + ti
                tw, to = TW[c], TO[c]
                smp = (c == 16)
                hkey = ("hT", c)
                ucur, uprev = ub[c % 2], ub[(c + 1) % 2]
                uk, upk = f"ub{c % 2}", f"ub{(c + 1) % 2}"
                for half in range(2):
                    wv_, wk_ = wwide(S_U, half)
                    for k in range(8):
                        P.op("pe", mm(PS[half][:tw, :], hT[:, k, to:to + tw], wv_[:, k, :],
                                      start=(k == 0), stop=(k == 7)), r=[hkey] + wk_, w=psk(half))
                for half in range(2):
                    P.op("act", acp(out=ucur[:tw, half * 512:(half + 1) * 512],
                                                            in_=PS[half][:tw, :]), r=psk(half), w=[uk])
                if (c == 15 or smp) and "pout" not in SKIP:
                    if smp:
                        for half in range(2):
                            P.op("dve", cp(u32[:tw, half * 512:(half + 1) * 512], PS[half][:tw, :]), r=psk(half),
                                 w=["u32"])
                    if c == 15:
                        for half in range(2):
                            wv_, wk_ = wwide(S_U, half)
                            for k in range(8):
                                P.op("pe", mm(PS[half][:15, :], hT[:, k, PT - 15:PT], wv_[:, k, :],
                                              start=(k == 0), stop=(k == 7)), r=[hkey] + wk_, w=psk(half))
                        for half in range(2):
                            P.op("dve", cp(u32[:15, half * 512:(half + 1) * 512], PS[half][:15, :]), r=psk(half),
                                 w=["u32"])
                        P.dma("sp", pool_p[l], u32[0:15, :], r=["u32"], is_output=True)
                    elif "spout" not in SKIP:
                        P.dma("sp", pool_s[l, 11:15].rearrange("j s c -> (j s) c"), u32[0:64, :], r=["u32"],
                              is_output=True)
                psy = [PS[2][:].rearrange("p (j t) -> p j t", j=4), PS[3][:].rearrange("p (j t) -> p j t", j=4)]
                for cc in range(8):
                    g = cc // 2
                    dst = psy[cc // 4][:, cc % 4, :tw]
                    lhs = ucur[:tw, cc * 128:(cc + 1) * 128]
                    if smp and "sband" in SKIP:
                        P.op("pe", mm(dst, lhs, CB[:64, CB_SN + g * 64:CB_SN + (g + 1) * 64], start=True, stop=True),
                             r=[uk, "CB"], w=psk(2 + cc // 4))
                    elif smp:
                        P.op("pe", mm(dst, lhs, CB[:64, CB_SN + g * 64:CB_SN + (g + 1) * 64], start=True, stop=False),
                             r=[uk, "CB"], w=psk(2 + cc // 4))
                        P.op("pe", mm(dst, hb0[:, cc * 128:(cc + 1) * 128], CB[:, CB_SH0 + g * 64:CB_SH0 + (g + 1) * 64],
                                      start=False, stop=False), r=["hb0", "CB"], w=psk(2 + cc // 4))
                        P.op("pe", mm(dst, hb1[:, cc * 128:(cc + 1) * 128],
                                      CB[:, CB_SH1 + g * 64:CB_SH1 + (g + 1) * 64], start=False, stop=True),
                             r=["hb1", "CB"], w=psk(2 + cc // 4))
                    elif c == 0:
                        P.op("pe", mm(dst, lhs, CB[:, CB_BF + g * 128:CB_BF + (g + 1) * 128]), r=[uk, "CB"],
                             w=psk(2 + cc // 4))
                    else:
                        P.op("pe", mm(dst, lhs, CB[:, CB_BC + g * 128:CB_BC + (g + 1) * 128], start=True, stop=False),
                             r=[uk, "CB"], w=psk(2 + cc // 4))
                        P.op("pe", mm(dst, uprev[:, cc * 128:(cc + 1) * 128],
                                      CB[:, CB_BP + g * 128:CB_BP + (g + 1) * 128], start=False, stop=True),
                             r=[upk, "CB"], w=psk(2 + cc // 4))
                for b in range(2):
                    P.op("act", acp(out=YT[:, 4 * b:4 * b + 4, :tw], in_=psy[b][:, :, :tw]),
                         r=psk(2 + b), w=["YT"])
                psl = [PS[4][:].rearrange("p (j t) -> p j t", j=4), PS[5][:].rearrange("p (j t) -> p j t", j=4)]
                for dc in range(8):
                    g = dc // 2
                    for kk in range(2):
                        P.op("pe", mm(psl[dc // 4][:, dc % 4, :tw], Wp[:, kk, g, (dc % 2) * 128:(dc % 2 + 1) * 128],
                                      YT[:, 2 * g + kk, :tw], start=(kk == 0), stop=(kk == 1)), r=["Wp", "YT"],
                             w=psk(4 + dc // 4))
                for b in range(2):
                    P.op("dve", tt(ypT[:, 4 * b:4 * b + 4, ti * 128:ti * 128 + tw], psl[b][:, :, :tw],
                                   colv[l][:, CV_PS + 4 * b:CV_PS + 4 * b + 4].unsqueeze(2).to_broadcast([128, 4, tw]),
                                   ALU.mult), r=psk(4 + b) + [f"colv{l}"], w=["ypT"])
            hk = gkeys("hT", g0, n)
            ok = [(k_[0], k_[1], hh, th_) for k_ in gkeys("oT", g0, n) for hh in range(2) for th_ in range(2)]
            for cc in range(8):
                wgp, kgp = wview(S_GP, 1024, cc * 128, 128)
                for k in range(8):
                    P.op("pe", mm(PS[6][:, :n], wgp[:, k, :], hT[:, k, g0:g0 + n], start=(k == 0), stop=(k == 7)),
                         r=[kgp] + hk, w=psk(6))
                P.op("act", actf(sgp[cc % 2][:, :n], PS[6][:, :n], AF.Silu), r=psk(6), w=[f"sgp{cc % 2}"])
                P.op("dve", tt(gyT[:, cc, :n], sgp[cc % 2][:, :n], ypT[:, cc, :n], ALU.mult),
                     r=[f"sgp{cc % 2}", "ypT"], w=["gyT"])
            for cc in range(8):
                wgb, kgb = wview(S_GB, 1024, cc * 128, 128)
                wpb, kpb = wview(S_PB, 1024, cc * 128, 128)
                bb = 6 + cc % 2
                bg = cc % 2
                for k in range(8):
                    P.op("pe", mm(PS[bg][:, :n], wgb[:, k, :], hT[:, k, g0:g0 + n], start=(k == 0), stop=(k == 7)),
                         r=[kgb] + hk, w=psk(bg))
                for k in range(8):
                    P.op("pe", mm(PS[bb][:, :n], wpb[:, k, :], gyT[:, k, :n], start=(k == 0), stop=(k == 7)),
                         r=[kpb, "gyT"], w=psk(bb))
                P.op("act", actf(sgb[cc % 2][:, :n], PS[bg][:, :n], AF.Sigmoid), r=psk(bg), w=[f"sgb{cc % 2}"])
                P.op("dve", tt(mb[:, :n], sgb[cc % 2][:, :n], PS[bb][:, :n], ALU.mult), r=[f"sgb{cc % 2}"] + psk(bb),
                     w=["mb"])
                P.op("dve", tt(oT[:, cc, g0:g0 + n], oT[:, cc, g0:g0 + n], mb[:, :n], ALU.add), r=["mb"] + ok, w=ok)

    def phaseC2(l, last):
        wreset()
        S_O, S_G = 8, 12
        load_w_wide(S_O, w_out[l], 0, 1024)
        load_w_wide(S_G, w_ple_gate[l], 0, 1024)
        if last:
            Wpp = wslot(7, 1, 256).rearrange("p k c -> p (k c)").rearrange("p (k c) -> p k c", k=2)
            wppk = ark(7)
        else:
            Wpp = walloc([128, 2, D], BF16, "Wpp")[:]
            wppk = ["Wpp"]
        P.dma("pool", Wpp, w_ple_proj[l].rearrange("(k p) c -> p k c", p=128), w=wppk)
        fn = None
        if last:
            fn = walloc([128, D], F32, "fn")
            P.dma("sp", fn[:], final_norm.partition_broadcast(128), w=["fn"])
        xsrc = x_all if l == 0 else xs
        recs = [Recorder() for _ in range(2)]
        for th in range(2):
            c2_thread(recs[th], l, last, th, (S_O, S_G), Wpp, wppk, fn, xsrc)
        merge_threads(P, recs)
        if not last:
            wl = w_in[l + 1]
            load_w(0, wl, C_Q, 512)
            load_w(2, wl, C_K, 512)
            load_w(4, wl, C_V, 512)
            load_w(6, wl, C_Z, 512)

    def c2_thread(P, l, last, th, slots, Wpp, wppk, fn, xsrc):
        S_O, S_G = slots
        T_ = f"_c{th}"
        b0, b1, b2, b3 = [4 * th + i for i in range(4)]

        def K_(n):
            return n + T_

        xt = walloc([128, D], F32, "xt")
        junk = walloc([128, D], BF16, "junk")
        ssq = walloc([128, 4], F32, "ssq")
        ssqp = walloc([128, 4], F32, "ssqp")
        hb = walloc([128, D], BF16, "hb")
        x1 = walloc([128, D], F32, "x1")
        hp = walloc([128, D], BF16, "hp")
        hpT = walloc([128, 8, 128], BF16, "hpT")
        sgt = walloc([128, D], F32, "sgt")
        pt = walloc([128, 256], F32, "pt")
        pbf = walloc([128, 256], BF16, "pbf")
        pT = walloc([128, 2, 128], BF16, "pT")
        xo = walloc([128, D], F32, "x2")
        for c in range(th, NT, 2):
            tw, to = TW[c], TO[c]
            mkeys = [("oT", c, hh_, th_) for hh_ in range(2) for th_ in range(2)]
            P.dma("sp", xt[:tw, :], xsrc[to:to + tw, :], r=[("xs", c)], w=[K_("xt")])
            P.dma("sp", pt[:tw, :], p_all[l, to:to + tw, :], w=[K_("pt")])
            for half in range(2):
                wv_, wk_ = wwide(S_O, half)
                for k in range(8):
                    P.op("pe", mm(PS[b0 + half][:tw, :], oT[:, k, to:to + tw], wv_[:, k, :],
                                  start=(k == 0), stop=(k == 7)), r=mkeys + wk_, w=psk(b0 + half))
            for half in range(2):
                P.op("dve", tt(x1[:tw, half * 512:(half + 1) * 512], xt[:tw, half * 512:(half + 1) * 512],
                               PS[b0 + half][:tw, :], ALU.add), r=[K_("xt")] + psk(b0 + half), w=[K_("x1")])
            P.op("act", actf(junk[:tw, :], x1[:tw, :], AF.Square, accum=ssqp[:tw, 0:1]), r=[K_("x1")],
                 w=[K_("junk"), K_("ssqp")])
            P.op("act", actf(ssqp[:tw, 1:2], ssqp[:tw, 0:1], AF.Sqrt, scale=1.0 / D, bias=epsc[:tw, 0:1]),
                 r=[K_("ssqp"), "epsc"], w=[K_("ssqp1")])
            P.op("dve", rcp(ssqp[:tw, 2:3], ssqp[:tw, 1:2]), r=[K_("ssqp1")], w=[K_("ssqp2")])
            P.op("dve", ts(hp[:tw, :], x1[:tw, :], ssqp[:tw, 2:3], ALU.mult), r=[K_("x1"), K_("ssqp2")], w=[K_("hp")])
            pst = PS[b2][:].bitcast(BF16).rearrange("p (k t) -> p k t", k=8)
            for k in range(8):
                P.op("pe", trp(pst[:, k, :tw], hp[:tw, k * 128:(k + 1) * 128], identb[:tw, :tw]), r=[K_("hp"), "CB"],
                     w=psk(b2))
            P.op("dve", tt(hpT[:, :, :tw], pst[:, :, :tw],
                           colv[l][:, CV_NP:CV_NP + 8].unsqueeze(2).to_broadcast([128, 8, tw]), ALU.mult),
                 r=psk(b2) + [f"colv{l}"], w=[K_("hpT")])
            for half in range(2):
                wv_, wk_ = wwide(S_G, half)
                for k in range(8):
                    P.op("pe", mm(PS[b0 + half][:tw, :], hpT[:, k, :tw], wv_[:, k, :],
                                  start=(k == 0), stop=(k == 7)), r=[K_("hpT")] + wk_, w=psk(b0 + half))
            for half in range(2):
                P.op("act", actf(sgt[:tw, half * 512:(half + 1) * 512], PS[b0 + half][:tw, :], AF.Sigmoid),
                     r=psk(b0 + half), w=[K_("sgt")])
            P.op("dve", cp(pbf[:tw, :], pt[:tw, :]), r=[K_("pt")], w=[K_("pbf")])
            pstp = PS[b3][:].bitcast(BF16)[:, 0:256].rearrange("p (k t) -> p k t", k=2)
            for k in range(2):
                P.op("pe", trp(pstp[:, k, :tw], pbf[:tw, k * 128:(k + 1) * 128], identb[:tw, :tw]),
                     r=[K_("pbf"), "CB"], w=psk(b3))
            P.op("act", acp(out=pT[:, :, :tw], in_=pstp[:, :, :tw]), r=psk(b3), w=[K_("pT")])
            for half in range(2):
                for k in range(2):
                    P.op("pe", mm(PS[b2 + half][:tw, :], pT[:, k, :tw], Wpp[:, k, half * 512:(half + 1) * 512],
                                  start=(k == 0), stop=(k == 1)), r=[K_("pT")] + wppk, w=psk(b2 + half))
            for half in range(2):
                sl = slice(half * 512, (half + 1) * 512)
                P.op("dve", tt(sgt[:tw, sl], sgt[:tw, sl], PS[b2 + half][:tw, :], ALU.mult),
                     r=[K_("sgt")] + psk(b2 + half), w=[K_("sgt")])
            P.op("dve", tt(xo[:tw, :], x1[:tw, :], sgt[:tw, :], ALU.add), r=[K_("x1"), K_("sgt")], w=[K_("x2")])
            if not last:
                P.dma("sp", xs[to:to + tw, :], xo[:tw, :], r=[K_("x2")], w=[("xs", c)])
                make_h(l + 1, c, xo, K_("x2"), (junk, ssq, hb), P=P, bank=b2, sfx=T_)
            else:
                P.op("act", actf(junk[:tw, :], xo[:tw, :], AF.Square, accum=ssq[:tw, 0:1]), r=[K_("x2")],
                     w=[K_("junk"), K_("ssq")])
                P.op("act", actf(ssq[:tw, 1:2], ssq[:tw, 0:1], AF.Sqrt, scale=1.0 / D, bias=epsc[:tw, 0:1]),
                     r=[K_("ssq"), "epsc"], w=[K_("ssq1")])
                P.op("dve", rcp(ssq[:tw, 2:3], ssq[:tw, 1:2]), r=[K_("ssq1")], w=[K_("ssq2")])
                P.op("dve", stt(x1[:tw, :], xo[:tw, :], ssq[:tw, 2:3], fn[:tw, :], ALU.mult, ALU.mult),
                     r=[K_("x2"), K_("ssq2"), "fn"], w=[K_("x1")])
                P.dma("sp", y_all[to:to + tw, :], x1[:tw, :], r=[K_("x1")], is_output=True)

    P.barrier()
    load_w(0, w_in[0], C_Q, 512)
    load_w(2, w_in[0], C_K, 512)
    load_w(4, w_in[0], C_V, 512)
    load_w(6, w_in[0], C_Z, 512)
    phase0()
    for l in range(depth):
        for hh in range(2):
            P.barrier()
            phaseA(l, hh)
            if stop_after == ("A", l, hh):
                break
        else:
            P.barrier()
            phaseC1(l)
            if stop_after == ("C1", l):
                break
            P.barrier()
            phaseB(l)
            if stop_after == ("B", l):
                break
            P.barrier()
            phaseC2(l, last=(l == depth - 1))
            continue
        break
    if dbg:
        dbg_h = dout("dbg_hT", [128, 8 * TOK])
        dbg_o = dout("dbg_oT", [128, 8 * TOK])
        P.barrier()
        wreset()
        stg = walloc([128, 2048], F32, "stg")
        for name, src, dst in (("h", hT, dbg_h), ("o", oT, dbg_o)):
            flat = src[:].rearrange("p k t -> p (k t)")
            for i in range(0, 8 * TOK, 2048):
                n = min(2048, 8 * TOK - i)
                P.op("dve", cp(stg[:, :n], flat[:, i:i + n]), w=["stg"])
                P.dma("sp", dst[:, i:i + n], stg[:, :n], r=["stg"], is_output=True)
    P.emit()
    return nc


_CACHE = {}


def make_in_maps(inputs):
    f = lambda a: np.ascontiguousarray(np.asarray(a, dtype=np.float32))
    xp, xsm = f(inputs["x_prompt"]), f(inputs["x_sample"])
    pp, psm = f(inputs["p_prompt"]), f(inputs["p_sample"])
    sc, sd, sp = f(inputs["state_conv"]), f(inputs["state_delta"]), f(inputs["state_pool"])
    cf, cb = make_consts()
    shared = {k: f(inputs[k]) for k in ("norm_mix", "w_in", "conv_w", "a_log", "dt_bias", "gdn_norm", "w_proj_a",
                                        "pool_w", "pool_scale", "w_proj_b", "w_out", "norm_ple", "w_ple_gate",
                                        "w_ple_proj", "final_norm")}
    shared["cf"] = cf
    shared["cb"] = cb
    maps = []
    for c in range(8):
        sl = slice(16 * c, 16 * c + 16)
        m = dict(shared)
        m["x_all"] = np.ascontiguousarray(np.concatenate([xp[c], xsm[sl].transpose(1, 0, 2).reshape(ST, D)], 0))
        m["p_all"] = np.ascontiguousarray(np.concatenate(
            [pp[:, c], psm[:, sl].transpose(0, 2, 1, 3).reshape(DEPTH, ST, 256)], 1))
        m["sconv"] = np.ascontiguousarray(sc[:, sl].transpose(0, 2, 1, 3).reshape(DEPTH, 48, 3072))
        m["sdelta"] = np.ascontiguousarray(sd[:, sl])
        m["spool"] = np.ascontiguousarray(sp[:, sl].transpose(0, 2, 1, 3).reshape(DEPTH, 240, D))
        maps.append(m)
    return maps


def kernel(**inputs):
    if "nc" not in _CACHE:
        _CACHE["nc"] = build()
    nc = _CACHE["nc"]
    maps = make_in_maps(inputs)
    res = run_bass_kernel_spmd(nc, maps, core_ids=list(range(8)))
    R = res.results
    y_prompt = np.stack([R[c]["y_all"][:PT] for c in range(8)], 0)
    y_sample = np.concatenate([R[c]["y_all"][PT:].reshape(4, 16, D).transpose(1, 0, 2) for c in range(8)], 0)
    conv_p = np.stack([R[c]["conv_p"] for c in range(8)], 1)
    delta_p = np.stack([R[c]["delta_p"] for c in range(8)], 1)
    pool_p = np.stack([R[c]["pool_p"] for c in range(8)], 1)
    conv_s = np.concatenate([R[c]["conv_s"].transpose(0, 2, 1, 3) for c in range(8)], 1)
    delta_s = np.concatenate([R[c]["delta_s"] for c in range(8)], 1)
    pool_s = np.concatenate([R[c]["pool_s"].transpose(0, 2, 1, 3) for c in range(8)], 1)
    out = (y_prompt, y_sample, conv_p, delta_p, pool_p, conv_s, delta_s, pool_s)
    return tuple(np.ascontiguousarray(o, dtype=np.float32) for o in out)
```

```python
import numpy as np
import concourse.bass as bass
import concourse.mybir as mybir
from concourse.bass_utils import run_bass_kernel_spmd

F32 = mybir.dt.float32
BF16 = mybir.dt.bfloat16
AF = mybir.ActivationFunctionType
ALU = mybir.AluOpType
AX = mybir.AxisListType

D = 1024
DEPTH = 2
NH = 8
PT = 2048
ST = 64
TOK = PT + ST
NT = 17
NSEQ = 16
INW = 8208
POOL_WINDOWS = (2, 4, 8, 16)
EPS = 1e-6
C_Q, C_K, C_V, C_Z, C_A, C_B, C_U, C_GP, C_GA, C_GB = 0, 1024, 2048, 3072, 4096, 4104, 4112, 5136, 6160, 7184

ENGS = ("pe", "act", "dve", "pool", "sp")


class Rec:
    __slots__ = ("eng", "fn", "deps", "signal", "ticket", "dma", "dsem", "dval")

    def __init__(self, eng, fn, dma=False):
        self.eng = eng
        self.fn = fn
        self.deps = []
        self.signal = False
        self.ticket = None
        self.dma = dma
        self.dsem = None
        self.dval = None


class Prog:
    def __init__(self, nc, n_dma_sems=12):
        self.nc = nc
        self.q = {e: [] for e in ENGS}
        self.last_w = {}
        self.readers = {}
        self.n_dma_sems = n_dma_sems
        self.dma_count = {e: 0 for e in ENGS}
        self.dma_hist = {e: [] for e in ENGS}
        self.out_dmas = []

    def _add_dep(self, x, d, kind):
        if d is None or d is x:
            return
        if not d.dma and d.eng == x.eng and not x.dma:
            if x.eng == "pe":
                return
        if d not in x.deps:
            x.deps.append(d)
            if not d.dma:
                d.signal = True

    def op(self, eng, fn, r=(), w=(), dma=False):
        x = Rec(eng, fn, dma)
        for k in r:
            self._add_dep(x, self.last_w.get(k), "RAW")
            if isinstance(k, tuple) and k[0] == "ps":
                for rd in self.readers.get(k, ()):
                    if rd.eng != eng:
                        self._add_dep(x, rd, "RAR")
        for k in w:
            self._add_dep(x, self.last_w.get(k), "WAW")
            for rd in self.readers.get(k, ()):
                self._add_dep(x, rd, "WAR")
        for k in r:
            self.readers.setdefault(k, []).append(x)
        for k in w:
            self.last_w[k] = x
            self.readers[k] = []
        if dma:
            n = self.dma_count[eng]
            self.dma_count[eng] += 1
            x.dsem = n % self.n_dma_sems
            x.dval = 16 * (n // self.n_dma_sems + 1)
            hist = self.dma_hist[eng]
            if n >= self.n_dma_sems:
                x.deps.append(hist[n - self.n_dma_sems])
            hist.append(x)
        self.q[eng].append(x)
        return x

    def dma(self, eng, out, in_, r=(), w=(), is_output=False, **kw):
        x = self.op(eng, lambda e: e.dma_start(out=out, in_=in_, **kw), r=r, w=w, dma=True)
        if is_output:
            self.out_dmas.append(x)
        return x

    def barrier(self):
        lasts = []
        for e in ENGS:
            for x in reversed(self.q[e]):
                if not x.dma and x.fn is not None:
                    lasts.append(x)
                    break
            lasts.extend(self.dma_hist[e][-self.n_dma_sems:])
        for e in ENGS:
            b = Rec(e, None)
            for d in lasts:
                if d.dma or d.eng != e:
                    b.deps.append(d)
                    if not d.dma:
                        d.signal = True
            self.q[e].append(b)
        self.last_w = {}
        self.readers = {}

    def emit(self):
        nc = self.nc
        from contextlib import ExitStack
        with ExitStack() as es:
            esem = {e: es.enter_context(nc.semaphore("sem_" + e)) for e in ENGS}
            dsem = {e: [es.enter_context(nc.semaphore(f"dsem_{e}_{i}")) for i in range(self.n_dma_sems)]
                    for e in ENGS if self.dma_count[e] > 0}
            for e in ENGS:
                c = 0
                for x in self.q[e]:
                    if x.signal and not x.dma:
                        c += 1
                        x.ticket = c
            final = Rec("sp", None)
            final.deps = list(self.out_dmas)
            block = es.enter_context(nc.Block())
            handles = {"pe": block.tensor, "act": block.scalar, "dve": block.vector, "pool": block.gpsimd,
                       "sp": block.sync}

            def run_engine(ename):
                def body(e):
                    seen = {}

                    def do_waits(x):
                        for d in x.deps:
                            if d.dma:
                                key = ("d", d.eng, d.dsem)
                                sem, val = dsem[d.eng][d.dsem], d.dval
                            else:
                                key = ("e", d.eng)
                                sem, val = esem[d.eng], d.ticket
                            if seen.get(key, 0) >= val:
                                continue
                            seen[key] = val
                            e.wait_ge(sem, val)

                    for x in self.q[ename]:
                        do_waits(x)
                        if x.fn is None:
                            continue
                        ins = x.fn(e)
                        if x.dma:
                            ins.then_inc(dsem[ename][x.dsem], 16)
                        elif x.signal:
                            ins.then_inc(esem[ename], 1)
                    if ename == "sp":
                        do_waits(final)
                return body

            for ename in ENGS:
                handles[ename](run_engine(ename))
        return nc


class Recorder:
    def __init__(self):
        self.ops = []

    def op(self, eng, fn, r=(), w=(), dma=False):
        self.ops.append(("op", eng, (fn,), dict(r=r, w=w, dma=dma)))

    def dma(self, eng, out, in_, r=(), w=(), is_output=False, **kw):
        self.ops.append(("dma", eng, (out, in_), dict(r=r, w=w, is_output=is_output, **kw)))


def merge_threads(P, recs, max_run=64):
    idx = [0] * len(recs)
    cur = 0
    n = len(recs)
    while any(idx[i] < len(recs[i].ops) for i in range(n)):
        if idx[cur] >= len(recs[cur].ops):
            cur = (cur + 1) % n
            continue
        ops = recs[cur].ops
        eng = ops[idx[cur]][1]
        cnt = 0
        while idx[cur] < len(ops) and ops[idx[cur]][1] == eng and cnt < max_run:
            kind, e_, a, kw = ops[idx[cur]]
            if kind == "op":
                P.op(e_, *a, **kw)
            else:
                P.dma(e_, *a, **kw)
            idx[cur] += 1
            cnt += 1
        cur = (cur + 1) % n


def mm(out, lhsT, rhs, start=True, stop=True):
    return lambda e: e.matmul(out, lhsT=lhsT, rhs=rhs, start=start, stop=stop)


def trp(out, in_, ident):
    return lambda e: e.transpose(out, in_, ident)


def actf(out, in_, func, scale=None, bias=None, accum=None):
    kw = {}
    if scale is not None:
        kw["scale"] = scale
    if bias is not None:
        kw["bias"] = bias
    if accum is not None:
        kw["accum_out"] = accum
    return lambda e: e.activation(out=out, in_=in_, func=func, **kw)


def tt(out, a, b, op):
    return lambda e: e.tensor_tensor(out=out, in0=a, in1=b, op=op)


def ts(out, a, s1, op0, s2=None, op1=None):
    if op1 is None:
        return lambda e: e.tensor_scalar(out=out, in0=a, scalar1=s1, scalar2=None, op0=op0)
    return lambda e: e.tensor_scalar(out=out, in0=a, scalar1=s1, scalar2=s2, op0=op0, op1=op1)


def stt(out, in0, scalar, in1, op0, op1):
    return lambda e: e.scalar_tensor_tensor(out=out, in0=in0, scalar=scalar, in1=in1, op0=op0, op1=op1)


def cp(out, in_):
    return lambda e: e.tensor_copy(out=out, in_=in_)


def acp(out, in_):
    return lambda e: e.copy(out=out, in_=in_)


def rcp(out, in_):
    return lambda e: e.reciprocal(out=out, in_=in_)


def red(out, in_):
    return lambda e: e.tensor_reduce(out=out, in_=in_, axis=AX.X, op=ALU.add)


def mset(out, v):
    return lambda e: e.memset(out, v)


CF_ID, CF_NM2, CF_MUD, CF_UTRI, CF_ONES = 0, 128, 384, 512, 640
CF_NM2S, CF_MUDS, CF_UTRIS, CF_ONESS, CF_SEGT = 768, 896, 960, 1024, 1088
NCF = 1104
CB_ID, CB_SEG, CB_BC, CB_BP, CB_BF, CB_SH0, CB_SH1, CB_SN = 0, 128, 1152, 1664, 2176, 2688, 2944, 3200
NCB = 3456


def make_consts():
    cf = np.zeros((128, NCF), np.float32)
    cb = np.zeros((128, NCB), np.float32)
    p = np.arange(128)[:, None]
    f = np.arange(128)[None, :]
    cf[:, CF_ID:CF_ID + 128] = np.eye(128)
    cf[:, CF_NM2:CF_NM2 + 128] = -1.0 * (f > p)
    cf[:, CF_NM2 + 128:CF_NM2 + 256] = -1.0 * (p > f)
    cf[:, CF_MUD:CF_MUD + 128] = (f >= p)
    cf[:, CF_UTRI:CF_UTRI + 128] = (p <= f)
    cf[:, CF_ONES:CF_ONES + 128] = 1.0
    ps = np.arange(64)[:, None]
    fs = np.arange(64)[None, :]
    same = (ps % 16) == (fs % 16)
    tp, tf = ps // 16, fs // 16
    cf[:64, CF_NM2S:CF_NM2S + 64] = -1.0 * (same & (tf > tp))
    cf[:64, CF_NM2S + 64:CF_NM2S + 128] = -1.0 * (same & (tp > tf))
    cf[:64, CF_MUDS:CF_MUDS + 64] = (same & (tf >= tp))
    cf[:64, CF_UTRIS:CF_UTRIS + 64] = (same & (tp <= tf))
    cf[:64, CF_ONESS:CF_ONESS + 64] = same
    cf[:64, CF_SEGT:CF_SEGT + 16] = (np.arange(64)[:, None] % 16) == np.arange(16)[None, :]
    cb[:, CB_ID:CB_ID + 128] = np.eye(128)
    seg = ((np.arange(64)[None, :] % 16) == np.arange(16)[:, None]).astype(np.float32)
    cb[:, CB_SEG:CB_SEG + 1024] = seg.reshape(1, 1024)
    for g, W in enumerate(POOL_WINDOWS):
        tq = np.arange(128)[:, None]
        t = np.arange(128)[None, :]
        cur = ((tq > t - W) & (tq <= t)) / W - (tq == t)
        prev = ((tq - 128 > t - W)) / W
        cnt = np.minimum(t + 1, W)
        first = ((tq >= np.maximum(0, t - W + 1)) & (tq <= t)) / cnt - (tq == t)
        cb[:, CB_BC + g * 128:CB_BC + (g + 1) * 128] = cur
        cb[:, CB_BP + g * 128:CB_BP + (g + 1) * 128] = prev
        cb[:, CB_BF + g * 128:CB_BF + (g + 1) * 128] = first
        r = np.arange(240)[:, None]
        j, s1 = r // 16, r % 16
        c = np.arange(64)[None, :]
        tau, s2 = c // 16, c % 16
        sh = ((s1 == s2) & (j >= 16 + tau - W)) / W
        cb[:, CB_SH0 + g * 64:CB_SH0 + (g + 1) * 64] = sh[:128]
        cb[:112, CB_SH1 + g * 64:CB_SH1 + (g + 1) * 64] = sh[128:]
        r = np.arange(64)[:, None]
        tau1, s1 = r // 16, r % 16
        sn = (s1 == s2) * (((tau1 > tau - W) & (tau1 <= tau)) / W - (tau1 == tau))
        cb[:64, CB_SN + g * 64:CB_SN + (g + 1) * 64] = sn
    return cf, cb


A_STAGGER = 0


def build(depth=DEPTH, stop_after=None, dbg=False, dbg_tile=0):
    nc = bass.Bass("TRN2", target_bir_lowering=False)
    P = Prog(nc)

    def din(name, shape):
        return nc.dram_tensor(name, list(shape), F32, kind="ExternalInput").ap()

    def dout(name, shape):
        return nc.dram_tensor(name, list(shape), F32, kind="ExternalOutput").ap()

    x_all = din("x_all", [TOK, D])
    p_all = din("p_all", [DEPTH, TOK, 256])
    sconv = din("sconv", [DEPTH, 48, 3072])
    sdelta = din("sdelta", [DEPTH, NSEQ, NH, 128, 128])
    spool = din("spool", [DEPTH, 240, D])
    norm_mix = din("norm_mix", [DEPTH, D])
    w_in = din("w_in", [DEPTH, D, INW])
    conv_w = din("conv_w", [DEPTH, 4, 3072])
    a_log = din("a_log", [DEPTH, NH])
    dt_bias = din("dt_bias", [DEPTH, NH])
    gdn_norm = din("gdn_norm", [DEPTH, 128])
    w_proj_a = din("w_proj_a", [DEPTH, D, D])
    pool_w = din("pool_w", [DEPTH, 4, 256, 256])
    pool_scale = din("pool_scale", [DEPTH, D])
    w_proj_b = din("w_proj_b", [DEPTH, D, D])
    w_out = din("w_out", [DEPTH, D, D])
    norm_ple = din("norm_ple", [DEPTH, D])
    w_ple_gate = din("w_ple_gate", [DEPTH, D, D])
    w_ple_proj = din("w_ple_proj", [DEPTH, 256, D])
    final_norm = din("final_norm", [D])
    cf_d = din("cf", [128, NCF])
    cb_d = din("cb", [128, NCB])

    y_all = dout("y_all", [TOK, D])
    conv_p = dout("conv_p", [DEPTH, 3, 3072])
    delta_p = dout("delta_p", [DEPTH, NH, 128, 128])
    pool_p = dout("pool_p", [DEPTH, 15, D])
    conv_s = dout("conv_s", [DEPTH, 3, NSEQ, 3072])
    delta_s = dout("delta_s", [DEPTH, NSEQ, NH, 128, 128])
    pool_s = dout("pool_s", [DEPTH, 15, NSEQ, D])
    xs = nc.dram_tensor("xscratch", [TOK, D], F32, kind="Internal").ap()
    dbg_seen = set()

    def DBG(name, ap, keys):
        if not dbg or name in dbg_seen:
            return
        dbg_seen.add(name)
        shp = list(ap.shape)
        d = nc.dram_tensor("dbg_" + name, shp, F32, kind="ExternalOutput").ap()
        P.dma("sp" if ap.dtype == F32 else "pool", d, ap, r=keys, is_output=True)

    total = (nc.sbuf_bytes_remaining // 64) * 64 - 256
    base = nc.bump_sbuf(total)[0]
    cursor = [0]

    def alloc(name, shape, dt, at=None):
        nbytes = int(np.prod(shape[1:])) * (4 if dt == F32 else 2)
        nbytes = (nbytes + 63) // 64 * 64
        if at is None:
            off = cursor[0]
            cursor[0] += nbytes
            assert cursor[0] <= total, (name, cursor[0], total)
        else:
            off = at
        return nc.alloc_sbuf_tensor_at(name, list(shape), dt, offset=base + off)

    hT = alloc("hT", [128, 8, TOK], BF16)
    oT = alloc("oT", [128, 8, TOK], BF16)
    AR_OFF = cursor[0]
    NSLOT = 16
    AR = alloc("AR", [128, NSLOT * 2048], BF16)
    CF = alloc("CF", [128, NCF], F32)
    CB = alloc("CB", [128, NCB], BF16)
    epsc = alloc("epsc", [128, 1], F32)
    onec = alloc("onec", [128, 1], F32)
    colv = [alloc(f"colv{l}", [128, 128], F32) for l in range(DEPTH)]
    dtb = [alloc(f"dtb{l}", [128, 8], F32) for l in range(DEPTH)]
    nexpA = [alloc(f"nexpA{l}", [128, 8], F32) for l in range(DEPTH)]
    W0 = cursor[0]
    uid = [0]

    def walloc(shape, dt, name=None):
        uid[0] += 1
        return alloc(f"{name or 'w'}_{uid[0]}", shape, dt)

    def wreset():
        cursor[0] = W0

    PS = [nc.alloc_psum_tensor(f"ps{b}", [128, 512], F32) for b in range(8)]

    def psk(*banks):
        return [("ps", b) for b in banks]

    def ark(*slots):
        return [("ar", s) for s in slots]

    TW = [128] * 16 + [64]
    TO = [128 * i for i in range(17)]
    identb = CB[:, CB_ID:CB_ID + 128]
    identf = CF[:, CF_ID:CF_ID + 128]

    def wslot(s0, n, ncols):
        return AR[:, s0 * 2048:(s0 + n) * 2048].rearrange("p (k c) -> p k c", k=8)

    def load_w(s0, wmat, col0, ncols):
        n = ncols // 256
        src = wmat.rearrange("(k p) c -> p k c", p=128)
        for i in range(n):
            P.dma("pool", wslot(s0 + i, 1, 256), src[:, :, col0 + i * 256:col0 + (i + 1) * 256], w=ark(s0 + i))

    def load_w_wide(s0, wmat, col0, ncols):
        src = wmat.rearrange("(k p) c -> p k c", p=128)
        for i in range(ncols // 512):
            P.dma("pool", wslot(s0 + 2 * i, 2, 512), src[:, :, col0 + i * 512:col0 + (i + 1) * 512],
                  w=ark(s0 + 2 * i, s0 + 2 * i + 1))

    def wwide(s0, i):
        return wslot(s0 + 2 * i, 2, 512), ark(s0 + 2 * i, s0 + 2 * i + 1)

    def wview(s0, ncols, c0, cw):
        s = s0 + c0 // 256
        cc = c0 % 256
        assert cc + cw <= 256
        return wslot(s, 1, 256)[:, :, cc:cc + cw], ("ar", s)

    P.dma("sp", CF[:], cf_d, w=["CF"])
    P.dma("pool", CB[:], cb_d, w=["CB"])
    P.op("dve", mset(epsc[:], EPS), w=["epsc"])
    P.op("dve", mset(onec[:], 1.0), w=["onec"])
    VR = walloc([128, 128], F32, "VR")
    for l in range(depth):
        P.op("dve", mset(VR[:], 0.0), w=["VR"])
        P.dma("sp", VR[0:96, :], conv_w[l].rearrange("t (j p) -> (t j) p", p=128), w=["VR"])
        P.dma("sp", VR[96:104, :], norm_mix[l].rearrange("(k p) -> k p", p=128), w=["VR"])
        P.dma("sp", VR[104:112, :], norm_ple[l].rearrange("(k p) -> k p", p=128), w=["VR"])
        P.dma("sp", VR[112:120, :], pool_scale[l].rearrange("(k p) -> k p", p=128), w=["VR"])
        P.dma("sp", VR[120:121, :], gdn_norm[l].rearrange("(k p) -> k p", p=128), w=["VR"])
        P.op("pe", mm(PS[0][:, 0:128], VR[:, :], identf), r=["VR", "CF"], w=psk(0))
        P.op("dve", cp(colv[l][:], PS[0][:, 0:128]), r=psk(0), w=[f"colv{l}"])
        P.dma("sp", dtb[l][:], dt_bias[l].partition_broadcast(128), w=[f"dtb{l}"])
        P.dma("sp", nexpA[l][:], a_log[l].partition_broadcast(128), w=[f"nexpA{l}"])
        P.op("act", actf(nexpA[l][:], nexpA[l][:], AF.Exp), r=[f"nexpA{l}"], w=[f"nexpA{l}"])
        P.op("dve", ts(nexpA[l][:], nexpA[l][:], -1.0, ALU.mult), r=[f"nexpA{l}"], w=[f"nexpA{l}"])
    CV_CW, CV_NM, CV_NP, CV_PS, CV_GN = 0, 96, 104, 112, 120

    def make_h(l, c, xt, xkey, bufs, P=P, bank=2, sfx=""):
        tw, to = TW[c], TO[c]
        junk, ssq, hb = bufs
        P.op("act", actf(junk[:tw, :], xt[:tw, :], AF.Square, accum=ssq[:tw, 0:1]), r=[xkey],
             w=["junk" + sfx, "ssq" + sfx])
        P.op("act", actf(ssq[:tw, 1:2], ssq[:tw, 0:1], AF.Sqrt, scale=1.0 / D, bias=epsc[:tw, 0:1]),
             r=["ssq" + sfx, "epsc"], w=["ssq1" + sfx])
        P.op("dve", rcp(ssq[:tw, 2:3], ssq[:tw, 1:2]), r=["ssq1" + sfx], w=["ssq2" + sfx])
        P.op("dve", ts(hb[:tw, :], xt[:tw, :], ssq[:tw, 2:3], ALU.mult), r=[xkey, "ssq2" + sfx], w=["hb" + sfx])
        pst = PS[bank][:].bitcast(BF16).rearrange("p (k t) -> p k t", k=8)
        for k in range(8):
            P.op("pe", trp(pst[:, k, :tw], hb[:tw, k * 128:(k + 1) * 128], identb[:tw, :tw]), r=["hb" + sfx, "CB"],
                 w=psk(bank))
        P.op("dve", tt(hT[:, :, to:to + tw], pst[:, :, :tw],
                       colv[l][:, CV_NM:CV_NM + 8].unsqueeze(2).to_broadcast([128, 8, tw]), ALU.mult),
             r=psk(bank) + [f"colv{l}"], w=[("hT", c)])

    def phase0():
        wreset()
        xts = [walloc([128, D], F32, "xt") for _ in range(2)]
        junk = walloc([128, D], BF16, "junk")
        ssq = walloc([128, 4], F32, "ssq")
        hb = walloc([128, D], BF16, "hb")
        for c in range(NT):
            xt = xts[c % 2]
            P.dma("sp", xt[:TW[c], :], x_all[TO[c]:TO[c] + TW[c], :], w=[f"xt{c % 2}"])
            make_h(0, c, xt, f"xt{c % 2}", (junk, ssq, hb))

    def phaseA(l, hh):
        wreset()
        wl = w_in[l]
        WB = 0 if hh == 0 else 8
        A2B = 8 if hh == 0 else 0
        S_Q, S_K, S_V, S_Z = WB, WB + 2, WB + 4, WB + 6
        if hh == 0:
            pass
        else:
            load_w(15, wl, C_Z + 512 + 256, 256)
        Wab = walloc([128, 8, 16], BF16, "Wab")
        P.dma("pool", Wab[:], wl.rearrange("(k p) c -> p k c", p=128)[:, :, C_A:C_A + 16], w=["Wab"])
        Dg = walloc([128, 12, 4, 128], BF16, "Dg")
        for qkv in range(3):
            cwv = colv[l][:, CV_CW:CV_CW + 96].rearrange("p (t j) -> p j t", t=4)[:, qkv * 8 + 4 * hh:qkv * 8 + 4 * hh + 4, :]
            P.op("dve", tt(Dg[:, qkv * 4:(qkv + 1) * 4, :, :],
                           identf.unsqueeze(1).unsqueeze(1).to_broadcast([128, 4, 4, 128]),
                           cwv.unsqueeze(3).to_broadcast([128, 4, 4, 128]), ALU.mult),
                 r=["CF", f"colv{l}"], w=["Dg"])
        a2 = [AR_OFF + A2B * 4096, AR_OFF + (A2B + 7) * 4096]

        def alloc2(name, shape, dt, always=False):
            nbytes = (int(np.prod(shape[1:])) * (4 if dt == F32 else 2) + 63) // 64 * 64
            i = 1 if always else 0
            off = a2[i]
            a2[i] += nbytes
            lim = AR_OFF + (A2B + 8) * 4096 if always else AR_OFF + (A2B + 7) * 4096
            assert a2[i] <= lim, (name, a2[i], lim)
            uid[0] += 1
            return alloc(f"{name}_{uid[0]}", shape, dt, at=off)

        H4 = 4 * hh
        gall = walloc([128, NT, 4], F32, "gall")
        ball = walloc([128, NT, 4], F32, "ball")
        sc3all = walloc([128, NT, 12], F32, "sc3all")
        eall = walloc([128, NT, 12], F32, "eall")
        negg = walloc([128, NT, 4], F32, "negg")
        psab = PS[0][:, 0:NT * 16].rearrange("p (c j) -> p c j", c=NT)
        for c in range(NT):
            for k in range(8):
                P.op("pe", mm(psab[:TW[c], c, :], hT[:, k, TO[c]:TO[c] + TW[c]], Wab[:, k, :], start=(k == 0),
                              stop=(k == 7)), r=["Wab", ("hT", c)], w=psk(0))
        psg = PS[1][:, 0:NT * 8].rearrange("p (c j) -> p c j", c=NT)
        REG = ((128, slice(0, 16), 16), (64, slice(16, 17), 1))
        for (np_, ts_, nt_) in REG:
            P.op("dve", tt(gall[:np_, ts_, :], psab[:np_, ts_, H4:H4 + 4],
                           dtb[l][:np_, H4:H4 + 4].unsqueeze(1).to_broadcast([np_, nt_, 4]), ALU.add),
                 r=psk(0) + [f"dtb{l}"], w=["gall"])
        for (np_, ts_, nt_) in REG:
            P.op("act", actf(gall[:np_, ts_, :], gall[:np_, ts_, :], AF.Exp), r=["gall"], w=["gall"])
        for (np_, ts_, nt_) in REG:
            P.op("act", actf(gall[:np_, ts_, :], gall[:np_, ts_, :], AF.Ln, bias=onec[:np_, 0:1]),
                 r=["gall", "onec"], w=["gall"])
        for (np_, ts_, nt_) in REG:
            P.op("dve", tt(gall[:np_, ts_, :], gall[:np_, ts_, :],
                           nexpA[l][:np_, H4:H4 + 4].unsqueeze(1).to_broadcast([np_, nt_, 4]), ALU.mult),
                 r=["gall", f"nexpA{l}"], w=["gall"])
        for (np_, ts_, nt_) in REG:
            P.op("act", actf(ball[:np_, ts_, :], psab[:np_, ts_, 8 + H4:8 + H4 + 4], AF.Exp, scale=-1.0), r=psk(0),
                 w=["ball"])
        for (np_, ts_, nt_) in REG:
            P.op("dve", ts(ball[:np_, ts_, :], ball[:np_, ts_, :], 1.0, ALU.add), r=["ball"], w=["ball"])
        for (np_, ts_, nt_) in REG:
            P.op("dve", rcp(ball[:np_, ts_, :], ball[:np_, ts_, :]), r=["ball"], w=["ball"])
        for c in range(NT):
            tw_ = TW[c]
            utri_ = CF[:tw_, CF_UTRIS:CF_UTRIS + tw_] if c == 16 else CF[:tw_, CF_UTRI:CF_UTRI + tw_]
            ones__ = CF[:tw_, CF_ONESS:CF_ONESS + tw_] if c == 16 else CF[:tw_, CF_ONES:CF_ONES + tw_]
            P.op("pe", mm(psg[:tw_, c, 0:4], utri_, gall[:tw_, c, :]), r=["CF", "gall"], w=psk(1))
            P.op("pe", mm(psg[:tw_, c, 4:8], ones__, gall[:tw_, c, :]), r=["CF", "gall"], w=psk(1))
        for (np_, ts_, nt_) in REG:
            P.op("dve", cp(sc3all[:np_, ts_, 0:4], psg[:np_, ts_, 0:4]), r=psk(1), w=["sc3all"])
        for (np_, ts_, nt_) in REG:
            P.op("dve", tt(sc3all[:np_, ts_, 4:8], psg[:np_, ts_, 4:8], sc3all[:np_, ts_, 0:4], ALU.subtract),
                 r=psk(1) + ["sc3all"], w=["sc3all"])
        for (np_, ts_, nt_) in REG:
            P.op("dve", cp(sc3all[:np_, ts_, 8:12], psg[:np_, ts_, 4:8]), r=psk(1), w=["sc3all"])
        for (np_, ts_, nt_) in REG:
            P.op("act", actf(eall[:np_, ts_, :], sc3all[:np_, ts_, :], AF.Exp), r=["sc3all"], w=["eall"])
        for (np_, ts_, nt_) in REG:
            P.op("dve", ts(negg[:np_, ts_, :], sc3all[:np_, ts_, 0:4], -1.0, ALU.mult), r=["sc3all"], w=["negg"])
        tabs = (gall, ball, eall, negg)

        recs = [Recorder() for _ in range(2)]
        for th in range(2):
            for _ in phaseA_thread(recs[th], l, hh, th, Wab, Dg, (S_Q, S_K, S_V, S_Z), alloc2, tabs):
                pass
        merge_threads(P, recs)
        smp_keys = [n + f"_{t}" for t in range(2)
                    for n in ("full", "sc16", "SS", "SSb", "kmask", "qmask", "uexp", "eGlS", "gexp")]

        def prefetch(s0, wmat, col0, ncols):
            n = ncols // 256
            src = wmat.rearrange("(k p) c -> p k c", p=128)
            for i in range(n):
                P.dma("pool", wslot(s0 + i, 1, 256), src[:, :, col0 + i * 256:col0 + (i + 1) * 256],
                      w=ark(s0 + i) + smp_keys)

        if hh == 0:
            prefetch(8, wl, C_Q + 512, 512)
            prefetch(10, wl, C_K + 512, 512)
            prefetch(12, wl, C_V + 512, 512)
            prefetch(14, wl, C_Z + 512, 256)
        else:
            prefetch(0, w_in[l], C_GA, 1024)
            prefetch(4, w_proj_a[l], 0, 768)

    def phaseA_thread(P, l, hh, th, Wab, Dg, slots, alloc2, tabs):
        gall, ball, eall, negg = tabs
        NHT = 2
        HW = NHT * 128
        H0 = 4 * hh + NHT * th
        S_Q, S_K, S_V, S_Z = slots
        T_ = f"_{th}"
        B = [4 * th + i for i in range(4)]
        b0, b1, b2, b3 = B

        def K_(name):
            return name + T_

        xpre = [walloc([128, 6, 131], BF16, "xpre") for _ in range(2)]
        qkvs = walloc([128, 768], F32, "qkvs")
        F4a = walloc([128, 512], F32, "F4a")
        QN = walloc([128, HW], BF16, "QN")
        KN = walloc([128, HW], BF16, "KN")
        KB = walloc([128, HW], BF16, "KB")
        KT = walloc([128, HW], BF16, "KT")
        VB = walloc([128, HW], BF16, "VB")
        QKT = walloc([128, 6, 128], BF16, "QKT")
        E = walloc([128, NHT, 128], F32, "E")
        EM2 = walloc([128, NHT, 2, 128], BF16, "EM2")
        EMd = walloc([128, NHT, 128], BF16, "EMd")
        X = [walloc([128, NHT, 2, 128], BF16, "X") for _ in range(2)]
        R = [walloc([128, NHT, 128], BF16, "R") for _ in range(2)]
        qkmT = walloc([128, NHT, 128], BF16, "qkmT")
        Sst = walloc([128, NHT, 128], F32, "S")
        Sb = walloc([128, NHT, 128], BF16, "Sb")
        rhs2 = walloc([128, HW], BF16, "rhs2")
        ubf = walloc([128, HW], BF16, "ubf")
        tmp2 = walloc([128, HW], F32, "tmp2")
        ofp = walloc([128, HW], F32, "o")
        zs = walloc([128, HW], F32, "zs")
        ofb = walloc([128, HW], BF16, "of")
        sc = walloc([128, 96], F32, "sc")
        full = alloc2("full", [128, 6, 112], BF16)
        sc16 = alloc2("sc16", [128, 768], BF16)
        SS = alloc2("SS", [128, 4, NHT, 128], F32)
        SSb = alloc2("SSb", [128, 4, NHT, 128], BF16)
        kmask = alloc2("kmask", [128, NHT, 4, 64], BF16)
        qmask = alloc2("qmask", [128, NHT, 4, 64], BF16)
        uexp = alloc2("uexp", [128, NHT, 4, 128], BF16)
        eGlS = alloc2("eGlS", [128, 16 * NHT], F32)
        gexp = alloc2("gexp", [128, 16 * NHT], F32)
        NTs = alloc2("NTs", [128, NHT, 128], BF16, always=True)
        rbf = alloc2("rbf", [128, HW], BF16, always=True)
        qse = alloc2("qse", [128, HW], F32, always=True)

        APRE, EA, SP_, G4, EB, BETA = 0, 4, 8, 12, 16, 20
        SC3, EALL, SSQ8, RS8, CQ, CKB, CKT, NCBG, SSO, RSO, NEGG = 24, 36, 48, 56, 64, 68, 72, 76, 80, 84, 88

        P.op("dve", mset(Sst[:], 0.0), w=[K_("S")])
        P.op("dve", mset(Sb[:], 0.0), w=[K_("Sb")])
        P.op("dve", mset(xpre[0][:, :, 0:3], 0.0), w=[K_("xpre0")])

        wq = wview(S_Q, 512, th * 256, 256)
        wk = wview(S_K, 512, th * 256, 256)
        wv = wview(S_V, 512, th * 256, 256)
        wz = wview(S_Z, 512, th * 256, 256)
        wqkv = [wq, wk, wv]

        def jg(j):
            return (j // 2) * 4 + 2 * th + (j % 2)

        for qkv in range(3):
            c0 = qkv * 1024 + hh * 512 + th * 256
            P.dma("pool", sc16[0:48, qkv * 256:(qkv + 1) * 256], sconv[l][:, c0:c0 + 256], w=[K_("sc16")])
        pst3 = PS[b3][:].bitcast(BF16)[:, 0:6 * 48].rearrange("p (j t) -> p j t", j=6)
        for j in range(6):
            P.op("pe", trp(pst3[:, j, :], sc16[0:48, j * 128:(j + 1) * 128], identb[0:48, 0:48]),
                 r=[K_("sc16"), "CB"], w=psk(b3))
        P.op("act", acp(out=full[:, :, 0:48], in_=pst3), r=psk(b3), w=[K_("full")])
        yield

        def psa(j, tw):
            return PS[b0][:, j * 128:j * 128 + tw] if j < 4 else PS[b1][:, (j - 4) * 128:(j - 4) * 128 + tw]

        def psb_bank(j):
            return b2 if j < 4 else b3

        def emit_A1(c_):
            tw_, to_ = TW[c_], TO[c_]
            for j in range(6):
                wvw, wkey = wqkv[j // 2]
                for k in range(8):
                    P.op("pe", mm(psa(j, tw_), wvw[:, k, (j % 2) * 128:(j % 2 + 1) * 128], hT[:, k, to_:to_ + tw_],
                                  start=(k == 0), stop=(k == 7)), r=[wkey, ("hT", c_)], w=psk(b0 if j < 4 else b1))

        def emit_E(c_):
            tw_ = TW[c_]
            utri_ = CF[:tw_, CF_UTRIS:CF_UTRIS + tw_] if c_ == 16 else CF[:tw_, CF_UTRI:CF_UTRI + tw_]
            psgr = PS[b3][:, 256:512].rearrange("p (h f) -> p h f", h=NHT)
            for h in range(NHT):
                P.op("pe", mm(psgr[:tw_, h, :tw_],
                              gall[:tw_, c_, 2 * th + h:2 * th + h + 1].to_broadcast([tw_, tw_]), utri_),
                     r=["gall", "CF"], w=psk(b3))
            for h in range(NHT):
                P.op("act", actf(E[:tw_, h, :tw_], psgr[:tw_, h, :tw_], AF.Abs,
                                 bias=negg[:tw_, c_, 2 * th + h:2 * th + h + 1]), r=psk(b3) + ["negg"], w=[K_("E")])
            P.op("act", actf(E[:tw_, :, :tw_], E[:tw_, :, :tw_], AF.Exp, scale=-1.0), r=[K_("E")], w=[K_("E")])

        order = [16] + list(range(16))
        emit_A1(order[0])
        emit_E(order[0])
        for oi, c in enumerate(order):
            tw, to = TW[c], TO[c]
            smp = (c == 16)
            xk = K_(f"xpre{c % 2}")
            xcur, xprev = xpre[c % 2], xpre[(c + 1) % 2]
            hkey = ("hT", c)
            yield
            dstx = full if smp else xcur
            dkey = K_("full") if smp else xk
            c_off = 48 if smp else 3
            P.op("act", acp(out=dstx[:, 0:4, c_off:c_off + tw],
                            in_=PS[b0][:].rearrange("p (j t) -> p j t", j=4)[:, :, :tw]), r=psk(b0), w=[dkey])
            P.op("act", acp(out=dstx[:, 4:6, c_off:c_off + tw],
                            in_=PS[b1][:, 0:256].rearrange("p (j t) -> p j t", j=2)[:, :, :tw]), r=psk(b1), w=[dkey])
            if not smp and c > 0:
                P.op("dve", cp(xcur[:, :, 0:3], xprev[:, :, 128:131]), r=[K_(f"xpre{(c + 1) % 2}")], w=[xk])
            XS, shift, xskey = (full, 16, K_("full")) if smp else (xcur, 1, xk)
            for j in range(6):
                dst = PS[b2][:tw, j * 128:(j + 1) * 128] if j < 4 else PS[b3][:tw, (j - 4) * 128:(j - 3) * 128]
                for tap in range(4):
                    P.op("pe", mm(dst, XS[:, j, tap * shift:tap * shift + tw], Dg[:, jg(j), tap, :],
                                  start=(tap == 0), stop=(tap == 3)), r=[xskey, "Dg"], w=psk(psb_bank(j)))
            yield
            wvz, kz = wz
            for k in range(8):
                P.op("pe", mm(PS[b3][:tw, 256:512], hT[:, k, to:to + tw], wvz[:, k, :], start=(k == 0), stop=(k == 7)),
                     r=[hkey, kz], w=psk(b3))
            P.op("act", actf(qkvs[:tw, 0:512], PS[b2][:tw, :], AF.Silu), r=psk(b2), w=[K_("qkvs")])
            P.op("act", actf(qkvs[:tw, 512:768], PS[b3][:tw, 0:256], AF.Silu), r=psk(b3), w=[K_("qkvs")])
            P.op("act", actf(zs[:tw, :], PS[b3][:tw, 256:512], AF.Silu), r=psk(b3), w=[K_("zs")])
            N2 = NHT
            yield
            yield
            eG = eall[:tw, c, 2 * th:2 * th + 2]
            eGlG = eall[:tw, c, 4 + 2 * th:4 + 2 * th + 2]
            beta2 = ball[:tw, c, 2 * th:2 * th + 2]
            P.op("dve", tt(F4a[:tw, :], qkvs[:tw, 0:512], qkvs[:tw, 0:512], ALU.mult), r=[K_("qkvs")], w=[K_("F4a")])
            P.op("dve", red(sc[:tw, SSQ8:SSQ8 + 4], F4a[:tw, :].rearrange("p (h d) -> p h d", h=4)), r=[K_("F4a")],
                 w=[K_("sc_ssq")])
            P.op("act", actf(sc[:tw, SSQ8:SSQ8 + 4], sc[:tw, SSQ8:SSQ8 + 4], AF.Sqrt, bias=epsc[:tw, 0:1]),
                 r=[K_("sc_ssq"), "epsc"], w=[K_("sc_ssq")])
            P.op("dve", rcp(sc[:tw, RS8:RS8 + 4], sc[:tw, SSQ8:SSQ8 + 4]), r=[K_("sc_ssq")], w=[K_("sc_rs")])
            RSK = RS8 + N2
            P.op("dve", ts(sc[:tw, CQ:CQ + N2], sc[:tw, RS8:RS8 + N2], 128.0 ** -0.5, ALU.mult), r=[K_("sc_rs")],
                 w=[K_("sc_cq")])
            P.op("dve", tt(sc[:tw, CKB:CKB + N2], sc[:tw, RSK:RSK + N2], beta2, ALU.mult),
                 r=[K_("sc_rs"), "ball"], w=[K_("sc_ckb")])
            P.op("dve", tt(sc[:tw, CKT:CKT + N2], sc[:tw, RSK:RSK + N2], eGlG, ALU.mult),
                 r=[K_("sc_rs"), "eall"], w=[K_("sc_ckt")])
            P.op("dve", stt(sc[:tw, NCBG:NCBG + N2], beta2, -1.0, eG, ALU.mult, ALU.mult),
                 r=["ball", "eall"], w=[K_("sc_ncbg")])

            def bcs(ap2):
                return ap2.unsqueeze(2).to_broadcast([tw, N2, 128])

            def bc(col):
                return sc[:tw, col:col + N2].unsqueeze(2).to_broadcast([tw, N2, 128])

            def v3(t_, c0=0):
                return t_[:tw, c0:c0 + HW].rearrange("p (h d) -> p h d", h=N2)

            P.op("dve", tt(v3(QN), v3(qkvs, 0), bc(CQ), ALU.mult), r=[K_("qkvs"), K_("sc_cq")], w=[K_("QN")])
            P.op("dve", tt(v3(KN), v3(qkvs, 256), bc(RSK), ALU.mult), r=[K_("qkvs"), K_("sc_rs")], w=[K_("KN")])
            P.op("dve", tt(v3(KB), v3(qkvs, 256), bc(CKB), ALU.mult), r=[K_("qkvs"), K_("sc_ckb")], w=[K_("KB")])
            P.op("dve", tt(v3(KT), v3(qkvs, 256), bc(CKT), ALU.mult), r=[K_("qkvs"), K_("sc_ckt")], w=[K_("KT")])
            P.op("dve", tt(v3(VB), v3(qkvs, 512), bcs(beta2), ALU.mult), r=[K_("qkvs"), "ball"], w=[K_("VB")])
            yield
            pst0 = PS[b0][:].bitcast(BF16).rearrange("p (j t) -> p j t", j=8)
            for i, (src, skey) in enumerate(((QN, "QN"), (KN, "KN"), (KB, "KB"))):
                for h in range(N2):
                    P.op("pe", trp(pst0[:, i * 2 + h, :tw], src[:tw, h * 128:(h + 1) * 128], identb[:tw, :tw]),
                         r=[K_(skey), "CB"], w=psk(b0))
            P.op("act", acp(out=QKT[:, :, :tw], in_=pst0[:, 0:6, :tw]), r=psk(b0), w=[K_("QKT")])
            yield
            psK = PS[b2][:].rearrange("p (h c f) -> p h c f", h=2, c=2)
            psQ = PS[b3][:, 0:256].rearrange("p (h f) -> p h f", h=N2)
            for h in range(N2):
                P.op("pe", mm(psK[:tw, h, 0, :tw], QKT[:, 2 + h, :tw], QKT[:, 4 + h, :tw]), r=[K_("QKT")], w=psk(b2))
                P.op("pe", mm(psK[:tw, h, 1, :tw], QKT[:, 4 + h, :tw], QKT[:, 2 + h, :tw]), r=[K_("QKT")], w=psk(b2))
                P.op("pe", mm(psQ[:tw, h, :tw], QKT[:, 2 + h, :tw], QKT[:, 0 + h, :tw]), r=[K_("QKT")], w=psk(b3))
            if smp:
                nm2 = CF[:tw, CF_NM2S:CF_NM2S + 128].rearrange("p (c f) -> p c f", c=2)
                mud = CF[:tw, CF_MUDS:CF_MUDS + 64]
            else:
                nm2 = CF[:tw, CF_NM2:CF_NM2 + 256].rearrange("p (c f) -> p c f", c=2)
                mud = CF[:tw, CF_MUD:CF_MUD + 128]
            P.op("dve", tt(EM2[:tw, :, :, :tw], E[:tw, :, :tw].unsqueeze(2).to_broadcast([tw, N2, 2, tw]),
                           nm2.unsqueeze(1).to_broadcast([tw, N2, 2, tw]), ALU.mult), r=[K_("E"), "CF"], w=[K_("EM2")])
            P.op("dve", tt(EMd[:tw, :, :tw], E[:tw, :, :tw], mud.unsqueeze(1).to_broadcast([tw, N2, tw]), ALU.mult),
                 r=[K_("E"), "CF"], w=[K_("EMd")])
            P.op("dve", tt(X[0][:tw, :, :, :tw], psK[:tw, :, :, :tw], EM2[:tw, :, :, :tw], ALU.mult),
                 r=psk(b2) + [K_("EM2")], w=[K_("X0"), K_("X0") + "b"])
            P.op("dve", tt(qkmT[:tw, :, :tw], psQ[:tw, :, :tw], EMd[:tw, :, :tw], ALU.mult), r=psk(b3) + [K_("EMd")],
                 w=[K_("qkmT")])
            P.op("act", acp(out=NTs[:tw, :, :tw], in_=X[0][:tw, :, 0, :tw]), r=[K_("X0")], w=[K_("NTs")])
            P.op("dve", tt(R[0][:tw, :, :tw], X[0][:tw, :, 0, :tw],
                           identb[:tw, :tw].unsqueeze(1).to_broadcast([tw, N2, tw]), ALU.add), r=[K_("X0"), "CB"],
                 w=[K_("R0")])
            yield
            L = 1 if smp else 6
            psKT = PS[b2][:, 0:256].rearrange("p (h f) -> p h f", h=N2)
            psKN = PS[b1][:, 256:512].rearrange("p (h f) -> p h f", h=N2)
            for lev in range(1, L + 1):
                xo, xn = X[(lev - 1) % 2], X[lev % 2]
                xok, xnk = K_(f"X{(lev - 1) % 2}"), K_(f"X{lev % 2}")
                ro, rn = R[(lev - 1) % 2], R[lev % 2]
                rok, rnk = K_(f"R{(lev - 1) % 2}"), K_(f"R{lev % 2}")
                last = (lev == L)
                for h in range(N2):
                    P.op("pe", mm(psKN[:tw, h, :tw], xo[:tw, h, 0, :tw], xo[:tw, h, 1, :tw]), r=[xok, xok + "b"],
                         w=psk(b1))
                if not last:
                    for h in range(N2):
                        P.op("pe", mm(psKT[:tw, h, :tw], xo[:tw, h, 1, :tw], xo[:tw, h, 0, :tw]), r=[xok, xok + "b"],
                             w=psk(b2))
                P.op("act", acp(out=xn[:tw, :, 1, :tw], in_=psKN[:tw, :, :tw]), r=psk(b1), w=[xnk + "b"])
                if not last:
                    P.op("dve", cp(xn[:tw, :, 0, :tw], psKT[:tw, :, :tw]), r=psk(b2), w=[xnk])
                for h in range(N2):
                    P.op("pe", mm(psQ[:tw, h, :tw], xn[:tw, h, 1, :tw], ro[:tw, h, :tw]), r=[xnk + "b", rok],
                         w=psk(b3))
                P.op("dve", tt(rn[:tw, :, :tw], ro[:tw, :, :tw], psQ[:tw, :, :tw], ALU.add), r=[rok] + psk(b3),
                     w=[rnk])
                yield
            TTm, ttk = R[L % 2], K_(f"R{L % 2}")
            ps_kS = PS[b0][:, 0:256].rearrange("p (h v) -> p h v", h=N2)
            ps_qS = PS[b0][:, 256:512].rearrange("p (h v) -> p h v", h=N2)
            ps_u = PS[b1][:, 0:256].rearrange("p (h v) -> p h v", h=N2)
            ps_au = PS[b1][:, 256:512].rearrange("p (h v) -> p h v", h=N2)
            ps_o2 = PS[b3][:, 256:512].rearrange("p (h v) -> p h v", h=N2)
            ps_ds = PS[b2][:, 0:256].rearrange("p (h v) -> p h v", h=N2)
            if not smp:
                for h in range(N2):
                    P.op("pe", mm(ps_kS[:tw, h, :], QKT[:, 2 + h, :tw], Sb[:, h, :]), r=[K_("QKT"), K_("Sb")],
                         w=psk(b0))
                    P.op("pe", mm(ps_qS[:tw, h, :], QKT[:, 0 + h, :tw], Sb[:, h, :]), r=[K_("QKT"), K_("Sb")],
                         w=psk(b0))
                P.op("dve", tt(v3(tmp2), ps_kS[:tw, :, :], bc(NCBG), ALU.mult), r=psk(b0) + [K_("sc_ncbg")],
                     w=[K_("tmp2")])
                P.op("dve", tt(v3(qse), ps_qS[:tw, :, :], bcs(eG), ALU.mult), r=psk(b0) + ["eall"],
                     w=[K_("qse")])
            else:
                segm = CB[:, CB_SEG:CB_SEG + 1024].rearrange("p (s t) -> p s t", s=16)
                P.op("dve", tt(gexp[:tw, :].rearrange("p (s h) -> p s h", s=16),
                               gall[:tw, c, 2 * th:2 * th + 2].unsqueeze(1).to_broadcast([tw, 16, N2]),
                               CF[:tw, CF_SEGT:CF_SEGT + 16].unsqueeze(2).to_broadcast([tw, 16, N2]), ALU.mult),
                     r=["gall", "CF"], w=[K_("gexp")])
                P.op("pe", mm(PS[b1][:, 0:16 * N2], CF[:tw, CF_ONES:CF_ONES + 128], gexp[:tw, :]),
                     r=["CF", K_("gexp")], w=psk(b1))
                P.op("act", actf(eGlS[:, :], PS[b1][:, 0:16 * N2], AF.Exp), r=psk(b1), w=[K_("eGlS")])
                for g in range(4):
                    for s_ in range(4):
                        P.dma("sp", SS[:, s_], sdelta[l, 4 * g + s_, H0:H0 + N2].rearrange("h k v -> k h v"),
                              w=[K_("SS")])
                    P.op("act", acp(out=SSb[:], in_=SS[:]), r=[K_("SS")], w=[K_("SSb")])
                    P.op("dve", tt(kmask[:], QKT[:, 2:4, :64].unsqueeze(2).to_broadcast([128, N2, 4, 64]),
                                   segm[:, 4 * g:4 * g + 4, :].unsqueeze(1).to_broadcast([128, N2, 4, 64]), ALU.mult),
                         r=[K_("QKT"), "CB"], w=[K_("kmask")])
                    P.op("dve", tt(qmask[:], QKT[:, 0:2, :64].unsqueeze(2).to_broadcast([128, N2, 4, 64]),
                                   segm[:, 4 * g:4 * g + 4, :].unsqueeze(1).to_broadcast([128, N2, 4, 64]), ALU.mult),
                         r=[K_("QKT"), "CB"], w=[K_("qmask")])
                    for s in range(4):
                        for h in range(N2):
                            first = (g == 0 and s == 0)
                            lastm = (g == 3 and s == 3)
                            P.op("pe", mm(PS[B[h]][:tw, 0:128], kmask[:, h, s, :], SSb[:, s, h, :], start=first,
                                          stop=lastm), r=[K_("kmask"), K_("SSb")], w=psk(B[h]))
                            P.op("pe", mm(PS[B[2 + h]][:tw, 0:128], qmask[:, h, s, :], SSb[:, s, h, :], start=first,
                                          stop=lastm), r=[K_("qmask"), K_("SSb")], w=psk(B[2 + h]))
                    yield
                for h in range(N2):
                    P.op("dve", ts(tmp2[:tw, h * 128:(h + 1) * 128], PS[B[h]][:tw, 0:128],
                                   sc[:tw, NCBG + h:NCBG + h + 1], ALU.mult), r=psk(B[h]) + [K_("sc_ncbg")],
                         w=[K_("tmp2")])
                    P.op("dve", ts(qse[:tw, h * 128:(h + 1) * 128], PS[B[2 + h]][:tw, 0:128],
                                   eall[:tw, c, 2 * th + h:2 * th + h + 1], ALU.mult), r=psk(B[2 + h]) + ["eall"],
                         w=[K_("qse")])
            P.op("dve", tt(rhs2[:tw, :], tmp2[:tw, :], VB[:tw, :], ALU.add), r=[K_("tmp2"), K_("VB")], w=[K_("rhs2")])
            for h in range(N2):
                P.op("pe", mm(ps_u[:tw, h, :], TTm[:tw, h, :tw], rhs2[:tw, h * 128:(h + 1) * 128]),
                     r=[ttk, K_("rhs2")], w=psk(b1))
            P.op("act", acp(out=ubf[:tw, :], in_=PS[b1][:tw, 0:256]), r=psk(b1), w=[K_("ubf")])
            yield
            for h in range(N2):
                P.op("pe", mm(ps_au[:tw, h, :], NTs[:tw, h, :tw], ubf[:tw, h * 128:(h + 1) * 128]),
                     r=[K_("NTs"), K_("ubf")], w=psk(b1))
            P.op("dve", tt(tmp2[:tw, :], rhs2[:tw, :], ubf[:tw, :], ALU.subtract), r=[K_("rhs2"), K_("ubf")],
                 w=[K_("tmp2")])
            P.op("dve", tt(rbf[:tw, :], tmp2[:tw, :], PS[b1][:tw, 256:512], ALU.add), r=[K_("tmp2")] + psk(b1),
                 w=[K_("rbf")])
            for h in range(N2):
                P.op("pe", mm(ps_u[:tw, h, :], TTm[:tw, h, :tw], rbf[:tw, h * 128:(h + 1) * 128]), r=[ttk, K_("rbf")],
                     w=psk(b1))
            P.op("dve", tt(ubf[:tw, :], ubf[:tw, :], PS[b1][:tw, 0:256], ALU.add), r=[K_("ubf")] + psk(b1),
                 w=[K_("ubf")])
            yield
            for h in range(N2):
                P.op("pe", mm(ps_o2[:tw, h, :], qkmT[:tw, h, :tw], ubf[:tw, h * 128:(h + 1) * 128]),
                     r=[K_("qkmT"), K_("ubf")], w=psk(b3))
            P.op("dve", tt(ofp[:tw, :], qse[:tw, :], PS[b3][:tw, 256:512], ALU.add), r=[K_("qse")] + psk(b3),
                 w=[K_("o")])
            if not smp:
                for h in range(N2):
                    P.op("pe", mm(ps_ds[:, h, :], KT[:tw, h * 128:(h + 1) * 128], ubf[:tw, h * 128:(h + 1) * 128]),
                         r=[K_("KT"), K_("ubf")], w=psk(b2))
                for h in range(N2):
                    P.op("dve", stt(Sst[:, h, :], Sst[:, h, :], eall[:, c, 8 + 2 * th + h:8 + 2 * th + h + 1],
                                    ps_ds[:, h, :], ALU.mult, ALU.add), r=[K_("S"), "eall"] + psk(b2), w=[K_("S")])
                P.op("act", acp(out=Sb[:], in_=Sst[:]), r=[K_("S")], w=[K_("Sb")])
                if c == 15:
                    P.dma("sp", delta_p[l, H0:H0 + N2].rearrange("h k v -> k h v"), Sst[:], r=[K_("S")],
                          is_output=True)
            else:
                segT = CF[:tw, CF_SEGT:CF_SEGT + 16]
                for g in range(4):
                    for s_ in range(4):
                        P.dma("sp", SS[:, s_], sdelta[l, 4 * g + s_, H0:H0 + N2].rearrange("h k v -> k h v"),
                              w=[K_("SS")])
                    P.op("dve", tt(uexp[:tw, :, :, :],
                                   ubf[:tw, :].rearrange("p (h v) -> p h v", h=N2).unsqueeze(2).to_broadcast([tw, N2, 4, 128]),
                                   segT[:, 4 * g:4 * g + 4].unsqueeze(1).unsqueeze(3).to_broadcast([tw, N2, 4, 128]),
                                   ALU.mult), r=[K_("ubf"), "CF"], w=[K_("uexp")])
                    for h in range(N2):
                        bank = B[2 + h]
                        psd = PS[bank][:].rearrange("p (s v) -> p s v", s=4)
                        P.op("pe", mm(PS[bank][:, :], KT[:tw, h * 128:(h + 1) * 128], uexp[:tw, h, :, :]),
                             r=[K_("KT"), K_("uexp")], w=psk(bank))
                        egl = eGlS[:, :].rearrange("p (s h) -> p s h", s=16)[:, 4 * g:4 * g + 4, h]
                        P.op("dve", tt(SS[:, :, h, :], SS[:, :, h, :], egl.unsqueeze(2).to_broadcast([128, 4, 128]),
                                       ALU.mult), r=[K_("SS"), K_("eGlS")], w=[K_("SS")])
                        P.op("dve", tt(SS[:, :, h, :], SS[:, :, h, :], psd, ALU.add), r=[K_("SS")] + psk(bank),
                             w=[K_("SS")])
                    for s_ in range(4):
                        P.dma("sp", delta_s[l, 4 * g + s_, H0:H0 + N2].rearrange("h k v -> k h v"), SS[:, s_],
                              r=[K_("SS")], is_output=True)
                    yield
            yield
            if oi + 1 < len(order):
                emit_A1(order[oi + 1])
            P.op("dve", tt(F4a[:tw, 0:HW], ofp[:tw, :], ofp[:tw, :], ALU.mult), r=[K_("o")], w=[K_("F4a")])
            P.op("dve", red(sc[:tw, SSO:SSO + N2], F4a[:tw, 0:HW].rearrange("p (h d) -> p h d", h=N2)),
                 r=[K_("F4a")], w=[K_("sc_sso")])
            P.op("act", actf(sc[:tw, SSO:SSO + N2], sc[:tw, SSO:SSO + N2], AF.Sqrt, scale=1.0 / 128,
                             bias=epsc[:tw, 0:1]), r=[K_("sc_sso"), "epsc"], w=[K_("sc_sso")])
            P.op("dve", rcp(sc[:tw, RSO:RSO + N2], sc[:tw, SSO:SSO + N2]), r=[K_("sc_sso")], w=[K_("sc_rso")])
            if oi + 1 < len(order):
                emit_E(order[oi + 1])
            P.op("dve", tt(v3(tmp2), v3(ofp), bc(RSO), ALU.mult), r=[K_("o"), K_("sc_rso")], w=[K_("tmp2")])
            P.op("dve", tt(ofb[:tw, :], tmp2[:tw, :], zs[:tw, :], ALU.mult), r=[K_("tmp2"), K_("zs")], w=[K_("of")])
            pso = PS[b2][:].bitcast(BF16).rearrange("p (j t) -> p j t", j=8)
            for h in range(N2):
                P.op("pe", trp(pso[:, h, :tw], ofb[:tw, h * 128:(h + 1) * 128], identb[:tw, :tw]),
                     r=[K_("of"), "CB"], w=psk(b2))
            P.op("act", actf(oT[:, H0:H0 + N2, to:to + tw], pso[:, 0:N2, :tw], AF.Copy,
                             scale=colv[l][:, CV_GN:CV_GN + 1]), r=psk(b2) + [f"colv{l}"], w=[("oT", c, hh, th)])
            yield
            if c == 15 or smp:
                lo = to + 125 if c == 15 else to
                n = 3 if c == 15 else 64
                for qkv in range(3):
                    wvw, wkey = wqkv[qkv]
                    dst = PS[b2][:n, qkv * 256:(qkv + 1) * 256] if qkv < 2 else PS[b3][:n, 0:256]
                    for k in range(8):
                        P.op("pe", mm(dst, hT[:, k, lo:lo + n], wvw[:, k, :], start=(k == 0), stop=(k == 7)),
                             r=[hkey, wkey], w=psk(b2 if qkv < 2 else b3))
                P.op("dve", cp(qkvs[:n, 0:512], PS[b2][:n, :]), r=psk(b2), w=[K_("qkvs")])
                P.op("dve", cp(qkvs[:n, 512:768], PS[b3][:n, 0:256]), r=psk(b3), w=[K_("qkvs")])
                for qkv in range(3):
                    c0 = qkv * 1024 + hh * 512 + th * 256
                    if c == 15:
                        P.dma("sp", conv_p[l, :, c0:c0 + 256], qkvs[0:3, qkv * 256:(qkv + 1) * 256], r=[K_("qkvs")],
                              is_output=True)
                    else:
                        P.dma("sp", conv_s[l, :, :, c0:c0 + 256].rearrange("j s c -> (j s) c"),
                              qkvs[16:64, qkv * 256:(qkv + 1) * 256], r=[K_("qkvs")], is_output=True)
                yield

    GROUPS = [(0, 512), (512, 512), (1024, 512), (1536, 512), (2048, 64)]

    def gkeys(name, g0, n):
        return [(name, c) for c in range(g0 // 128, (g0 + n + 127) // 128)]

    def phaseC1(l):
        wreset()
        S_GA, S_PA = 0, 4
        load_w(7, w_proj_a[l], 768, 256)
        sg = [walloc([128, 512], F32, "sg") for _ in range(2)]
        mag = walloc([128, 8, 512], BF16, "mag")
        for (g0, n) in GROUPS:
            hk = gkeys("hT", g0, n)
            ok = [(k_[0], k_[1], hh, th_) for k_ in gkeys("oT", g0, n) for hh in range(2) for th_ in range(2)]
            for cc in range(8):
                wga, kga = wview(S_GA, 1024, cc * 128, 128)
                wpa, kpa = wview(S_PA, 1024, cc * 128, 128)
                ba, by = cc % 4, 4 + cc % 4
                for k in range(8):
                    P.op("pe", mm(PS[ba][:, :n], wga[:, k, :], hT[:, k, g0:g0 + n], start=(k == 0), stop=(k == 7)),
                         r=[kga] + hk, w=psk(ba))
                for k in range(8):
                    P.op("pe", mm(PS[by][:, :n], wpa[:, k, :], oT[:, k, g0:g0 + n], start=(k == 0), stop=(k == 7)),
                         r=[kpa] + ok, w=psk(by))
                P.op("act", actf(sg[cc % 2][:, :n], PS[ba][:, :n], AF.Sigmoid), r=psk(ba), w=[f"sg{cc % 2}"])
                P.op("dve", tt(mag[:, cc, :n], sg[cc % 2][:, :n], PS[by][:, :n], ALU.mult),
                     r=[f"sg{cc % 2}"] + psk(by), w=["mag"])
            P.op("pool", cp(oT[:, :, g0:g0 + n], mag[:, :, :n]), r=["mag"], w=ok)
        load_w_wide(8, w_in[l], C_U, 1024)
        load_w(12, w_in[l], C_GP, 1024)

    def phaseB(l):
        wreset()
        S_U, S_GP, S_GB, S_PB = 8, 12, 0, 4
        Wp = walloc([128, 2, 4, 256], BF16, "Wp")
        for kk_ in range(2):
            P.dma("pool", Wp[:, kk_], pool_w[l][:, kk_ * 128:(kk_ + 1) * 128, :].rearrange("g p d -> p g d"),
                  w=["Wp"])
        for i_ in range(4):
            load_w(S_GB + i_, w_in[l], C_GB + 256 * i_, 256)
            load_w(S_PB + i_, w_proj_b[l], 256 * i_, 256)
        ub = [walloc([128, D], BF16, "ub") for _ in range(2)]
        u32 = walloc([128, D], F32, "u32")
        hb0 = walloc([128, D], BF16, "hb0")
        hb1 = walloc([128, D], BF16, "hb1")
        YT = walloc([128, 8, 128], BF16, "YT")
        ypT = walloc([128, 8, 512], BF16, "ypT")
        sgp = [walloc([128, 512], F32, "sgp") for _ in range(2)]
        gyT = walloc([128, 8, 512], BF16, "gyT")
        sgb = [walloc([128, 512], F32, "sgb") for _ in range(2)]
        mb = walloc([128, 512], F32, "mb")
        P.dma("pool", hb0[:, :], spool[l, 0:128, :], w=["hb0"])
        P.op("dve", mset(hb1[:, :], 0.0), w=["hb1"])
        P.dma("pool", hb1[0:64, :], spool[l, 128:192, :], w=["hb1"])
        P.dma("pool", hb1[64:112, :], spool[l, 192:240, :], w=["hb1"])
        ps_flat = pool_s[l].rearrange("j s c -> (j s) c")
        import os
        SKIP = os.environ.get("KB_SKIP", "").split(",")
        for (r0, nr) in (((64, 128), (192, 48)) if "hist" not in SKIP else ()):
            P.dma("sp", u32[0:nr, :], spool[l, r0:r0 + nr, :], w=["u32"])
            P.dma("sp", ps_flat[r0 - 64:r0 - 64 + nr, :], u32[0:nr, :], r=["u32"], is_output=True)

        def band(off, g, tw, w=128):
            return CB[:tw, off + g * w:off + g * w + (tw if w == 128 else w)]

        for (g0, n) in (GROUPS if "smp" not in SKIP else GROUPS[:-1]):
            c0 = g0 // 128
            nt = max(1, n // 128)
            for ti in range(nt):
                c = c0 + ti
                tw, to = TW[c], TO[c]
                smp = (c == 16)
                hkey = ("hT", c)
                ucur, uprev = ub[c % 2], ub[(c + 1) % 2]
                uk, upk = f"ub{c % 2}", f"ub{(c + 1) % 2}"
                for half in range(2):
                    wv_, wk_ = wwide(S_U, half)
                    for k in range(8):
                        P.op("pe", mm(PS[half][:tw, :], hT[:, k, to:to + tw], wv_[:, k, :],
                                      start=(k == 0), stop=(k == 7)), r=[hkey] + wk_, w=psk(half))
                for half in range(2):
                    P.op("act", acp(out=ucur[:tw, half * 512:(half + 1) * 512],
                                                            in_=PS[half][:tw, :]), r=psk(half), w=[uk])
                if (c == 15 or smp) and "pout" not in SKIP:
                    if smp:
                        for half in range(2):
                            P.op("dve", cp(u32[:tw, half * 512:(half + 1) * 512], PS[half][:tw, :]), r=psk(half),
                                 w=["u32"])
                    if c == 15:
                        for half in range(2):
                            wv_, wk_ = wwide(S_U, half)
                            for k in range(8):
                                P.op("pe", mm(PS[half][:15, :], hT[:, k, PT - 15:PT], wv_[:, k, :],
                                              start=(k == 0), stop=(k == 7)), r=[hkey] + wk_, w=psk(half))
                        for half in range(2):
                            P.op("dve", cp(u32[:15, half * 512:(half + 1) * 512], PS[half][:15, :]), r=psk(half),
                                 w=["u32"])
                        P.dma("sp", pool_p[l], u32[0:15, :], r=["u32"], is_output=True)
                    elif "spout" not in SKIP:
                        P.dma("sp", pool_s[l, 11:15].rearrange("j s c -> (j s) c"), u32[0:64, :], r=["u32"],
                              is_output=True)
                psy = [PS[2][:].rearrange("p (j t) -> p j t", j=4), PS[3][:].rearrange("p (j t) -> p j t", j=4)]
                for cc in range(8):
                    g = cc // 2
                    dst = psy[cc // 4][:, cc % 4, :tw]
                    lhs = ucur[:tw, cc * 128:(cc + 1) * 128]
                    if smp and "sband" in SKIP:
                        P.op("pe", mm(dst, lhs, CB[:64, CB_SN + g * 64:CB_SN + (g + 1) * 64], start=True, stop=True),
                             r=[uk, "CB"], w=psk(2 + cc // 4))
                    elif smp:
                        P.op("pe", mm(dst, lhs, CB[:64, CB_SN + g * 64:CB_SN + (g + 1) * 64], start=True, stop=False),
                             r=[uk, "CB"], w=psk(2 + cc // 4))
                        P.op("pe", mm(dst, hb0[:, cc * 128:(cc + 1) * 128], CB[:, CB_SH0 + g * 64:CB_SH0 + (g + 1) * 64],
                                      start=False, stop=False), r=["hb0", "CB"], w=psk(2 + cc // 4))
                        P.op("pe", mm(dst, hb1[:, cc * 128:(cc + 1) * 128],
                                      CB[:, CB_SH1 + g * 64:CB_SH1 + (g + 1) * 64], start=False, stop=True),
                             r=["hb1", "CB"], w=psk(2 + cc // 4))
                    elif c == 0:
                        P.op("pe", mm(dst, lhs, CB[:, CB_BF + g * 128:CB_BF + (g + 1) * 128]), r=[uk, "CB"],
                             w=psk(2 + cc // 4))
                    else:
                        P.op("pe", mm(dst, lhs, CB[:, CB_BC + g * 128:CB_BC + (g + 1) * 128], start=True, stop=False),
                             r=[uk, "CB"], w=psk(2 + cc // 4))
                        P.op("pe", mm(dst, uprev[:, cc * 128:(cc + 1) * 128],
                                      CB[:, CB_BP + g * 128:CB_BP + (g + 1) * 128], start=False, stop=True),
                             r=[upk, "CB"], w=psk(2 + cc // 4))
                for b in range(2):
                    P.op("act", acp(out=YT[:, 4 * b:4 * b + 4, :tw], in_=psy[b][:, :, :tw]),
                         r=psk(2 + b), w=["YT"])
                psl = [PS[4][:].rearrange("p (j t) -> p j t", j=4), PS[5][:].rearrange("p (j t) -> p j t", j=4)]
                for dc in range(8):
                    g = dc // 2
                    for kk in range(2):
                        P.op("pe", mm(psl[dc // 4][:, dc % 4, :tw], Wp[:, kk, g, (dc % 2) * 128:(dc % 2 + 1) * 128],
                                      YT[:, 2 * g + kk, :tw], start=(kk == 0), stop=(kk == 1)), r=["Wp", "YT"],
                             w=psk(4 + dc // 4))
                for b in range(2):
                    P.op("dve", tt(ypT[:, 4 * b:4 * b + 4, ti * 128:ti * 128 + tw], psl[b][:, :, :tw],
                                   colv[l][:, CV_PS + 4 * b:CV_PS + 4 * b + 4].unsqueeze(2).to_broadcast([128, 4, tw]),
                                   ALU.mult), r=psk(4 + b) + [f"colv{l}"], w=["ypT"])
            hk = gkeys("hT", g0, n)
            ok = [(k_[0], k_[1], hh, th_) for k_ in gkeys("oT", g0, n) for hh in range(2) for th_ in range(2)]
            for cc in range(8):
                wgp, kgp = wview(S_GP, 1024, cc * 128, 128)
                for k in range(8):
                    P.op("pe", mm(PS[6][:, :n], wgp[:, k, :], hT[:, k, g0:g0 + n], start=(k == 0), stop=(k == 7)),
                         r=[kgp] + hk, w=psk(6))
                P.op("act", actf(sgp[cc % 2][:, :n], PS[6][:, :n], AF.Silu), r=psk(6), w=[f"sgp{cc % 2}"])
                P.op("dve", tt(gyT[:, cc, :n], sgp[cc % 2][:, :n], ypT[:, cc, :n], ALU.mult),
                     r=[f"sgp{cc % 2}", "ypT"], w=["gyT"])
            for cc in range(8):
                wgb, kgb = wview(S_GB, 1024, cc * 128, 128)
                wpb, kpb = wview(S_PB, 1024, cc * 128, 128)
                bb = 6 + cc % 2
                bg = cc % 2
                for k in range(8):
                    P.op("pe", mm(PS[bg][:, :n], wgb[:, k, :], hT[:, k, g0:g0 + n], start=(k == 0), stop=(k == 7)),
                         r=[kgb] + hk, w=psk(bg))
                for k in range(8):
                    P.op("pe", mm(PS[bb][:, :n], wpb[:, k, :], gyT[:, k, :n], start=(k == 0), stop=(k == 7)),
                         r=[kpb, "gyT"], w=psk(bb))
                P.op("act", actf(sgb[cc % 2][:, :n], PS[bg][:, :n], AF.Sigmoid), r=psk(bg), w=[f"sgb{cc % 2}"])
                P.op("dve", tt(mb[:, :n], sgb[cc % 2][:, :n], PS[bb][:, :n], ALU.mult), r=[f"sgb{cc % 2}"] + psk(bb),
                     w=["mb"])
                P.op("dve", tt(oT[:, cc, g0:g0 + n], oT[:, cc, g0:g0 + n], mb[:, :n], ALU.add), r=["mb"] + ok, w=ok)

    def phaseC2(l, last):
        wreset()
        S_O, S_G = 8, 12
        load_w_wide(S_O, w_out[l], 0, 1024)
        load_w_wide(S_G, w_ple_gate[l], 0, 1024)
        if last:
            Wpp = wslot(7, 1, 256).rearrange("p k c -> p (k c)").rearrange("p (k c) -> p k c", k=2)
            wppk = ark(7)
        else:
            Wpp = walloc([128, 2, D], BF16, "Wpp")[:]
            wppk = ["Wpp"]
        P.dma("pool", Wpp, w_ple_proj[l].rearrange("(k p) c -> p k c", p=128), w=wppk)
        fn = None
        if last:
            fn = walloc([128, D], F32, "fn")
            P.dma("sp", fn[:], final_norm.partition_broadcast(128), w=["fn"])
        xsrc = x_all if l == 0 else xs
        recs = [Recorder() for _ in range(2)]
        for th in range(2):
            c2_thread(recs[th], l, last, th, (S_O, S_G), Wpp, wppk, fn, xsrc)
        merge_threads(P, recs)
        if not last:
            wl = w_in[l + 1]
            load_w(0, wl, C_Q, 512)
            load_w(2, wl, C_K, 512)
            load_w(4, wl, C_V, 512)
            load_w(6, wl, C_Z, 512)

    def c2_thread(P, l, last, th, slots, Wpp, wppk, fn, xsrc):
        S_O, S_G = slots
        T_ = f"_c{th}"
        b0, b1, b2, b3 = [4 * th + i for i in range(4)]

        def K_(n):
            return n + T_

        xt = walloc([128, D], F32, "xt")
        junk = walloc([128, D], BF16, "junk")
        ssq = walloc([128, 4], F32, "ssq")
        ssqp = walloc([128, 4], F32, "ssqp")
        hb = walloc([128, D], BF16, "hb")
        x1 = walloc([128, D], F32, "x1")
        hp = walloc([128, D], BF16, "hp")
        hpT = walloc([128, 8, 128], BF16, "hpT")
        sgt = walloc([128, D], F32, "sgt")
        pt = walloc([128, 256], F32, "pt")
        pbf = walloc([128, 256], BF16, "pbf")
        pT = walloc([128, 2, 128], BF16, "pT")
        xo = walloc([128, D], F32, "x2")
        for c in range(th, NT, 2):
            tw, to = TW[c], TO[c]
            mkeys = [("oT", c, hh_, th_) for hh_ in range(2) for th_ in range(2)]
            P.dma("sp", xt[:tw, :], xsrc[to:to + tw, :], r=[("xs", c)], w=[K_("xt")])
            P.dma("sp", pt[:tw, :], p_all[l, to:to + tw, :], w=[K_("pt")])
            for half in range(2):
                wv_, wk_ = wwide(S_O, half)
                for k in range(8):
                    P.op("pe", mm(PS[b0 + half][:tw, :], oT[:, k, to:to + tw], wv_[:, k, :],
                                  start=(k == 0), stop=(k == 7)), r=mkeys + wk_, w=psk(b0 + half))
            for half in range(2):
                P.op("dve", tt(x1[:tw, half * 512:(half + 1) * 512], xt[:tw, half * 512:(half + 1) * 512],
                               PS[b0 + half][:tw, :], ALU.add), r=[K_("xt")] + psk(b0 + half), w=[K_("x1")])
            P.op("act", actf(junk[:tw, :], x1[:tw, :], AF.Square, accum=ssqp[:tw, 0:1]), r=[K_("x1")],
                 w=[K_("junk"), K_("ssqp")])
            P.op("act", actf(ssqp[:tw, 1:2], ssqp[:tw, 0:1], AF.Sqrt, scale=1.0 / D, bias=epsc[:tw, 0:1]),
                 r=[K_("ssqp"), "epsc"], w=[K_("ssqp1")])
            P.op("dve", rcp(ssqp[:tw, 2:3], ssqp[:tw, 1:2]), r=[K_("ssqp1")], w=[K_("ssqp2")])
            P.op("dve", ts(hp[:tw, :], x1[:tw, :], ssqp[:tw, 2:3], ALU.mult), r=[K_("x1"), K_("ssqp2")], w=[K_("hp")])
            pst = PS[b2][:].bitcast(BF16).rearrange("p (k t) -> p k t", k=8)
            for k in range(8):
                P.op("pe", trp(pst[:, k, :tw], hp[:tw, k * 128:(k + 1) * 128], identb[:tw, :tw]), r=[K_("hp"), "CB"],
                     w=psk(b2))
            P.op("dve", tt(hpT[:, :, :tw], pst[:, :, :tw],
                           colv[l][:, CV_NP:CV_NP + 8].unsqueeze(2).to_broadcast([128, 8, tw]), ALU.mult),
                 r=psk(b2) + [f"colv{l}"], w=[K_("hpT")])
            for half in range(2):
                wv_, wk_ = wwide(S_G, half)
                for k in range(8):
                    P.op("pe", mm(PS[b0 + half][:tw, :], hpT[:, k, :tw], wv_[:, k, :],
                                  start=(k == 0), stop=(k == 7)), r=[K_("hpT")] + wk_, w=psk(b0 + half))
            for half in range(2):
                P.op("act", actf(sgt[:tw, half * 512:(half + 1) * 512], PS[b0 + half][:tw, :], AF.Sigmoid),
                     r=psk(b0 + half), w=[K_("sgt")])
            P.op("dve", cp(pbf[:tw, :], pt[:tw, :]), r=[K_("pt")], w=[K_("pbf")])
            pstp = PS[b3][:].bitcast(BF16)[:, 0:256].rearrange("p (k t) -> p k t", k=2)
            for k in range(2):
                P.op("pe", trp(pstp[:, k, :tw], pbf[:tw, k * 128:(k + 1) * 128], identb[:tw, :tw]),
                     r=[K_("pbf"), "CB"], w=psk(b3))
            P.op("act", acp(out=pT[:, :, :tw], in_=pstp[:, :, :tw]), r=psk(b3), w=[K_("pT")])
            for half in range(2):
                for k in range(2):
                    P.op("pe", mm(PS[b2 + half][:tw, :], pT[:, k, :tw], Wpp[:, k, half * 512:(half + 1) * 512],
                                  start=(k == 0), stop=(k == 1)), r=[K_("pT")] + wppk, w=psk(b2 + half))
            for half in range(2):
                sl = slice(half * 512, (half + 1) * 512)
                P.op("dve", tt(sgt[:tw, sl], sgt[:tw, sl], PS[b2 + half][:tw, :], ALU.mult),
                     r=[K_("sgt")] + psk(b2 + half), w=[K_("sgt")])
            P.op("dve", tt(xo[:tw, :], x1[:tw, :], sgt[:tw, :], ALU.add), r=[K_("x1"), K_("sgt")], w=[K_("x2")])
            if not last:
                P.dma("sp", xs[to:to + tw, :], xo[:tw, :], r=[K_("x2")], w=[("xs", c)])
                make_h(l + 1, c, xo, K_("x2"), (junk, ssq, hb), P=P, bank=b2, sfx=T_)
            else:
                P.op("act", actf(junk[:tw, :], xo[:tw, :], AF.Square, accum=ssq[:tw, 0:1]), r=[K_("x2")],
                     w=[K_("junk"), K_("ssq")])
                P.op("act", actf(ssq[:tw, 1:2], ssq[:tw, 0:1], AF.Sqrt, scale=1.0 / D, bias=epsc[:tw, 0:1]),
                     r=[K_("ssq"), "epsc"], w=[K_("ssq1")])
                P.op("dve", rcp(ssq[:tw, 2:3], ssq[:tw, 1:2]), r=[K_("ssq1")], w=[K_("ssq2")])
                P.op("dve", stt(x1[:tw, :], xo[:tw, :], ssq[:tw, 2:3], fn[:tw, :], ALU.mult, ALU.mult),
                     r=[K_("x2"), K_("ssq2"), "fn"], w=[K_("x1")])
                P.dma("sp", y_all[to:to + tw, :], x1[:tw, :], r=[K_("x1")], is_output=True)

    P.barrier()
    load_w(0, w_in[0], C_Q, 512)
    load_w(2, w_in[0], C_K, 512)
    load_w(4, w_in[0], C_V, 512)
    load_w(6, w_in[0], C_Z, 512)
    phase0()
    for l in range(depth):
        for hh in range(2):
            P.barrier()
            phaseA(l, hh)
            if stop_after == ("A", l, hh):
                break
        else:
            P.barrier()
            phaseC1(l)
            if stop_after == ("C1", l):
                break
            P.barrier()
            phaseB(l)
            if stop_after == ("B", l):
                break
            P.barrier()
            phaseC2(l, last=(l == depth - 1))
            continue
        break
    if dbg:
        dbg_h = dout("dbg_hT", [128, 8 * TOK])
        dbg_o = dout("dbg_oT", [128, 8 * TOK])
        P.barrier()
        wreset()
        stg = walloc([128, 2048], F32, "stg")
        for name, src, dst in (("h", hT, dbg_h), ("o", oT, dbg_o)):
            flat = src[:].rearrange("p k t -> p (k t)")
            for i in range(0, 8 * TOK, 2048):
                n = min(2048, 8 * TOK - i)
                P.op("dve", cp(stg[:, :n], flat[:, i:i + n]), w=["stg"])
                P.dma("sp", dst[:, i:i + n], stg[:, :n], r=["stg"], is_output=True)
    P.emit()
    return nc


_CACHE = {}


def make_in_maps(inputs):
    f = lambda a: np.ascontiguousarray(np.asarray(a, dtype=np.float32))
    xp, xsm = f(inputs["x_prompt"]), f(inputs["x_sample"])
    pp, psm = f(inputs["p_prompt"]), f(inputs["p_sample"])
    sc, sd, sp = f(inputs["state_conv"]), f(inputs["state_delta"]), f(inputs["state_pool"])
    cf, cb = make_consts()
    shared = {k: f(inputs[k]) for k in ("norm_mix", "w_in", "conv_w", "a_log", "dt_bias", "gdn_norm", "w_proj_a",
                                        "pool_w", "pool_scale", "w_proj_b", "w_out", "norm_ple", "w_ple_gate",
                                        "w_ple_proj", "final_norm")}
    shared["cf"] = cf
    shared["cb"] = cb
    maps = []
    for c in range(8):
        sl = slice(16 * c, 16 * c + 16)
        m = dict(shared)
        m["x_all"] = np.ascontiguousarray(np.concatenate([xp[c], xsm[sl].transpose(1, 0, 2).reshape(ST, D)], 0))
        m["p_all"] = np.ascontiguousarray(np.concatenate(
            [pp[:, c], psm[:, sl].transpose(0, 2, 1, 3).reshape(DEPTH, ST, 256)], 1))
        m["sconv"] = np.ascontiguousarray(sc[:, sl].transpose(0, 2, 1, 3).reshape(DEPTH, 48, 3072))
        m["sdelta"] = np.ascontiguousarray(sd[:, sl])
        m["spool"] = np.ascontiguousarray(sp[:, sl].transpose(0, 2, 1, 3).reshape(DEPTH, 240, D))
        maps.append(m)
    return maps


def kernel(**inputs):
    if "nc" not in _CACHE:
        _CACHE["nc"] = build()
    nc = _CACHE["nc"]
    maps = make_in_maps(inputs)
    res = run_bass_kernel_spmd(nc, maps, core_ids=list(range(8)))
    R = res.results
    y_prompt = np.stack([R[c]["y_all"][:PT] for c in range(8)], 0)
    y_sample = np.concatenate([R[c]["y_all"][PT:].reshape(4, 16, D).transpose(1, 0, 2) for c in range(8)], 0)
    conv_p = np.stack([R[c]["conv_p"] for c in range(8)], 1)
    delta_p = np.stack([R[c]["delta_p"] for c in range(8)], 1)
    pool_p = np.stack([R[c]["pool_p"] for c in range(8)], 1)
    conv_s = np.concatenate([R[c]["conv_s"].transpose(0, 2, 1, 3) for c in range(8)], 1)
    delta_s = np.concatenate([R[c]["delta_s"] for c in range(8)], 1)
    pool_s = np.concatenate([R[c]["pool_s"].transpose(0, 2, 1, 3) for c in range(8)], 1)
    out = (y_prompt, y_sample, conv_p, delta_p, pool_p, conv_s, delta_s, pool_s)
    return tuple(np.ascontiguousarray(o, dtype=np.float32) for o in out)
```

```python
import numpy as np
import concourse.bass as bass
import concourse.mybir as mybir
from concourse.bass_utils import run_bass_kernel_spmd

F32 = mybir.dt.float32
BF16 = mybir.dt.bfloat16
AF = mybir.ActivationFunctionType
ALU = mybir.AluOpType
AX = mybir.AxisListType

D = 1024
DEPTH = 2
NH = 8
PT = 2048
ST = 64
TOK = PT + ST
NT = 17
NSEQ = 16
INW = 8208
POOL_WINDOWS = (2, 4, 8, 16)
EPS = 1e-6
C_Q, C_K, C_V, C_Z, C_A, C_B, C_U, C_GP, C_GA, C_GB = 0, 1024, 2048, 3072, 4096, 4104, 4112, 5136, 6160, 7184

ENGS = ("pe", "act", "dve", "pool", "sp")


class Rec:
    __slots__ = ("eng", "fn", "deps", "signal", "ticket", "dma", "dsem", "dval")

    def __init__(self, eng, fn, dma=False):
        self.eng = eng
        self.fn = fn
        self.deps = []
        self.signal = False
        self.ticket = None
        self.dma = dma
        self.dsem = None
        self.dval = None


class Prog:
    def __init__(self, nc, n_dma_sems=12):
        self.nc = nc
        self.q = {e: [] for e in ENGS}
        self.last_w = {}
        self.readers = {}
        self.n_dma_sems = n_dma_sems
        self.dma_count = {e: 0 for e in ENGS}
        self.dma_hist = {e: [] for e in ENGS}
        self.out_dmas = []

    def _add_dep(self, x, d, kind):
        if d is None or d is x:
            return
        if not d.dma and d.eng == x.eng and not x.dma:
            if x.eng == "pe":
                return
        if d not in x.deps:
            x.deps.append(d)
            if not d.dma:
                d.signal = True

    def op(self, eng, fn, r=(), w=(), dma=False):
        x = Rec(eng, fn, dma)
        for k in r:
            self._add_dep(x, self.last_w.get(k), "RAW")
            if isinstance(k, tuple) and k[0] == "ps":
                for rd in self.readers.get(k, ()):
                    if rd.eng != eng:
                        self._add_dep(x, rd, "RAR")
        for k in w:
            self._add_dep(x, self.last_w.get(k), "WAW")
            for rd in self.readers.get(k, ()):
                self._add_dep(x, rd, "WAR")
        for k in r:
            self.readers.setdefault(k, []).append(x)
        for k in w:
            self.last_w[k] = x
            self.readers[k] = []
        if dma:
            n = self.dma_count[eng]
            self.dma_count[eng] += 1
            x.dsem = n % self.n_dma_sems
            x.dval = 16 * (n // self.n_dma_sems + 1)
            hist = self.dma_hist[eng]
            if n >= self.n_dma_sems:
                x.deps.append(hist[n - self.n_dma_sems])
            hist.append(x)
        self.q[eng].append(x)
        return x

    def dma(self, eng, out, in_, r=(), w=(), is_output=False, **kw):
        x = self.op(eng, lambda e: e.dma_start(out=out, in_=in_, **kw), r=r, w=w, dma=True)
        if is_output:
            self.out_dmas.append(x)
        return x

    def barrier(self):
        lasts = []
        for e in ENGS:
            for x in reversed(self.q[e]):
                if not x.dma and x.fn is not None:
                    lasts.append(x)
                    break
            lasts.extend(self.dma_hist[e][-self.n_dma_sems:])
        for e in ENGS:
            b = Rec(e, None)
            for d in lasts:
                if d.dma or d.eng != e:
                    b.deps.append(d)
                    if not d.dma:
                        d.signal = True
            self.q[e].append(b)
        self.last_w = {}
        self.readers = {}

    def emit(self):
        nc = self.nc
        from contextlib import ExitStack
        with ExitStack() as es:
            esem = {e: es.enter_context(nc.semaphore("sem_" + e)) for e in ENGS}
            dsem = {e: [es.enter_context(nc.semaphore(f"dsem_{e}_{i}")) for i in range(self.n_dma_sems)]
                    for e in ENGS if self.dma_count[e] > 0}
            for e in ENGS:
                c = 0
                for x in self.q[e]:
                    if x.signal and not x.dma:
                        c += 1
                        x.ticket = c
            final = Rec("sp", None)
            final.deps = list(self.out_dmas)
            block = es.enter_context(nc.Block())
            handles = {"pe": block.tensor, "act": block.scalar, "dve": block.vector, "pool": block.gpsimd,
                       "sp": block.sync}

            def run_engine(ename):
                def body(e):
                    seen = {}

                    def do_waits(x):
                        for d in x.deps:
                            if d.dma:
                                key = ("d", d.eng, d.dsem)
                                sem, val = dsem[d.eng][d.dsem], d.dval
                            else:
                                key = ("e", d.eng)
                                sem, val = esem[d.eng], d.ticket
                            if seen.get(key, 0) >= val:
                                continue
                            seen[key] = val
                            e.wait_ge(sem, val)

                    for x in self.q[ename]:
                        do_waits(x)
                        if x.fn is None:
                            continue
                        ins = x.fn(e)
                        if x.dma:
                            ins.then_inc(dsem[ename][x.dsem], 16)
                        elif x.signal:
                            ins.then_inc(esem[ename], 1)
                    if ename == "sp":
                        do_waits(final)
                return body

            for ename in ENGS:
                handles[ename](run_engine(ename))
        return nc


class Recorder:
    def __init__(self):
        self.ops = []

    def op(self, eng, fn, r=(), w=(), dma=False):
        self.ops.append(("op", eng, (fn,), dict(r=r, w=w, dma=dma)))

    def dma(self, eng, out, in_, r=(), w=(), is_output=False, **kw):
        self.ops.append(("dma", eng, (out, in_), dict(r=r, w=w, is_output=is_output, **kw)))


def merge_threads(P, recs, max_run=64):
    idx = [0] * len(recs)
    cur = 0
    n = len(recs)
    while any(idx[i] < len(recs[i].ops) for i in range(n)):
        if idx[cur] >= len(recs[cur].ops):
            cur = (cur + 1) % n
            continue
        ops = recs[cur].ops
        eng = ops[idx[cur]][1]
        cnt = 0
        while idx[cur] < len(ops) and ops[idx[cur]][1] == eng and cnt < max_run:
            kind, e_, a, kw = ops[idx[cur]]
            if kind == "op":
                P.op(e_, *a, **kw)
            else:
                P.dma(e_, *a, **kw)
            idx[cur] += 1
            cnt += 1
        cur = (cur + 1) % n


def mm(out, lhsT, rhs, start=True, stop=True):
    return lambda e: e.matmul(out, lhsT=lhsT, rhs=rhs, start=start, stop=stop)


def trp(out, in_, ident):
    return lambda e: e.transpose(out, in_, ident)


def actf(out, in_, func, scale=None, bias=None, accum=None):
    kw = {}
    if scale is not None:
        kw["scale"] = scale
    if bias is not None:
        kw["bias"] = bias
    if accum is not None:
        kw["accum_out"] = accum
    return lambda e: e.activation(out=out, in_=in_, func=func, **kw)


def tt(out, a, b, op):
    return lambda e: e.tensor_tensor(out=out, in0=a, in1=b, op=op)


def ts(out, a, s1, op0, s2=None, op1=None):
    if op1 is None:
        return lambda e: e.tensor_scalar(out=out, in0=a, scalar1=s1, scalar2=None, op0=op0)
    return lambda e: e.tensor_scalar(out=out, in0=a, scalar1=s1, scalar2=s2, op0=op0, op1=op1)


def stt(out, in0, scalar, in1, op0, op1):
    return lambda e: e.scalar_tensor_tensor(out=out, in0=in0, scalar=scalar, in1=in1, op0=op0, op1=op1)


def cp(out, in_):
    return lambda e: e.tensor_copy(out=out, in_=in_)


def acp(out, in_):
    return lambda e: e.copy(out=out, in_=in_)


def rcp(out, in_):
    return lambda e: e.reciprocal(out=out, in_=in_)


def red(out, in_):
    return lambda e: e.tensor_reduce(out=out, in_=in_, axis=AX.X, op=ALU.add)


def mset(out, v):
    return lambda e: e.memset(out, v)


CF_ID, CF_NM2, CF_MUD, CF_UTRI, CF_ONES = 0, 128, 384, 512, 640
CF_NM2S, CF_MUDS, CF_UTRIS, CF_ONESS, CF_SEGT = 768, 896, 960, 1024, 1088
NCF = 1104
CB_ID, CB_SEG, CB_BC, CB_BP, CB_BF, CB_SH0, CB_SH1, CB_SN = 0, 128, 1152, 1664, 2176, 2688, 2944, 3200
NCB = 3456


def make_consts():
    cf = np.zeros((128, NCF), np.float32)
    cb = np.zeros((128, NCB), np.float32)
    p = np.arange(128)[:, None]
    f = np.arange(128)[None, :]
    cf[:, CF_ID:CF_ID + 128] = np.eye(128)
    cf[:, CF_NM2:CF_NM2 + 128] = -1.0 * (f > p)
    cf[:, CF_NM2 + 128:CF_NM2 + 256] = -1.0 * (p > f)
    cf[:, CF_MUD:CF_MUD + 128] = (f >= p)
    cf[:, CF_UTRI:CF_UTRI + 128] = (p <= f)
    cf[:, CF_ONES:CF_ONES + 128] = 1.0
    ps = np.arange(64)[:, None]
    fs = np.arange(64)[None, :]
    same = (ps % 16) == (fs % 16)
    tp, tf = ps // 16, fs // 16
    cf[:64, CF_NM2S:CF_NM2S + 64] = -1.0 * (same & (tf > tp))
    cf[:64, CF_NM2S + 64:CF_NM2S + 128] = -1.0 * (same & (tp > tf))
    cf[:64, CF_MUDS:CF_MUDS + 64] = (same & (tf >= tp))
    cf[:64, CF_UTRIS:CF_UTRIS + 64] = (same & (tp <= tf))
    cf[:64, CF_ONESS:CF_ONESS + 64] = same
    cf[:64, CF_SEGT:CF_SEGT + 16] = (np.arange(64)[:, None] % 16) == np.arange(16)[None, :]
    cb[:, CB_ID:CB_ID + 128] = np.eye(128)
    seg = ((np.arange(64)[None, :] % 16) == np.arange(16)[:, None]).astype(np.float32)
    cb[:, CB_SEG:CB_SEG + 1024] = seg.reshape(1, 1024)
    for g, W in enumerate(POOL_WINDOWS):
        tq = np.arange(128)[:, None]
        t = np.arange(128)[None, :]
        cur = ((tq > t - W) & (tq <= t)) / W - (tq == t)
        prev = ((tq - 128 > t - W)) / W
        cnt = np.minimum(t + 1, W)
        first = ((tq >= np.maximum(0, t - W + 1)) & (tq <= t)) / cnt - (tq == t)
        cb[:, CB_BC + g * 128:CB_BC + (g + 1) * 128] = cur
        cb[:, CB_BP + g * 128:CB_BP + (g + 1) * 128] = prev
        cb[:, CB_BF + g * 128:CB_BF + (g + 1) * 128] = first
        r = np.arange(240)[:, None]
        j, s1 = r // 16, r % 16
        c = np.arange(64)[None, :]
        tau, s2 = c // 16, c % 16
        sh = ((s1 == s2) & (j >= 16 + tau - W)) / W
        cb[:, CB_SH0 + g * 64:CB_SH0 + (g + 1) * 64] = sh[:128]
        cb[:112, CB_SH1 + g * 64:CB_SH1 + (g + 1) * 64] = sh[128:]
        r = np.arange(64)[:, None]
        tau1, s1 = r // 16, r % 16
        sn = (s1 == s2) * (((tau1 > tau - W) & (tau1 <= tau)) / W - (tau1 == tau))
        cb[:64, CB_SN + g * 64:CB_SN + (g + 1) * 64] = sn
    return cf, cb


A_STAGGER = 0


def build(depth=DEPTH, stop_after=None, dbg=False, dbg_tile=0):
    nc = bass.Bass("TRN2", target_bir_lowering=False)
    P = Prog(nc)

    def din(name, shape):
        return nc.dram_tensor(name, list(shape), F32, kind="ExternalInput").ap()

    def dout(name, shape):
        return nc.dram_tensor(name, list(shape), F32, kind="ExternalOutput").ap()

    x_all = din("x_all", [TOK, D])
    p_all = din("p_all", [DEPTH, TOK, 256])
    sconv = din("sconv", [DEPTH, 48, 3072])
    sdelta = din("sdelta", [DEPTH, NSEQ, NH, 128, 128])
    spool = din("spool", [DEPTH, 240, D])
    norm_mix = din("norm_mix", [DEPTH, D])
    w_in = din("w_in", [DEPTH, D, INW])
    conv_w = din("conv_w", [DEPTH, 4, 3072])
    a_log = din("a_log", [DEPTH, NH])
    dt_bias = din("dt_bias", [DEPTH, NH])
    gdn_norm = din("gdn_norm", [DEPTH, 128])
    w_proj_a = din("w_proj_a", [DEPTH, D, D])
    pool_w = din("pool_w", [DEPTH, 4, 256, 256])
    pool_scale = din("pool_scale", [DEPTH, D])
    w_proj_b = din("w_proj_b", [DEPTH, D, D])
    w_out = din("w_out", [DEPTH, D, D])
    norm_ple = din("norm_ple", [DEPTH, D])
    w_ple_gate = din("w_ple_gate", [DEPTH, D, D])
    w_ple_proj = din("w_ple_proj", [DEPTH, 256, D])
    final_norm = din("final_norm", [D])
    cf_d = din("cf", [128, NCF])
    cb_d = din("cb", [128, NCB])

    y_all = dout("y_all", [TOK, D])
    conv_p = dout("conv_p", [DEPTH, 3, 3072])
    delta_p = dout("delta_p", [DEPTH, NH, 128, 128])
    pool_p = dout("pool_p", [DEPTH, 15, D])
    conv_s = dout("conv_s", [DEPTH, 3, NSEQ, 3072])
    delta_s = dout("delta_s", [DEPTH, NSEQ, NH, 128, 128])
    pool_s = dout("pool_s", [DEPTH, 15, NSEQ, D])
    xs = nc.dram_tensor("xscratch", [TOK, D], F32, kind="Internal").ap()
    dbg_seen = set()

    def DBG(name, ap, keys):
        if not dbg or name in dbg_seen:
            return
        dbg_seen.add(name)
        shp = list(ap.shape)
        d = nc.dram_tensor("dbg_" + name, shp, F32, kind="ExternalOutput").ap()
        P.dma("sp" if ap.dtype == F32 else "pool", d, ap, r=keys, is_output=True)

    total = (nc.sbuf_bytes_remaining // 64) * 64 - 256
    base = nc.bump_sbuf(total)[0]
    cursor = [0]

    def alloc(name, shape, dt, at=None):
        nbytes = int(np.prod(shape[1:])) * (4 if dt == F32 else 2)
        nbytes = (nbytes + 63) // 64 * 64
        if at is None:
            off = cursor[0]
            cursor[0] += nbytes
            assert cursor[0] <= total, (name, cursor[0], total)
        else:
            off = at
        return nc.alloc_sbuf_tensor_at(name, list(shape), dt, offset=base + off)

    hT = alloc("hT", [128, 8, TOK], BF16)
    oT = alloc("oT", [128, 8, TOK], BF16)
    AR_OFF = cursor[0]
    NSLOT = 16
    AR = alloc("AR", [128, NSLOT * 2048], BF16)
    CF = alloc("CF", [128, NCF], F32)
    CB = alloc("CB", [128, NCB], BF16)
    epsc = alloc("epsc", [128, 1], F32)
    onec = alloc("onec", [128, 1], F32)
    colv = [alloc(f"colv{l}", [128, 128], F32) for l in range(DEPTH)]
    dtb = [alloc(f"dtb{l}", [128, 8], F32) for l in range(DEPTH)]
    nexpA = [alloc(f"nexpA{l}", [128, 8], F32) for l in range(DEPTH)]
    W0 = cursor[0]
    uid = [0]

    def walloc(shape, dt, name=None):
        uid[0] += 1
        return alloc(f"{name or 'w'}_{uid[0]}", shape, dt)

    def wreset():
        cursor[0] = W0

    PS = [nc.alloc_psum_tensor(f"ps{b}", [128, 512], F32) for b in range(8)]

    def psk(*banks):
        return [("ps", b) for b in banks]

    def ark(*slots):
        return [("ar", s) for s in slots]

    TW = [128] * 16 + [64]
    TO = [128 * i for i in range(17)]
    identb = CB[:, CB_ID:CB_ID + 128]
    identf = CF[:, CF_ID:CF_ID + 128]

    def wslot(s0, n, ncols):
        return AR[:, s0 * 2048:(s0 + n) * 2048].rearrange("p (k c) -> p k c", k=8)

    def load_w(s0, wmat, col0, ncols):
        n = ncols // 256
        src = wmat.rearrange("(k p) c -> p k c", p=128)
        for i in range(n):
            P.dma("pool", wslot(s0 + i, 1, 256), src[:, :, col0 + i * 256:col0 + (i + 1) * 256], w=ark(s0 + i))

    def load_w_wide(s0, wmat, col0, ncols):
        src = wmat.rearrange("(k p) c -> p k c", p=128)
        for i in range(ncols // 512):
            P.dma("pool", wslot(s0 + 2 * i, 2, 512), src[:, :, col0 + i * 512:col0 + (i + 1) * 512],
                  w=ark(s0 + 2 * i, s0 + 2 * i + 1))

    def wwide(s0, i):
        return wslot(s0 + 2 * i, 2, 512), ark(s0 + 2 * i, s0 + 2 * i + 1)

    def wview(s0, ncols, c0, cw):
        s = s0 + c0 // 256
        cc = c0 % 256
        assert cc + cw <= 256
        return wslot(s, 1, 256)[:, :, cc:cc + cw], ("ar", s)

    P.dma("sp", CF[:], cf_d, w=["CF"])
    P.dma("pool", CB[:], cb_d, w=["CB"])
    P.op("dve", mset(epsc[:], EPS), w=["epsc"])
    P.op("dve", mset(onec[:], 1.0), w=["onec"])
    VR = walloc([128, 128], F32, "VR")
    for l in range(depth):
        P.op("dve", mset(VR[:], 0.0), w=["VR"])
        P.dma("sp", VR[0:96, :], conv_w[l].rearrange("t (j p) -> (t j) p", p=128), w=["VR"])
        P.dma("sp", VR[96:104, :], norm_mix[l].rearrange("(k p) -> k p", p=128), w=["VR"])
        P.dma("sp", VR[104:112, :], norm_ple[l].rearrange("(k p) -> k p", p=128), w=["VR"])
        P.dma("sp", VR[112:120, :], pool_scale[l].rearrange("(k p) -> k p", p=128), w=["VR"])
        P.dma("sp", VR[120:121, :], gdn_norm[l].rearrange("(k p) -> k p", p=128), w=["VR"])
        P.op("pe", mm(PS[0][:, 0:128], VR[:, :], identf), r=["VR", "CF"], w=psk(0))
        P.op("dve", cp(colv[l][:], PS[0][:, 0:128]), r=psk(0), w=[f"colv{l}"])
        P.dma("sp", dtb[l][:], dt_bias[l].partition_broadcast(128), w=[f"dtb{l}"])
        P.dma("sp", nexpA[l][:], a_log[l].partition_broadcast(128), w=[f"nexpA{l}"])
        P.op("act", actf(nexpA[l][:], nexpA[l][:], AF.Exp), r=[f"nexpA{l}"], w=[f"nexpA{l}"])
        P.op("dve", ts(nexpA[l][:], nexpA[l][:], -1.0, ALU.mult), r=[f"nexpA{l}"], w=[f"nexpA{l}"])
    CV_CW, CV_NM, CV_NP, CV_PS, CV_GN = 0, 96, 104, 112, 120

    def make_h(l, c, xt, xkey, bufs, P=P, bank=2, sfx=""):
        tw, to = TW[c], TO[c]
        junk, ssq, hb = bufs
        P.op("act", actf(junk[:tw, :], xt[:tw, :], AF.Square, accum=ssq[:tw, 0:1]), r=[xkey],
             w=["junk" + sfx, "ssq" + sfx])
        P.op("act", actf(ssq[:tw, 1:2], ssq[:tw, 0:1], AF.Sqrt, scale=1.0 / D, bias=epsc[:tw, 0:1]),
             r=["ssq" + sfx, "epsc"], w=["ssq1" + sfx])
        P.op("dve", rcp(ssq[:tw, 2:3], ssq[:tw, 1:2]), r=["ssq1" + sfx], w=["ssq2" + sfx])
        P.op("dve", ts(hb[:tw, :], xt[:tw, :], ssq[:tw, 2:3], ALU.mult), r=[xkey, "ssq2" + sfx], w=["hb" + sfx])
        pst = PS[bank][:].bitcast(BF16).rearrange("p (k t) -> p k t", k=8)
        for k in range(8):
            P.op("pe", trp(pst[:, k, :tw], hb[:tw, k * 128:(k + 1) * 128], identb[:tw, :tw]), r=["hb" + sfx, "CB"],
                 w=psk(bank))
        P.op("dve", tt(hT[:, :, to:to + tw], pst[:, :, :tw],
                       colv[l][:, CV_NM:CV_NM + 8].unsqueeze(2).to_broadcast([128, 8, tw]), ALU.mult),
             r=psk(bank) + [f"colv{l}"], w=[("hT", c)])

    def phase0():
        wreset()
        xts = [walloc([128, D], F32, "xt") for _ in range(2)]
        junk = walloc([128, D], BF16, "junk")
        ssq = walloc([128, 4], F32, "ssq")
        hb = walloc([128, D], BF16, "hb")
        for c in range(NT):
            xt = xts[c % 2]
            P.dma("sp", xt[:TW[c], :], x_all[TO[c]:TO[c] + TW[c], :], w=[f"xt{c % 2}"])
            make_h(0, c, xt, f"xt{c % 2}", (junk, ssq, hb))

    def phaseA(l, hh):
        wreset()
        wl = w_in[l]
        WB = 0 if hh == 0 else 8
        A2B = 8 if hh == 0 else 0
        S_Q, S_K, S_V, S_Z = WB, WB + 2, WB + 4, WB + 6
        if hh == 0:
            pass
        else:
            load_w(15, wl, C_Z + 512 + 256, 256)
        Wab = walloc([128, 8, 16], BF16, "Wab")
        P.dma("pool", Wab[:], wl.rearrange("(k p) c -> p k c", p=128)[:, :, C_A:C_A + 16], w=["Wab"])
        Dg = walloc([128, 12, 4, 128], BF16, "Dg")
        for qkv in range(3):
            cwv = colv[l][:, CV_CW:CV_CW + 96].rearrange("p (t j) -> p j t", t=4)[:, qkv * 8 + 4 * hh:qkv * 8 + 4 * hh + 4, :]
            P.op("dve", tt(Dg[:, qkv * 4:(qkv + 1) * 4, :, :],
                           identf.unsqueeze(1).unsqueeze(1).to_broadcast([128, 4, 4, 128]),
                           cwv.unsqueeze(3).to_broadcast([128, 4, 4, 128]), ALU.mult),
                 r=["CF", f"colv{l}"], w=["Dg"])
        a2 = [AR_OFF + A2B * 4096, AR_OFF + (A2B + 7) * 4096]

        def alloc2(name, shape, dt, always=False):
            nbytes = (int(np.prod(shape[1:])) * (4 if dt == F32 else 2) + 63) // 64 * 64
            i = 1 if always else 0
            off = a2[i]
            a2[i] += nbytes
            lim = AR_OFF + (A2B + 8) * 4096 if always else AR_OFF + (A2B + 7) * 4096
            assert a2[i] <= lim, (name, a2[i], lim)
            uid[0] += 1
            return alloc(f"{name}_{uid[0]}", shape, dt, at=off)

        H4 = 4 * hh
        gall = walloc([128, NT, 4], F32, "gall")
        ball = walloc([128, NT, 4], F32, "ball")
        sc3all = walloc([128, NT, 12], F32, "sc3all")
        eall = walloc([128, NT, 12], F32, "eall")
        negg = walloc([128, NT, 4], F32, "negg")
        psab = PS[0][:, 0:NT * 16].rearrange("p (c j) -> p c j", c=NT)
        for c in range(NT):
            for k in range(8):
                P.op("pe", mm(psab[:TW[c], c, :], hT[:, k, TO[c]:TO[c] + TW[c]], Wab[:, k, :], start=(k == 0),
                              stop=(k == 7)), r=["Wab", ("hT", c)], w=psk(0))
        psg = PS[1][:, 0:NT * 8].rearrange("p (c j) -> p c j", c=NT)
        REG = ((128, slice(0, 16), 16), (64, slice(16, 17), 1))
        for (np_, ts_, nt_) in REG:
            P.op("dve", tt(gall[:np_, ts_, :], psab[:np_, ts_, H4:H4 + 4],
                           dtb[l][:np_, H4:H4 + 4].unsqueeze(1).to_broadcast([np_, nt_, 4]), ALU.add),
                 r=psk(0) + [f"dtb{l}"], w=["gall"])
        for (np_, ts_, nt_) in REG:
            P.op("act", actf(gall[:np_, ts_, :], gall[:np_, ts_, :], AF.Exp), r=["gall"], w=["gall"])
        for (np_, ts_, nt_) in REG:
            P.op("act", actf(gall[:np_, ts_, :], gall[:np_, ts_, :], AF.Ln, bias=onec[:np_, 0:1]),
                 r=["gall", "onec"], w=["gall"])
        for (np_, ts_, nt_) in REG:
            P.op("dve", tt(gall[:np_, ts_, :], gall[:np_, ts_, :],
                           nexpA[l][:np_, H4:H4 + 4].unsqueeze(1).to_broadcast([np_, nt_, 4]), ALU.mult),
                 r=["gall", f"nexpA{l}"], w=["gall"])
        for (np_, ts_, nt_) in REG:
            P.op("act", actf(ball[:np_, ts_, :], psab[:np_, ts_, 8 + H4:8 + H4 + 4], AF.Exp, scale=-1.0), r=psk(0),
                 w=["ball"])
        for (np_, ts_, nt_) in REG:
            P.op("dve", ts(ball[:np_, ts_, :], ball[:np_, ts_, :], 1.0, ALU.add), r=["ball"], w=["ball"])
        for (np_, ts_, nt_) in REG:
            P.op("dve", rcp(ball[:np_, ts_, :], ball[:np_, ts_, :]), r=["ball"], w=["ball"])
        for c in range(NT):
            tw_ = TW[c]
            utri_ = CF[:tw_, CF_UTRIS:CF_UTRIS + tw_] if c == 16 else CF[:tw_, CF_UTRI:CF_UTRI + tw_]
            ones__ = CF[:tw_, CF_ONESS:CF_ONESS + tw_] if c == 16 else CF[:tw_, CF_ONES:CF_ONES + tw_]
            P.op("pe", mm(psg[:tw_, c, 0:4], utri_, gall[:tw_, c, :]), r=["CF", "gall"], w=psk(1))
            P.op("pe", mm(psg[:tw_, c, 4:8], ones__, gall[:tw_, c, :]), r=["CF", "gall"], w=psk(1))
        for (np_, ts_, nt_) in REG:
            P.op("dve", cp(sc3all[:np_, ts_, 0:4], psg[:np_, ts_, 0:4]), r=psk(1), w=["sc3all"])
        for (np_, ts_, nt_) in REG:
            P.op("dve", tt(sc3all[:np_, ts_, 4:8], psg[:np_, ts_, 4:8], sc3all[:np_, ts_, 0:4], ALU.subtract),
                 r=psk(1) + ["sc3all"], w=["sc3all"])
        for (np_, ts_, nt_) in REG:
            P.op("dve", cp(sc3all[:np_, ts_, 8:12], psg[:np_, ts_, 4:8]), r=psk(1), w=["sc3all"])
        for (np_, ts_, nt_) in REG:
            P.op("act", actf(eall[:np_, ts_, :], sc3all[:np_, ts_, :], AF.Exp), r=["sc3all"], w=["eall"])
        for (np_, ts_, nt_) in REG:
            P.op("dve", ts(negg[:np_, ts_, :], sc3all[:np_, ts_, 0:4], -1.0, ALU.mult), r=["sc3all"], w=["negg"])
        tabs = (gall, ball, eall, negg)

        recs = [Recorder() for _ in range(2)]
        for th in range(2):
            for _ in phaseA_thread(recs[th], l, hh, th, Wab, Dg, (S_Q, S_K, S_V, S_Z), alloc2, tabs):
                pass
        merge_threads(P, recs, max_run=12)
        smp_keys = [n + f"_{t}" for t in range(2)
                    for n in ("full", "sc16", "SS", "SSb", "kmask", "qmask", "uexp", "eGlS", "gexp")]

        def prefetch(s0, wmat, col0, ncols):
            n = ncols // 256
            src = wmat.rearrange("(k p) c -> p k c", p=128)
            for i in range(n):
                P.dma("pool", wslot(s0 + i, 1, 256), src[:, :, col0 + i * 256:col0 + (i + 1) * 256],
                      w=ark(s0 + i) + smp_keys)

        if hh == 0:
            prefetch(8, wl, C_Q + 512, 512)
            prefetch(10, wl, C_K + 512, 512)
            prefetch(12, wl, C_V + 512, 512)
            prefetch(14, wl, C_Z + 512, 256)
        else:
            prefetch(0, w_in[l], C_GA, 1024)
            prefetch(4, w_proj_a[l], 0, 768)

    def phaseA_thread(P, l, hh, th, Wab, Dg, slots, alloc2, tabs):
        gall, ball, eall, negg = tabs
        NHT = 2
        HW = NHT * 128
        H0 = 4 * hh + NHT * th
        S_Q, S_K, S_V, S_Z = slots
        T_ = f"_{th}"
        B = [4 * th + i for i in range(4)]
        b0, b1, b2, b3 = B

        def K_(name):
            return name + T_

        xpre = [walloc([128, 6, 131], BF16, "xpre") for _ in range(2)]
        qkvs = walloc([128, 768], F32, "qkvs")
        F4a = walloc([128, 512], F32, "F4a")
        QN = walloc([128, HW], BF16, "QN")
        KN = walloc([128, HW], BF16, "KN")
        KB = walloc([128, HW], BF16, "KB")
        KT = walloc([128, HW], BF16, "KT")
        VB = walloc([128, HW], BF16, "VB")
        QKT = walloc([128, 6, 128], BF16, "QKT")
        E = walloc([128, NHT, 128], F32, "E")
        EM2 = walloc([128, NHT, 2, 128], BF16, "EM2")
        EMd = walloc([128, NHT, 128], BF16, "EMd")
        X = [walloc([128, NHT, 2, 128], BF16, "X") for _ in range(2)]
        R = [walloc([128, NHT, 128], BF16, "R") for _ in range(2)]
        qkmT = walloc([128, NHT, 128], BF16, "qkmT")
        Sst = walloc([128, NHT, 128], F32, "S")
        Sb = walloc([128, NHT, 128], BF16, "Sb")
        rhs2 = walloc([128, HW], BF16, "rhs2")
        ubf = walloc([128, HW], BF16, "ubf")
        tmp2 = walloc([128, HW], F32, "tmp2")
        ofp = walloc([128, HW], F32, "o")
        zs = walloc([128, HW], F32, "zs")
        ofb = walloc([128, HW], BF16, "of")
        sc = walloc([128, 96], F32, "sc")
        full = alloc2("full", [128, 6, 112], BF16)
        sc16 = alloc2("sc16", [128, 768], BF16)
        SS = alloc2("SS", [128, 4, NHT, 128], F32)
        SSb = alloc2("SSb", [128, 4, NHT, 128], BF16)
        kmask = alloc2("kmask", [128, NHT, 4, 64], BF16)
        qmask = alloc2("qmask", [128, NHT, 4, 64], BF16)
        uexp = alloc2("uexp", [128, NHT, 4, 128], BF16)
        eGlS = alloc2("eGlS", [128, 16 * NHT], F32)
        gexp = alloc2("gexp", [128, 16 * NHT], F32)
        NTs = alloc2("NTs", [128, NHT, 128], BF16, always=True)
        rbf = alloc2("rbf", [128, HW], BF16, always=True)
        qse = alloc2("qse", [128, HW], F32, always=True)

        APRE, EA, SP_, G4, EB, BETA = 0, 4, 8, 12, 16, 20
        SC3, EALL, SSQ8, RS8, CQ, CKB, CKT, NCBG, SSO, RSO, NEGG = 24, 36, 48, 56, 64, 68, 72, 76, 80, 84, 88

        P.op("dve", mset(Sst[:], 0.0), w=[K_("S")])
        P.op("dve", mset(Sb[:], 0.0), w=[K_("Sb")])
        P.op("dve", mset(xpre[0][:, :, 0:3], 0.0), w=[K_("xpre0")])

        wq = wview(S_Q, 512, th * 256, 256)
        wk = wview(S_K, 512, th * 256, 256)
        wv = wview(S_V, 512, th * 256, 256)
        wz = wview(S_Z, 512, th * 256, 256)
        wqkv = [wq, wk, wv]

        def jg(j):
            return (j // 2) * 4 + 2 * th + (j % 2)

        for qkv in range(3):
            c0 = qkv * 1024 + hh * 512 + th * 256
            P.dma("pool", sc16[0:48, qkv * 256:(qkv + 1) * 256], sconv[l][:, c0:c0 + 256], w=[K_("sc16")])
        pst3 = PS[b3][:].bitcast(BF16)[:, 0:6 * 48].rearrange("p (j t) -> p j t", j=6)
        for j in range(6):
            P.op("pe", trp(pst3[:, j, :], sc16[0:48, j * 128:(j + 1) * 128], identb[0:48, 0:48]),
                 r=[K_("sc16"), "CB"], w=psk(b3))
        P.op("act", acp(out=full[:, :, 0:48], in_=pst3), r=psk(b3), w=[K_("full")])
        yield

        def psa(j, tw):
            return PS[b0][:, j * 128:j * 128 + tw] if j < 4 else PS[b1][:, (j - 4) * 128:(j - 4) * 128 + tw]

        def psb_bank(j):
            return b2 if j < 4 else b3

        def emit_A1(c_):
            tw_, to_ = TW[c_], TO[c_]
            for j in range(6):
                wvw, wkey = wqkv[j // 2]
                for k in range(8):
                    P.op("pe", mm(psa(j, tw_), wvw[:, k, (j % 2) * 128:(j % 2 + 1) * 128], hT[:, k, to_:to_ + tw_],
                                  start=(k == 0), stop=(k == 7)), r=[wkey, ("hT", c_)], w=psk(b0 if j < 4 else b1))

        def emit_E(c_):
            tw_ = TW[c_]
            utri_ = CF[:tw_, CF_UTRIS:CF_UTRIS + tw_] if c_ == 16 else CF[:tw_, CF_UTRI:CF_UTRI + tw_]
            psgr = PS[b3][:, 256:512].rearrange("p (h f) -> p h f", h=NHT)
            for h in range(NHT):
                P.op("pe", mm(psgr[:tw_, h, :tw_],
                              gall[:tw_, c_, 2 * th + h:2 * th + h + 1].to_broadcast([tw_, tw_]), utri_),
                     r=["gall", "CF"], w=psk(b3))
            for h in range(NHT):
                P.op("act", actf(E[:tw_, h, :tw_], psgr[:tw_, h, :tw_], AF.Abs,
                                 bias=negg[:tw_, c_, 2 * th + h:2 * th + h + 1]), r=psk(b3) + ["negg"], w=[K_("E")])
            P.op("act", actf(E[:tw_, :, :tw_], E[:tw_, :, :tw_], AF.Exp, scale=-1.0), r=[K_("E")], w=[K_("E")])

        order = [16] + list(range(16))
        emit_A1(order[0])
        emit_E(order[0])
        for oi, c in enumerate(order):
            tw, to = TW[c], TO[c]
            smp = (c == 16)
            xk = K_(f"xpre{c % 2}")
            xcur, xprev = xpre[c % 2], xpre[(c + 1) % 2]
            hkey = ("hT", c)
            yield
            dstx = full if smp else xcur
            dkey = K_("full") if smp else xk
            c_off = 48 if smp else 3
            P.op("act", acp(out=dstx[:, 0:4, c_off:c_off + tw],
                            in_=PS[b0][:].rearrange("p (j t) -> p j t", j=4)[:, :, :tw]), r=psk(b0), w=[dkey])
            P.op("act", acp(out=dstx[:, 4:6, c_off:c_off + tw],
                            in_=PS[b1][:, 0:256].rearrange("p (j t) -> p j t", j=2)[:, :, :tw]), r=psk(b1), w=[dkey])
            if not smp and c > 0:
                P.op("dve", cp(xcur[:, :, 0:3], xprev[:, :, 128:131]), r=[K_(f"xpre{(c + 1) % 2}")], w=[xk])
            XS, shift, xskey = (full, 16, K_("full")) if smp else (xcur, 1, xk)
            for j in range(6):
                dst = PS[b2][:tw, j * 128:(j + 1) * 128] if j < 4 else PS[b3][:tw, (j - 4) * 128:(j - 3) * 128]
                for tap in range(4):
                    P.op("pe", mm(dst, XS[:, j, tap * shift:tap * shift + tw], Dg[:, jg(j), tap, :],
                                  start=(tap == 0), stop=(tap == 3)), r=[xskey, "Dg"], w=psk(psb_bank(j)))
            yield
            wvz, kz = wz
            for k in range(8):
                P.op("pe", mm(PS[b3][:tw, 256:512], hT[:, k, to:to + tw], wvz[:, k, :], start=(k == 0), stop=(k == 7)),
                     r=[hkey, kz], w=psk(b3))
            P.op("act", actf(qkvs[:tw, 0:512], PS[b2][:tw, :], AF.Silu), r=psk(b2), w=[K_("qkvs")])
            P.op("act", actf(qkvs[:tw, 512:768], PS[b3][:tw, 0:256], AF.Silu), r=psk(b3), w=[K_("qkvs")])
            P.op("act", actf(zs[:tw, :], PS[b3][:tw, 256:512], AF.Silu), r=psk(b3), w=[K_("zs")])
            N2 = NHT
            yield
            yield
            eG = eall[:tw, c, 2 * th:2 * th + 2]
            eGlG = eall[:tw, c, 4 + 2 * th:4 + 2 * th + 2]
            beta2 = ball[:tw, c, 2 * th:2 * th + 2]
            P.op("dve", tt(F4a[:tw, :], qkvs[:tw, 0:512], qkvs[:tw, 0:512], ALU.mult), r=[K_("qkvs")], w=[K_("F4a")])
            P.op("dve", red(sc[:tw, SSQ8:SSQ8 + 4], F4a[:tw, :].rearrange("p (h d) -> p h d", h=4)), r=[K_("F4a")],
                 w=[K_("sc_ssq")])
            P.op("act", actf(sc[:tw, SSQ8:SSQ8 + 4], sc[:tw, SSQ8:SSQ8 + 4], AF.Sqrt, bias=epsc[:tw, 0:1]),
                 r=[K_("sc_ssq"), "epsc"], w=[K_("sc_ssq")])
            P.op("dve", rcp(sc[:tw, RS8:RS8 + 4], sc[:tw, SSQ8:SSQ8 + 4]), r=[K_("sc_ssq")], w=[K_("sc_rs")])
            RSK = RS8 + N2
            P.op("dve", ts(sc[:tw, CQ:CQ + N2], sc[:tw, RS8:RS8 + N2], 128.0 ** -0.5, ALU.mult), r=[K_("sc_rs")],
                 w=[K_("sc_cq")])
            P.op("dve", tt(sc[:tw, CKB:CKB + N2], sc[:tw, RSK:RSK + N2], beta2, ALU.mult),
                 r=[K_("sc_rs"), "ball"], w=[K_("sc_ckb")])
            P.op("dve", tt(sc[:tw, CKT:CKT + N2], sc[:tw, RSK:RSK + N2], eGlG, ALU.mult),
                 r=[K_("sc_rs"), "eall"], w=[K_("sc_ckt")])
            P.op("dve", stt(sc[:tw, NCBG:NCBG + N2], beta2, -1.0, eG, ALU.mult, ALU.mult),
                 r=["ball", "eall"], w=[K_("sc_ncbg")])

            def bcs(ap2):
                return ap2.unsqueeze(2).to_broadcast([tw, N2, 128])

            def bc(col):
                return sc[:tw, col:col + N2].unsqueeze(2).to_broadcast([tw, N2, 128])

            def v3(t_, c0=0):
                return t_[:tw, c0:c0 + HW].rearrange("p (h d) -> p h d", h=N2)

            P.op("dve", tt(v3(QN), v3(qkvs, 0), bc(CQ), ALU.mult), r=[K_("qkvs"), K_("sc_cq")], w=[K_("QN")])
            P.op("dve", tt(v3(KN), v3(qkvs, 256), bc(RSK), ALU.mult), r=[K_("qkvs"), K_("sc_rs")], w=[K_("KN")])
            P.op("dve", tt(v3(KB), v3(qkvs, 256), bc(CKB), ALU.mult), r=[K_("qkvs"), K_("sc_ckb")], w=[K_("KB")])
            P.op("dve", tt(v3(KT), v3(qkvs, 256), bc(CKT), ALU.mult), r=[K_("qkvs"), K_("sc_ckt")], w=[K_("KT")])
            P.op("dve", tt(v3(VB), v3(qkvs, 512), bcs(beta2), ALU.mult), r=[K_("qkvs"), "ball"], w=[K_("VB")])
            yield
            pst0 = PS[b0][:].bitcast(BF16).rearrange("p (j t) -> p j t", j=8)
            for i, (src, skey) in enumerate(((QN, "QN"), (KN, "KN"), (KB, "KB"))):
                for h in range(N2):
                    P.op("pe", trp(pst0[:, i * 2 + h, :tw], src[:tw, h * 128:(h + 1) * 128], identb[:tw, :tw]),
                         r=[K_(skey), "CB"], w=psk(b0))
            P.op("act", acp(out=QKT[:, :, :tw], in_=pst0[:, 0:6, :tw]), r=psk(b0), w=[K_("QKT")])
            yield
            psK = PS[b2][:].rearrange("p (h c f) -> p h c f", h=2, c=2)
            psQ = PS[b3][:, 0:256].rearrange("p (h f) -> p h f", h=N2)
            for h in range(N2):
                P.op("pe", mm(psK[:tw, h, 0, :tw], QKT[:, 2 + h, :tw], QKT[:, 4 + h, :tw]), r=[K_("QKT")], w=psk(b2))
                P.op("pe", mm(psK[:tw, h, 1, :tw], QKT[:, 4 + h, :tw], QKT[:, 2 + h, :tw]), r=[K_("QKT")], w=psk(b2))
                P.op("pe", mm(psQ[:tw, h, :tw], QKT[:, 2 + h, :tw], QKT[:, 0 + h, :tw]), r=[K_("QKT")], w=psk(b3))
            if smp:
                nm2 = CF[:tw, CF_NM2S:CF_NM2S + 128].rearrange("p (c f) -> p c f", c=2)
                mud = CF[:tw, CF_MUDS:CF_MUDS + 64]
            else:
                nm2 = CF[:tw, CF_NM2:CF_NM2 + 256].rearrange("p (c f) -> p c f", c=2)
                mud = CF[:tw, CF_MUD:CF_MUD + 128]
            P.op("dve", tt(EM2[:tw, :, :, :tw], E[:tw, :, :tw].unsqueeze(2).to_broadcast([tw, N2, 2, tw]),
                           nm2.unsqueeze(1).to_broadcast([tw, N2, 2, tw]), ALU.mult), r=[K_("E"), "CF"], w=[K_("EM2")])
            P.op("dve", tt(EMd[:tw, :, :tw], E[:tw, :, :tw], mud.unsqueeze(1).to_broadcast([tw, N2, tw]), ALU.mult),
                 r=[K_("E"), "CF"], w=[K_("EMd")])
            P.op("dve", tt(X[0][:tw, :, :, :tw], psK[:tw, :, :, :tw], EM2[:tw, :, :, :tw], ALU.mult),
                 r=psk(b2) + [K_("EM2")], w=[K_("X0"), K_("X0") + "b"])
            P.op("dve", tt(qkmT[:tw, :, :tw], psQ[:tw, :, :tw], EMd[:tw, :, :tw], ALU.mult), r=psk(b3) + [K_("EMd")],
                 w=[K_("qkmT")])
            P.op("act", acp(out=NTs[:tw, :, :tw], in_=X[0][:tw, :, 0, :tw]), r=[K_("X0")], w=[K_("NTs")])
            P.op("dve", tt(R[0][:tw, :, :tw], X[0][:tw, :, 0, :tw],
                           identb[:tw, :tw].unsqueeze(1).to_broadcast([tw, N2, tw]), ALU.add), r=[K_("X0"), "CB"],
                 w=[K_("R0")])
            yield
            L = 1 if smp else 6
            psKT = PS[b2][:, 0:256].rearrange("p (h f) -> p h f", h=N2)
            psKN = PS[b1][:, 256:512].rearrange("p (h f) -> p h f", h=N2)
            for lev in range(1, L + 1):
                xo, xn = X[(lev - 1) % 2], X[lev % 2]
                xok, xnk = K_(f"X{(lev - 1) % 2}"), K_(f"X{lev % 2}")
                ro, rn = R[(lev - 1) % 2], R[lev % 2]
                rok, rnk = K_(f"R{(lev - 1) % 2}"), K_(f"R{lev % 2}")
                last = (lev == L)
                for h in range(N2):
                    P.op("pe", mm(psKN[:tw, h, :tw], xo[:tw, h, 0, :tw], xo[:tw, h, 1, :tw]), r=[xok, xok + "b"],
                         w=psk(b1))
                if not last:
                    for h in range(N2):
                        P.op("pe", mm(psKT[:tw, h, :tw], xo[:tw, h, 1, :tw], xo[:tw, h, 0, :tw]), r=[xok, xok + "b"],
                             w=psk(b2))
                P.op("act", acp(out=xn[:tw, :, 1, :tw], in_=psKN[:tw, :, :tw]), r=psk(b1), w=[xnk + "b"])
                if not last:
                    P.op("dve", cp(xn[:tw, :, 0, :tw], psKT[:tw, :, :tw]), r=psk(b2), w=[xnk])
                for h in range(N2):
                    P.op("pe", mm(psQ[:tw, h, :tw], xn[:tw, h, 1, :tw], ro[:tw, h, :tw]), r=[xnk + "b", rok],
                         w=psk(b3))
                P.op("dve", tt(rn[:tw, :, :tw], ro[:tw, :, :tw], psQ[:tw, :, :tw], ALU.add), r=[rok] + psk(b3),
                     w=[rnk])
                yield
            TTm, ttk = R[L % 2], K_(f"R{L % 2}")
            ps_kS = PS[b0][:, 0:256].rearrange("p (h v) -> p h v", h=N2)
            ps_qS = PS[b0][:, 256:512].rearrange("p (h v) -> p h v", h=N2)
            ps_u = PS[b1][:, 0:256].rearrange("p (h v) -> p h v", h=N2)
            ps_au = PS[b1][:, 256:512].rearrange("p (h v) -> p h v", h=N2)
            ps_o2 = PS[b3][:, 256:512].rearrange("p (h v) -> p h v", h=N2)
            ps_ds = PS[b2][:, 0:256].rearrange("p (h v) -> p h v", h=N2)
            if not smp:
                for h in range(N2):
                    P.op("pe", mm(ps_kS[:tw, h, :], QKT[:, 2 + h, :tw], Sb[:, h, :]), r=[K_("QKT"), K_("Sb")],
                         w=psk(b0))
                    P.op("pe", mm(ps_qS[:tw, h, :], QKT[:, 0 + h, :tw], Sb[:, h, :]), r=[K_("QKT"), K_("Sb")],
                         w=psk(b0))
                P.op("dve", tt(v3(tmp2), ps_kS[:tw, :, :], bc(NCBG), ALU.mult), r=psk(b0) + [K_("sc_ncbg")],
                     w=[K_("tmp2")])
                P.op("dve", tt(v3(qse), ps_qS[:tw, :, :], bcs(eG), ALU.mult), r=psk(b0) + ["eall"],
                     w=[K_("qse")])
            else:
                segm = CB[:, CB_SEG:CB_SEG + 1024].rearrange("p (s t) -> p s t", s=16)
                P.op("dve", tt(gexp[:tw, :].rearrange("p (s h) -> p s h", s=16),
                               gall[:tw, c, 2 * th:2 * th + 2].unsqueeze(1).to_broadcast([tw, 16, N2]),
                               CF[:tw, CF_SEGT:CF_SEGT + 16].unsqueeze(2).to_broadcast([tw, 16, N2]), ALU.mult),
                     r=["gall", "CF"], w=[K_("gexp")])
                P.op("pe", mm(PS[b1][:, 0:16 * N2], CF[:tw, CF_ONES:CF_ONES + 128], gexp[:tw, :]),
                     r=["CF", K_("gexp")], w=psk(b1))
                P.op("act", actf(eGlS[:, :], PS[b1][:, 0:16 * N2], AF.Exp), r=psk(b1), w=[K_("eGlS")])
                for g in range(4):
                    for s_ in range(4):
                        P.dma("sp", SS[:, s_], sdelta[l, 4 * g + s_, H0:H0 + N2].rearrange("h k v -> k h v"),
                              w=[K_("SS")])
                    P.op("act", acp(out=SSb[:], in_=SS[:]), r=[K_("SS")], w=[K_("SSb")])
                    P.op("dve", tt(kmask[:], QKT[:, 2:4, :64].unsqueeze(2).to_broadcast([128, N2, 4, 64]),
                                   segm[:, 4 * g:4 * g + 4, :].unsqueeze(1).to_broadcast([128, N2, 4, 64]), ALU.mult),
                         r=[K_("QKT"), "CB"], w=[K_("kmask")])
                    P.op("dve", tt(qmask[:], QKT[:, 0:2, :64].unsqueeze(2).to_broadcast([128, N2, 4, 64]),
                                   segm[:, 4 * g:4 * g + 4, :].unsqueeze(1).to_broadcast([128, N2, 4, 64]), ALU.mult),
                         r=[K_("QKT"), "CB"], w=[K_("qmask")])
                    for s in range(4):
                        for h in range(N2):
                            first = (g == 0 and s == 0)
                            lastm = (g == 3 and s == 3)
                            P.op("pe", mm(PS[B[h]][:tw, 0:128], kmask[:, h, s, :], SSb[:, s, h, :], start=first,
                                          stop=lastm), r=[K_("kmask"), K_("SSb")], w=psk(B[h]))
                            P.op("pe", mm(PS[B[2 + h]][:tw, 0:128], qmask[:, h, s, :], SSb[:, s, h, :], start=first,
                                          stop=lastm), r=[K_("qmask"), K_("SSb")], w=psk(B[2 + h]))
                    yield
                for h in range(N2):
                    P.op("dve", ts(tmp2[:tw, h * 128:(h + 1) * 128], PS[B[h]][:tw, 0:128],
                                   sc[:tw, NCBG + h:NCBG + h + 1], ALU.mult), r=psk(B[h]) + [K_("sc_ncbg")],
                         w=[K_("tmp2")])
                    P.op("dve", ts(qse[:tw, h * 128:(h + 1) * 128], PS[B[2 + h]][:tw, 0:128],
                                   eall[:tw, c, 2 * th + h:2 * th + h + 1], ALU.mult), r=psk(B[2 + h]) + ["eall"],
                         w=[K_("qse")])
            P.op("dve", tt(rhs2[:tw, :], tmp2[:tw, :], VB[:tw, :], ALU.add), r=[K_("tmp2"), K_("VB")], w=[K_("rhs2")])
            for h in range(N2):
                P.op("pe", mm(ps_u[:tw, h, :], TTm[:tw, h, :tw], rhs2[:tw, h * 128:(h + 1) * 128]),
                     r=[ttk, K_("rhs2")], w=psk(b1))
            P.op("act", acp(out=ubf[:tw, :], in_=PS[b1][:tw, 0:256]), r=psk(b1), w=[K_("ubf")])
            yield
            for h in range(N2):
                P.op("pe", mm(ps_au[:tw, h, :], NTs[:tw, h, :tw], ubf[:tw, h * 128:(h + 1) * 128]),
                     r=[K_("NTs"), K_("ubf")], w=psk(b1))
            P.op("dve", tt(tmp2[:tw, :], rhs2[:tw, :], ubf[:tw, :], ALU.subtract), r=[K_("rhs2"), K_("ubf")],
                 w=[K_("tmp2")])
            P.op("dve", tt(rbf[:tw, :], tmp2[:tw, :], PS[b1][:tw, 256:512], ALU.add), r=[K_("tmp2")] + psk(b1),
                 w=[K_("rbf")])
            for h in range(N2):
                P.op("pe", mm(ps_u[:tw, h, :], TTm[:tw, h, :tw], rbf[:tw, h * 128:(h + 1) * 128]), r=[ttk, K_("rbf")],
                     w=psk(b1))
            P.op("dve", tt(ubf[:tw, :], ubf[:tw, :], PS[b1][:tw, 0:256], ALU.add), r=[K_("ubf")] + psk(b1),
                 w=[K_("ubf")])
            yield
            for h in range(N2):
                P.op("pe", mm(ps_o2[:tw, h, :], qkmT[:tw, h, :tw], ubf[:tw, h * 128:(h + 1) * 128]),
                     r=[K_("qkmT"), K_("ubf")], w=psk(b3))
            P.op("dve", tt(ofp[:tw, :], qse[:tw, :], PS[b3][:tw, 256:512], ALU.add), r=[K_("qse")] + psk(b3),
                 w=[K_("o")])
            if not smp:
                for h in range(N2):
                    P.op("pe", mm(ps_ds[:, h, :], KT[:tw, h * 128:(h + 1) * 128], ubf[:tw, h * 128:(h + 1) * 128]),
                         r=[K_("KT"), K_("ubf")], w=psk(b2))
                for h in range(N2):
                    P.op("dve", stt(Sst[:, h, :], Sst[:, h, :], eall[:, c, 8 + 2 * th + h:8 + 2 * th + h + 1],
                                    ps_ds[:, h, :], ALU.mult, ALU.add), r=[K_("S"), "eall"] + psk(b2), w=[K_("S")])
                P.op("act", acp(out=Sb[:], in_=Sst[:]), r=[K_("S")], w=[K_("Sb")])
                if c == 15:
                    P.dma("sp", delta_p[l, H0:H0 + N2].rearrange("h k v -> k h v"), Sst[:], r=[K_("S")],
                          is_output=True)
            else:
                segT = CF[:tw, CF_SEGT:CF_SEGT + 16]
                for g in range(4):
                    for s_ in range(4):
                        P.dma("sp", SS[:, s_], sdelta[l, 4 * g + s_, H0:H0 + N2].rearrange("h k v -> k h v"),
                              w=[K_("SS")])
                    P.op("dve", tt(uexp[:tw, :, :, :],
                                   ubf[:tw, :].rearrange("p (h v) -> p h v", h=N2).unsqueeze(2).to_broadcast([tw, N2, 4, 128]),
                                   segT[:, 4 * g:4 * g + 4].unsqueeze(1).unsqueeze(3).to_broadcast([tw, N2, 4, 128]),
                                   ALU.mult), r=[K_("ubf"), "CF"], w=[K_("uexp")])
                    for h in range(N2):
                        bank = B[2 + h]
                        psd = PS[bank][:].rearrange("p (s v) -> p s v", s=4)
                        P.op("pe", mm(PS[bank][:, :], KT[:tw, h * 128:(h + 1) * 128], uexp[:tw, h, :, :]),
                             r=[K_("KT"), K_("uexp")], w=psk(bank))
                        egl = eGlS[:, :].rearrange("p (s h) -> p s h", s=16)[:, 4 * g:4 * g + 4, h]
                        P.op("dve", tt(SS[:, :, h, :], SS[:, :, h, :], egl.unsqueeze(2).to_broadcast([128, 4, 128]),
                                       ALU.mult), r=[K_("SS"), K_("eGlS")], w=[K_("SS")])
                        P.op("dve", tt(SS[:, :, h, :], SS[:, :, h, :], psd, ALU.add), r=[K_("SS")] + psk(bank),
                             w=[K_("SS")])
                    for s_ in range(4):
                        P.dma("sp", delta_s[l, 4 * g + s_, H0:H0 + N2].rearrange("h k v -> k h v"), SS[:, s_],
                              r=[K_("SS")], is_output=True)
                    yield
            yield
            if oi + 1 < len(order):
                emit_A1(order[oi + 1])
            P.op("dve", tt(F4a[:tw, 0:HW], ofp[:tw, :], ofp[:tw, :], ALU.mult), r=[K_("o")], w=[K_("F4a")])
            P.op("dve", red(sc[:tw, SSO:SSO + N2], F4a[:tw, 0:HW].rearrange("p (h d) -> p h d", h=N2)),
                 r=[K_("F4a")], w=[K_("sc_sso")])
            P.op("act", actf(sc[:tw, SSO:SSO + N2], sc[:tw, SSO:SSO + N2], AF.Sqrt, scale=1.0 / 128,
                             bias=epsc[:tw, 0:1]), r=[K_("sc_sso"), "epsc"], w=[K_("sc_sso")])
            P.op("dve", rcp(sc[:tw, RSO:RSO + N2], sc[:tw, SSO:SSO + N2]), r=[K_("sc_sso")], w=[K_("sc_rso")])
            if oi + 1 < len(order):
                emit_E(order[oi + 1])
            P.op("dve", tt(v3(tmp2), v3(ofp), bc(RSO), ALU.mult), r=[K_("o"), K_("sc_rso")], w=[K_("tmp2")])
            P.op("dve", tt(ofb[:tw, :], tmp2[:tw, :], zs[:tw, :], ALU.mult), r=[K_("tmp2"), K_("zs")], w=[K_("of")])
            pso = PS[b2][:].bitcast(BF16).rearrange("p (j t) -> p j t", j=8)
            for h in range(N2):
                P.op("pe", trp(pso[:, h, :tw], ofb[:tw, h * 128:(h + 1) * 128], identb[:tw, :tw]),
                     r=[K_("of"), "CB"], w=psk(b2))
            P.op("act", actf(oT[:, H0:H0 + N2, to:to + tw], pso[:, 0:N2, :tw], AF.Copy,
                             scale=colv[l][:, CV_GN:CV_GN + 1]), r=psk(b2) + [f"colv{l}"], w=[("oT", c, hh, th)])
            yield
            if c == 15 or smp:
                lo = to + 125 if c == 15 else to
                n = 3 if c == 15 else 64
                for qkv in range(3):
                    wvw, wkey = wqkv[qkv]
                    dst = PS[b2][:n, qkv * 256:(qkv + 1) * 256] if qkv < 2 else PS[b3][:n, 0:256]
                    for k in range(8):
                        P.op("pe", mm(dst, hT[:, k, lo:lo + n], wvw[:, k, :], start=(k == 0), stop=(k == 7)),
                             r=[hkey, wkey], w=psk(b2 if qkv < 2 else b3))
                P.op("dve", cp(qkvs[:n, 0:512], PS[b2][:n, :]), r=psk(b2), w=[K_("qkvs")])
                P.op("dve", cp(qkvs[:n, 512:768], PS[b3][:n, 0:256]), r=psk(b3), w=[K_("qkvs")])
                for qkv in range(3):
                    c0 = qkv * 1024 + hh * 512 + th * 256
                    if c == 15:
                        P.dma("sp", conv_p[l, :, c0:c0 + 256], qkvs[0:3, qkv * 256:(qkv + 1) * 256], r=[K_("qkvs")],
                              is_output=True)
                    else:
                        P.dma("sp", conv_s[l, :, :, c0:c0 + 256].rearrange("j s c -> (j s) c"),
                              qkvs[16:64, qkv * 256:(qkv + 1) * 256], r=[K_("qkvs")], is_output=True)
                yield

    GROUPS = [(0, 512), (512, 512), (1024, 512), (1536, 512), (2048, 64)]

    def gkeys(name, g0, n):
        return [(name, c) for c in range(g0 // 128, (g0 + n + 127) // 128)]

    def phaseC1(l):
        wreset()
        S_GA, S_PA = 0, 4
        load_w(7, w_proj_a[l], 768, 256)
        sg = [walloc([128, 512], F32, "sg") for _ in range(2)]
        mag = walloc([128, 8, 512], BF16, "mag")
        for (g0, n) in GROUPS:
            hk = gkeys("hT", g0, n)
            ok = [(k_[0], k_[1], hh, th_) for k_ in gkeys("oT", g0, n) for hh in range(2) for th_ in range(2)]
            for cc in range(8):
                wga, kga = wview(S_GA, 1024, cc * 128, 128)
                wpa, kpa = wview(S_PA, 1024, cc * 128, 128)
                ba, by = cc % 4, 4 + cc % 4
                for k in range(8):
                    P.op("pe", mm(PS[ba][:, :n], wga[:, k, :], hT[:, k, g0:g0 + n], start=(k == 0), stop=(k == 7)),
                         r=[kga] + hk, w=psk(ba))
                for k in range(8):
                    P.op("pe", mm(PS[by][:, :n], wpa[:, k, :], oT[:, k, g0:g0 + n], start=(k == 0), stop=(k == 7)),
                         r=[kpa] + ok, w=psk(by))
                P.op("act", actf(sg[cc % 2][:, :n], PS[ba][:, :n], AF.Sigmoid), r=psk(ba), w=[f"sg{cc % 2}"])
                P.op("dve", tt(mag[:, cc, :n], sg[cc % 2][:, :n], PS[by][:, :n], ALU.mult),
                     r=[f"sg{cc % 2}"] + psk(by), w=["mag"])
            P.op("pool", cp(oT[:, :, g0:g0 + n], mag[:, :, :n]), r=["mag"], w=ok)
        load_w_wide(8, w_in[l], C_U, 1024)
        load_w(12, w_in[l], C_GP, 1024)

    def phaseB(l):
        wreset()
        S_U, S_GP, S_GB, S_PB = 8, 12, 0, 4
        Wp = walloc([128, 2, 4, 256], BF16, "Wp")
        for kk_ in range(2):
            P.dma("pool", Wp[:, kk_], pool_w[l][:, kk_ * 128:(kk_ + 1) * 128, :].rearrange("g p d -> p g d"),
                  w=["Wp"])
        for i_ in range(4):
            load_w(S_GB + i_, w_in[l], C_GB + 256 * i_, 256)
            load_w(S_PB + i_, w_proj_b[l], 256 * i_, 256)
        ub = [walloc([128, D], BF16, "ub") for _ in range(2)]
        u32 = walloc([128, D], F32, "u32")
        hb0 = walloc([128, D], BF16, "hb0")
        hb1 = walloc([128, D], BF16, "hb1")
        YT = walloc([128, 8, 128], BF16, "YT")
        ypT = walloc([128, 8, 512], BF16, "ypT")
        sgp = [walloc([128, 512], F32, "sgp") for _ in range(2)]
        gyT = walloc([128, 8, 512], BF16, "gyT")
        sgb = [walloc([128, 512], F32, "sgb") for _ in range(2)]
        mb = walloc([128, 512], F32, "mb")
        P.dma("pool", hb0[:, :], spool[l, 0:128, :], w=["hb0"])
        P.op("dve", mset(hb1[:, :], 0.0), w=["hb1"])
        P.dma("pool", hb1[0:64, :], spool[l, 128:192, :], w=["hb1"])
        P.dma("pool", hb1[64:112, :], spool[l, 192:240, :], w=["hb1"])
        ps_flat = pool_s[l].rearrange("j s c -> (j s) c")
        import os
        SKIP = os.environ.get("KB_SKIP", "").split(",")
        for (r0, nr) in (((64, 128), (192, 48)) if "hist" not in SKIP else ()):
            P.dma("sp", u32[0:nr, :], spool[l, r0:r0 + nr, :], w=["u32"])
            P.dma("sp", ps_flat[r0 - 64:r0 - 64 + nr, :], u32[0:nr, :], r=["u32"], is_output=True)

        def band(off, g, tw, w=128):
            return CB[:tw, off + g * w:off + g * w + (tw if w == 128 else w)]

        for (g0, n) in (GROUPS if "smp" not in SKIP else GROUPS[:-1]):
            c0 = g0 // 128
            nt = max(1, n // 128)
            for ti in range(nt):
                c = c0 + ti
                tw, to = TW[c], TO[c]
                smp = (c == 16)
                hkey = ("hT", c)
                ucur, uprev = ub[c % 2], ub[(c + 1) % 2]
                uk, upk = f"ub{c % 2}", f"ub{(c + 1) % 2}"
                for half in range(2):
                    wv_, wk_ = wwide(S_U, half)
                    for k in range(8):
                        P.op("pe", mm(PS[half][:tw, :], hT[:, k, to:to + tw], wv_[:, k, :],
                                      start=(k == 0), stop=(k == 7)), r=[hkey] + wk_, w=psk(half))
                for half in range(2):
                    P.op("act", acp(out=ucur[:tw, half * 512:(half + 1) * 512],
                                                            in_=PS[half][:tw, :]), r=psk(half), w=[uk])
                if (c == 15 or smp) and "pout" not in SKIP:
                    if smp:
                        for half in range(2):
                            P.op("dve", cp(u32[:tw, half * 512:(half + 1) * 512], PS[half][:tw, :]), r=psk(half),
                                 w=["u32"])
                    if c == 15:
                        for half in range(2):
                            wv_, wk_ = wwide(S_U, half)
                            for k in range(8):
                                P.op("pe", mm(PS[half][:15, :], hT[:, k, PT - 15:PT], wv_[:, k, :],
                                              start=(k == 0), stop=(k == 7)), r=[hkey] + wk_, w=psk(half))
                        for half in range(2):
                            P.op("dve", cp(u32[:15, half * 512:(half + 1) * 512], PS[half][:15, :]), r=psk(half),
                                 w=["u32"])
                        P.dma("sp", pool_p[l], u32[0:15, :], r=["u32"], is_output=True)
                    elif "spout" not in SKIP:
                        P.dma("sp", pool_s[l, 11:15].rearrange("j s c -> (j s) c"), u32[0:64, :], r=["u32"],
                              is_output=True)
                psy = [PS[2][:].rearrange("p (j t) -> p j t", j=4), PS[3][:].rearrange("p (j t) -> p j t", j=4)]
                for cc in range(8):
                    g = cc // 2
                    dst = psy[cc // 4][:, cc % 4, :tw]
                    lhs = ucur[:tw, cc * 128:(cc + 1) * 128]
                    if smp and "sband" in SKIP:
                        P.op("pe", mm(dst, lhs, CB[:64, CB_SN + g * 64:CB_SN + (g + 1) * 64], start=True, stop=True),
                             r=[uk, "CB"], w=psk(2 + cc // 4))
                    elif smp:
                        P.op("pe", mm(dst, lhs, CB[:64, CB_SN + g * 64:CB_SN + (g + 1) * 64], start=True, stop=False),
                             r=[uk, "CB"], w=psk(2 + cc // 4))
                        P.op("pe", mm(dst, hb0[:, cc * 128:(cc + 1) * 128], CB[:, CB_SH0 + g * 64:CB_SH0 + (g + 1) * 64],
                                      start=False, stop=False), r=["hb0", "CB"], w=psk(2 + cc // 4))
                        P.op("pe", mm(dst, hb1[:, cc * 128:(cc + 1) * 128],
                                      CB[:, CB_SH1 + g * 64:CB_SH1 + (g + 1) * 64], start=False, stop=True),
                             r=["hb1", "CB"], w=psk(2 + cc // 4))
                    elif c == 0:
                        P.op("pe", mm(dst, lhs, CB[:, CB_BF + g * 128:CB_BF + (g + 1) * 128]), r=[uk, "CB"],
                             w=psk(2 + cc // 4))
                    else:
                        P.op("pe", mm(dst, lhs, CB[:, CB_BC + g * 128:CB_BC + (g + 1) * 128], start=True, stop=False),
                             r=[uk, "CB"], w=psk(2 + cc // 4))
                        P.op("pe", mm(dst, uprev[:, cc * 128:(cc + 1) * 128],
                                      CB[:, CB_BP + g * 128:CB_BP + (g + 1) * 128], start=False, stop=True),
                             r=[upk, "CB"], w=psk(2 + cc // 4))
                for b in range(2):
                    P.op("act", acp(out=YT[:, 4 * b:4 * b + 4, :tw], in_=psy[b][:, :, :tw]),
                         r=psk(2 + b), w=["YT"])
                psl = [PS[4][:].rearrange("p (j t) -> p j t", j=4), PS[5][:].rearrange("p (j t) -> p j t", j=4)]
                for dc in range(8):
                    g = dc // 2
                    for kk in range(2):
                        P.op("pe", mm(psl[dc // 4][:, dc % 4, :tw], Wp[:, kk, g, (dc % 2) * 128:(dc % 2 + 1) * 128],
                                      YT[:, 2 * g + kk, :tw], start=(kk == 0), stop=(kk == 1)), r=["Wp", "YT"],
                             w=psk(4 + dc // 4))
                for b in range(2):
                    P.op("dve", tt(ypT[:, 4 * b:4 * b + 4, ti * 128:ti * 128 + tw], psl[b][:, :, :tw],
                                   colv[l][:, CV_PS + 4 * b:CV_PS + 4 * b + 4].unsqueeze(2).to_broadcast([128, 4, tw]),
                                   ALU.mult), r=psk(4 + b) + [f"colv{l}"], w=["ypT"])
            hk = gkeys("hT", g0, n)
            ok = [(k_[0], k_[1], hh, th_) for k_ in gkeys("oT", g0, n) for hh in range(2) for th_ in range(2)]
            for cc in range(8):
                wgp, kgp = wview(S_GP, 1024, cc * 128, 128)
                for k in range(8):
                    P.op("pe", mm(PS[6][:, :n], wgp[:, k, :], hT[:, k, g0:g0 + n], start=(k == 0), stop=(k == 7)),
                         r=[kgp] + hk, w=psk(6))
                P.op("act", actf(sgp[cc % 2][:, :n], PS[6][:, :n], AF.Silu), r=psk(6), w=[f"sgp{cc % 2}"])
                P.op("dve", tt(gyT[:, cc, :n], sgp[cc % 2][:, :n], ypT[:, cc, :n], ALU.mult),
                     r=[f"sgp{cc % 2}", "ypT"], w=["gyT"])
            for cc in range(8):
                wgb, kgb = wview(S_GB, 1024, cc * 128, 128)
                wpb, kpb = wview(S_PB, 1024, cc * 128, 128)
                bb = 6 + cc % 2
                bg = cc % 2
                for k in range(8):
                    P.op("pe", mm(PS[bg][:, :n], wgb[:, k, :], hT[:, k, g0:g0 + n], start=(k == 0), stop=(k == 7)),
                         r=[kgb] + hk, w=psk(bg))
                for k in range(8):
                    P.op("pe", mm(PS[bb][:, :n], wpb[:, k, :], gyT[:, k, :n], start=(k == 0), stop=(k == 7)),
                         r=[kpb, "gyT"], w=psk(bb))
                P.op("act", actf(sgb[cc % 2][:, :n], PS[bg][:, :n], AF.Sigmoid), r=psk(bg), w=[f"sgb{cc % 2}"])
                P.op("dve", tt(mb[:, :n], sgb[cc % 2][:, :n], PS[bb][:, :n], ALU.mult), r=[f"sgb{cc % 2}"] + psk(bb),
                     w=["mb"])
                P.op("dve", tt(oT[:, cc, g0:g0 + n], oT[:, cc, g0:g0 + n], mb[:, :n], ALU.add), r=["mb"] + ok, w=ok)

    def phaseC2(l, last):
        wreset()
        S_O, S_G = 8, 12
        load_w_wide(S_O, w_out[l], 0, 1024)
        load_w_wide(S_G, w_ple_gate[l], 0, 1024)
        if last:
            Wpp = wslot(7, 1, 256).rearrange("p k c -> p (k c)").rearrange("p (k c) -> p k c", k=2)
            wppk = ark(7)
        else:
            Wpp = walloc([128, 2, D], BF16, "Wpp")[:]
            wppk = ["Wpp"]
        P.dma("pool", Wpp, w_ple_proj[l].rearrange("(k p) c -> p k c", p=128), w=wppk)
        fn = None
        if last:
            fn = walloc([128, D], F32, "fn")
            P.dma("sp", fn[:], final_norm.partition_broadcast(128), w=["fn"])
        xsrc = x_all if l == 0 else xs
        recs = [Recorder() for _ in range(2)]
        for th in range(2):
            c2_thread(recs[th], l, last, th, (S_O, S_G), Wpp, wppk, fn, xsrc)
        merge_threads(P, recs)
        if not last:
            wl = w_in[l + 1]
            load_w(0, wl, C_Q, 512)
            load_w(2, wl, C_K, 512)
            load_w(4, wl, C_V, 512)
            load_w(6, wl, C_Z, 512)

    def c2_thread(P, l, last, th, slots, Wpp, wppk, fn, xsrc):
        S_O, S_G = slots
        T_ = f"_c{th}"
        b0, b1, b2, b3 = [4 * th + i for i in range(4)]

        def K_(n):
            return n + T_

        xt = walloc([128, D], F32, "xt")
        junk = walloc([128, D], BF16, "junk")
        ssq = walloc([128, 4], F32, "ssq")
        ssqp = walloc([128, 4], F32, "ssqp")
        hb = walloc([128, D], BF16, "hb")
        x1 = walloc([128, D], F32, "x1")
        hp = walloc([128, D], BF16, "hp")
        hpT = walloc([128, 8, 128], BF16, "hpT")
        sgt = walloc([128, D], F32, "sgt")
        pt = walloc([128, 256], F32, "pt")
        pbf = walloc([128, 256], BF16, "pbf")
        pT = walloc([128, 2, 128], BF16, "pT")
        xo = walloc([128, D], F32, "x2")
        for c in range(th, NT, 2):
            tw, to = TW[c], TO[c]
            mkeys = [("oT", c, hh_, th_) for hh_ in range(2) for th_ in range(2)]
            P.dma("sp", xt[:tw, :], xsrc[to:to + tw, :], r=[("xs", c)], w=[K_("xt")])
            P.dma("sp", pt[:tw, :], p_all[l, to:to + tw, :], w=[K_("pt")])
            for half in range(2):
                wv_, wk_ = wwide(S_O, half)
                for k in range(8):
                    P.op("pe", mm(PS[b0 + half][:tw, :], oT[:, k, to:to + tw], wv_[:, k, :],
                                  start=(k == 0), stop=(k == 7)), r=mkeys + wk_, w=psk(b0 + half))
            for half in range(2):
                P.op("dve", tt(x1[:tw, half * 512:(half + 1) * 512], xt[:tw, half * 512:(half + 1) * 512],
                               PS[b0 + half][:tw, :], ALU.add), r=[K_("xt")] + psk(b0 + half), w=[K_("x1")])
            P.op("act", actf(junk[:tw, :], x1[:tw, :], AF.Square, accum=ssqp[:tw, 0:1]), r=[K_("x1")],
                 w=[K_("junk"), K_("ssqp")])
            P.op("act", actf(ssqp[:tw, 1:2], ssqp[:tw, 0:1], AF.Sqrt, scale=1.0 / D, bias=epsc[:tw, 0:1]),
                 r=[K_("ssqp"), "epsc"], w=[K_("ssqp1")])
            P.op("dve", rcp(ssqp[:tw, 2:3], ssqp[:tw, 1:2]), r=[K_("ssqp1")], w=[K_("ssqp2")])
            P.op("dve", ts(hp[:tw, :], x1[:tw, :], ssqp[:tw, 2:3], ALU.mult), r=[K_("x1"), K_("ssqp2")], w=[K_("hp")])
            pst = PS[b2][:].bitcast(BF16).rearrange("p (k t) -> p k t", k=8)
            for k in range(8):
                P.op("pe", trp(pst[:, k, :tw], hp[:tw, k * 128:(k + 1) * 128], identb[:tw, :tw]), r=[K_("hp"), "CB"],
                     w=psk(b2))
            P.op("dve", tt(hpT[:, :, :tw], pst[:, :, :tw],
                           colv[l][:, CV_NP:CV_NP + 8].unsqueeze(2).to_broadcast([128, 8, tw]), ALU.mult),
                 r=psk(b2) + [f"colv{l}"], w=[K_("hpT")])
            for half in range(2):
                wv_, wk_ = wwide(S_G, half)
                for k in range(8):
                    P.op("pe", mm(PS[b0 + half][:tw, :], hpT[:, k, :tw], wv_[:, k, :],
                                  start=(k == 0), stop=(k == 7)), r=[K_("hpT")] + wk_, w=psk(b0 + half))
            for half in range(2):
                P.op("act", actf(sgt[:tw, half * 512:(half + 1) * 512], PS[b0 + half][:tw, :], AF.Sigmoid),
                     r=psk(b0 + half), w=[K_("sgt")])
            P.op("dve", cp(pbf[:tw, :], pt[:tw, :]), r=[K_("pt")], w=[K_("pbf")])
            pstp = PS[b3][:].bitcast(BF16)[:, 0:256].rearrange("p (k t) -> p k t", k=2)
            for k in range(2):
                P.op("pe", trp(pstp[:, k, :tw], pbf[:tw, k * 128:(k + 1) * 128], identb[:tw, :tw]),
                     r=[K_("pbf"), "CB"], w=psk(b3))
            P.op("act", acp(out=pT[:, :, :tw], in_=pstp[:, :, :tw]), r=psk(b3), w=[K_("pT")])
            for half in range(2):
                for k in range(2):
                    P.op("pe", mm(PS[b2 + half][:tw, :], pT[:, k, :tw], Wpp[:, k, half * 512:(half + 1) * 512],
                                  start=(k == 0), stop=(k == 1)), r=[K_("pT")] + wppk, w=psk(b2 + half))
            for half in range(2):
                sl = slice(half * 512, (half + 1) * 512)
                P.op("dve", tt(sgt[:tw, sl], sgt[:tw, sl], PS[b2 + half][:tw, :], ALU.mult),
                     r=[K_("sgt")] + psk(b2 + half), w=[K_("sgt")])
            P.op("dve", tt(xo[:tw, :], x1[:tw, :], sgt[:tw, :], ALU.add), r=[K_("x1"), K_("sgt")], w=[K_("x2")])
            if not last:
                P.dma("sp", xs[to:to + tw, :], xo[:tw, :], r=[K_("x2")], w=[("xs", c)])
                make_h(l + 1, c, xo, K_("x2"), (junk, ssq, hb), P=P, bank=b2, sfx=T_)
            else:
                P.op("act", actf(junk[:tw, :], xo[:tw, :], AF.Square, accum=ssq[:tw, 0:1]), r=[K_("x2")],
                     w=[K_("junk"), K_("ssq")])
                P.op("act", actf(ssq[:tw, 1:2], ssq[:tw, 0:1], AF.Sqrt, scale=1.0 / D, bias=epsc[:tw, 0:1]),
                     r=[K_("ssq"), "epsc"], w=[K_("ssq1")])
                P.op("dve", rcp(ssq[:tw, 2:3], ssq[:tw, 1:2]), r=[K_("ssq1")], w=[K_("ssq2")])
                P.op("dve", stt(x1[:tw, :], xo[:tw, :], ssq[:tw, 2:3], fn[:tw, :], ALU.mult, ALU.mult),
                     r=[K_("x2"), K_("ssq2"), "fn"], w=[K_("x1")])
                P.dma("sp", y_all[to:to + tw, :], x1[:tw, :], r=[K_("x1")], is_output=True)

    P.barrier()
    load_w(0, w_in[0], C_Q, 512)
    load_w(2, w_in[0], C_K, 512)
    load_w(4, w_in[0], C_V, 512)
    load_w(6, w_in[0], C_Z, 512)
    phase0()
    for l in range(depth):
        for hh in range(2):
            P.barrier()
            phaseA(l, hh)
            if stop_after == ("A", l, hh):
                break
        else:
            P.barrier()
            phaseC1(l)
            if stop_after == ("C1", l):
                break
            P.barrier()
            phaseB(l)
            if stop_after == ("B", l):
                break
            P.barrier()
            phaseC2(l, last=(l == depth - 1))
            continue
        break
    if dbg:
        dbg_h = dout("dbg_hT", [128, 8 * TOK])
        dbg_o = dout("dbg_oT", [128, 8 * TOK])
        P.barrier()
        wreset()
        stg = walloc([128, 2048], F32, "stg")
        for name, src, dst in (("h", hT, dbg_h), ("o", oT, dbg_o)):
            flat = src[:].rearrange("p k t -> p (k t)")
            for i in range(0, 8 * TOK, 2048):
                n = min(2048, 8 * TOK - i)
                P.op("dve", cp(stg[:, :n], flat[:, i:i + n]), w=["stg"])
                P.dma("sp", dst[:, i:i + n], stg[:, :n], r=["stg"], is_output=True)
    P.emit()
    return nc


_CACHE = {}


def make_in_maps(inputs):
    f = lambda a: np.ascontiguousarray(np.asarray(a, dtype=np.float32))
    xp, xsm = f(inputs["x_prompt"]), f(inputs["x_sample"])
    pp, psm = f(inputs["p_prompt"]), f(inputs["p_sample"])
    sc, sd, sp = f(inputs["state_conv"]), f(inputs["state_delta"]), f(inputs["state_pool"])
    cf, cb = make_consts()
    shared = {k: f(inputs[k]) for k in ("norm_mix", "w_in", "conv_w", "a_log", "dt_bias", "gdn_norm", "w_proj_a",
                                        "pool_w", "pool_scale", "w_proj_b", "w_out", "norm_ple", "w_ple_gate",
                                        "w_ple_proj", "final_norm")}
    shared["cf"] = cf
    shared["cb"] = cb
    maps = []
    for c in range(8):
        sl = slice(16 * c, 16 * c + 16)
        m = dict(shared)
        m["x_all"] = np.ascontiguousarray(np.concatenate([xp[c], xsm[sl].transpose(1, 0, 2).reshape(ST, D)], 0))
        m["p_all"] = np.ascontiguousarray(np.concatenate(
            [pp[:, c], psm[:, sl].transpose(0, 2, 1, 3).reshape(DEPTH, ST, 256)], 1))
        m["sconv"] = np.ascontiguousarray(sc[:, sl].transpose(0, 2, 1, 3).reshape(DEPTH, 48, 3072))
        m["sdelta"] = np.ascontiguousarray(sd[:, sl])
        m["spool"] = np.ascontiguousarray(sp[:, sl].transpose(0, 2, 1, 3).reshape(DEPTH, 240, D))
        maps.append(m)
    return maps


def kernel(**inputs):
    if "nc" not in _CACHE:
        _CACHE["nc"] = build()
    nc = _CACHE["nc"]
    maps = make_in_maps(inputs)
    res = run_bass_kernel_spmd(nc, maps, core_ids=list(range(8)))
    R = res.results
    y_prompt = np.stack([R[c]["y_all"][:PT] for c in range(8)], 0)
    y_sample = np.concatenate([R[c]["y_all"][PT:].reshape(4, 16, D).transpose(1, 0, 2) for c in range(8)], 0)
    conv_p = np.stack([R[c]["conv_p"] for c in range(8)], 1)
    delta_p = np.stack([R[c]["delta_p"] for c in range(8)], 1)
    pool_p = np.stack([R[c]["pool_p"] for c in range(8)], 1)
    conv_s = np.concatenate([R[c]["conv_s"].transpose(0, 2, 1, 3) for c in range(8)], 1)
    delta_s = np.concatenate([R[c]["delta_s"] for c in range(8)], 1)
    pool_s = np.concatenate([R[c]["pool_s"].transpose(0, 2, 1, 3) for c in range(8)], 1)
    out = (y_prompt, y_sample, conv_p, delta_p, pool_p, conv_s, delta_s, pool_s)
    return tuple(np.ascontiguousarray(o, dtype=np.float32) for o in out)
```
